# Optimizing a Trainium2 kernel written in Bass

```python
import jax, jax.numpy as jnp
from jax import lax
import numpy as np

D_MODEL = 1024
BATCH = 2
SEQ = 16384
DEPTH = 4

HEAD_DIM = 128
DILATED_GROUPS = ((128, 1), (512, 4), (2048, 16))
HEADS_PER_GROUP = 4
N_ATTN_HEADS = HEADS_PER_GROUP * len(DILATED_GROUPS)
ATTN_WIDTH = N_ATTN_HEADS * HEAD_DIM
ATTN_OUT = HEADS_PER_GROUP * HEAD_DIM
BLOCK = 128
ROPE_THETA = 500000.0
ROT_DIM = HEAD_DIM // 4
D_CONV = D_MODEL
CONV_K = 31
D_FF = 256 * ((8 * D_MODEL // 3 + 255) // 256)
FFN_K = 3
IN_WIDTH = 3 * ATTN_WIDTH + 2 * D_CONV + 2 * D_MODEL
EPS = 1e-6

kernel_name = "hybrid_dilated_attn_conformer_convffn_adaln"


def rms_norm(x, g):
    x32 = x.astype(jnp.float32)
    y = x32 * lax.rsqrt(jnp.mean(x32 * x32, axis=-1, keepdims=True) + EPS)
    return (y * g.astype(jnp.float32)).astype(x.dtype)


def layer_norm(x, g, b):
    x32 = x.astype(jnp.float32)
    mu = jnp.mean(x32, axis=-1, keepdims=True)
    xc = x32 - mu
    y = xc * lax.rsqrt(jnp.mean(xc * xc, axis=-1, keepdims=True) + EPS)
    return (y * g.astype(jnp.float32) + b.astype(jnp.float32)).astype(x.dtype)


def causal_dwconv(x, w, b):
    K, C = w.shape
    y = lax.conv_general_dilated(x, w[:, None, :].astype(x.dtype), window_strides=(1,),
                                 padding=[(K - 1, 0)], dimension_numbers=("NWC", "WIO", "NWC"),
                                 feature_group_count=C)
    return y + b.astype(x.dtype)


def partial_rope(t, cos, sin):
    half = ROT_DIM // 2
    t1, t2 = t[..., :half], t[..., half:ROT_DIM]
    return jnp.concatenate([t1 * cos - t2 * sin, t2 * cos + t1 * sin, t[..., ROT_DIM:]], axis=-1)


def dilated_band_attention(q, k, v, window, dilation):
    B, S, H, Dh = q.shape
    band = window // dilation
    sub_len = -(-S // (dilation * BLOCK)) * BLOCK
    pad = sub_len * dilation - S
    nb = sub_len // BLOCK

    def to_sub(t):
        t = jnp.pad(t, ((0, 0), (0, pad), (0, 0), (0, 0)))
        t = t.reshape(B, sub_len, dilation, H, Dh).transpose(0, 2, 3, 1, 4)
        return t.reshape(B, dilation, H, nb, BLOCK, Dh)

    def with_prev(t):
        prev = jnp.pad(t[:, :, :, :-1], ((0, 0), (0, 0), (0, 0), (1, 0), (0, 0), (0, 0)))
        return jnp.concatenate([prev, t], axis=4)

    qb = to_sub(q)
    kw = with_prev(to_sub(k))
    vw = with_prev(to_sub(v))
    s = jnp.einsum('brhnqd,brhnkd->brhnqk', qb, kw) * (Dh ** -0.5)
    qi = jnp.arange(BLOCK)[:, None]
    kj = jnp.arange(2 * BLOCK)[None, :]
    dist = qi + BLOCK - kj
    valid = (dist >= 0) & (dist <= band)
    not_first = jnp.arange(nb)[:, None, None] > 0
    valid = valid[None] & (not_first | (kj >= BLOCK)[None])
    s = jnp.where(valid, s, -jnp.inf)
    m = jnp.max(s, axis=-1, keepdims=True)
    p = jnp.exp(s - m)
    l = jnp.sum(p, axis=-1, keepdims=True)
    o = jnp.einsum('brhnqk,brhnkd->brhnqd', p, vw) / l
    lse = (m + jnp.log(l))[..., 0]
    o = o.reshape(B, dilation, H, sub_len, Dh).transpose(0, 3, 1, 2, 4)
    o = o.reshape(B, sub_len * dilation, H, Dh)[:, :S]
    lse = lse.reshape(B, dilation, H, sub_len).transpose(0, 3, 1, 2)
    lse = lse.reshape(B, sub_len * dilation, H)[:, :S]
    return o, lse


def setup_inputs(seed: int = 0) -> dict:
    key = jax.random.key(seed)
    ks = jax.random.split(key, 24)
    f32 = jnp.float32

    def nrm(k, shape, fan_in, s=1.0):
        return jax.random.normal(k, shape, f32) * (s * fan_in ** -0.5)

    def gain(k, shape):
        return 1.0 + 0.05 * jax.random.normal(k, shape, f32)

    def bias(k, shape):
        return 0.02 * jax.random.normal(k, shape, f32)

    return {
        "x": jax.random.normal(ks[0], (BATCH, SEQ, D_MODEL), f32),
        "c": jax.random.normal(ks[1], (BATCH, D_MODEL), f32),
        "positions": jnp.broadcast_to(jnp.arange(SEQ, dtype=jnp.int32), (BATCH, SEQ)),
        "w_ada": nrm(ks[2], (DEPTH, D_MODEL, 6 * D_MODEL), D_MODEL, 0.5),
        "b_ada": bias(ks[3], (DEPTH, 6 * D_MODEL)),
        "g_norm1": gain(ks[4], (DEPTH, D_MODEL)),
        "w_in": nrm(ks[5], (DEPTH, D_MODEL, IN_WIDTH), D_MODEL),
        "g_q": gain(ks[6], (DEPTH, HEAD_DIM)),
        "g_k": gain(ks[7], (DEPTH, HEAD_DIM)),
        "w_attn_proj": nrm(ks[8], (DEPTH, ATTN_OUT, D_MODEL), ATTN_OUT),
        "w_conv_dw": nrm(ks[9], (DEPTH, CONV_K, D_CONV), CONV_K),
        "b_conv_dw": bias(ks[10], (DEPTH, D_CONV)),
        "g_conv_ln": gain(ks[11], (DEPTH, D_CONV)),
        "b_conv_ln": bias(ks[12], (DEPTH, D_CONV)),
        "w_conv_out": nrm(ks[13], (DEPTH, D_CONV, D_MODEL), D_CONV),
        "w_o": nrm(ks[14], (DEPTH, D_MODEL, D_MODEL), D_MODEL),
        "g_norm2": gain(ks[15], (DEPTH, D_MODEL)),
        "w_ffn_in": nrm(ks[16], (DEPTH, D_MODEL, 2 * D_FF), D_MODEL),
        "w_ffn_dw": nrm(ks[17], (DEPTH, FFN_K, D_FF), FFN_K),
        "b_ffn_dw": bias(ks[18], (DEPTH, D_FF)),
        "w_ffn_down": nrm(ks[19], (DEPTH, D_FF, D_MODEL), D_FF),
    }


def reference(x, c, positions, w_ada, b_ada, g_norm1, w_in, g_q, g_k, w_attn_proj,
              w_conv_dw, b_conv_dw, g_conv_ln, b_conv_ln, w_conv_out, w_o, g_norm2,
              w_ffn_in, w_ffn_dw, b_ffn_dw, w_ffn_down):
    f32 = jnp.float32
    B, S, _ = x.shape
    inv_freq = ROPE_THETA ** (-jnp.arange(0, ROT_DIM, 2, dtype=f32) / ROT_DIM)
    ang = positions.astype(f32)[..., None] * inv_freq
    cos = jnp.cos(ang)[:, :, None, :]
    sin = jnp.sin(ang)[:, :, None, :]
    c_act = jax.nn.silu(c)
    split_at = [ATTN_WIDTH, 2 * ATTN_WIDTH, 3 * ATTN_WIDTH,
                3 * ATTN_WIDTH + D_CONV, 3 * ATTN_WIDTH + 2 * D_CONV,
                3 * ATTN_WIDTH + 2 * D_CONV + D_MODEL]

    for l in range(DEPTH):
        mod = (c_act @ w_ada[l] + b_ada[l])[:, None, :]
        sh1, sc1, gt1, sh2, sc2, gt2 = jnp.split(mod, 6, axis=-1)

        h = rms_norm(x, g_norm1[l]) * (1.0 + sc1) + sh1
        z = h @ w_in[l]
        q, k, v, c_val, c_gate, gate_a, gate_b = jnp.split(z, split_at, axis=-1)

        q = partial_rope(rms_norm(q.astype(f32).reshape(B, S, N_ATTN_HEADS, HEAD_DIM), g_q[l]), cos, sin)
        k = partial_rope(rms_norm(k.astype(f32).reshape(B, S, N_ATTN_HEADS, HEAD_DIM), g_k[l]), cos, sin)
        v = v.astype(f32).reshape(B, S, N_ATTN_HEADS, HEAD_DIM)
        outs, lses = [], []
        for gi, (win, dil) in enumerate(DILATED_GROUPS):
            hs = slice(gi * HEADS_PER_GROUP, (gi + 1) * HEADS_PER_GROUP)
            o_g, lse_g = dilated_band_attention(q[:, :, hs], k[:, :, hs], v[:, :, hs], win, dil)
            outs.append(o_g)
            lses.append(lse_g)
        wts = jax.nn.softmax(jnp.stack(lses, axis=0), axis=0)
        attn = jnp.sum(wts[..., None] * jnp.stack(outs, axis=0), axis=0)
        y_a = attn.reshape(B, S, ATTN_OUT).astype(x.dtype) @ w_attn_proj[l]

        u = c_val * jax.nn.sigmoid(c_gate)
        u = causal_dwconv(u, w_conv_dw[l], b_conv_dw[l])
        u = jax.nn.silu(layer_norm(u, g_conv_ln[l], b_conv_ln[l]))
        y_b = u @ w_conv_out[l]

        merged = jax.nn.sigmoid(gate_a) * y_a + jax.nn.sigmoid(gate_b) * y_b
        x = x + gt1 * (merged @ w_o[l])

        h2 = rms_norm(x, g_norm2[l]) * (1.0 + sc2) + sh2
        gu = h2 @ w_ffn_in[l]
        g_path, u_path = jnp.split(gu, 2, axis=-1)
        g_path = causal_dwconv(g_path, w_ffn_dw[l], b_ffn_dw[l])
        x = x + gt2 * ((jax.nn.silu(g_path) * u_path) @ w_ffn_down[l])

    return x
```

```python
import contextlib
import numpy as np
import ml_dtypes

import concourse.bass as bass
import concourse.mybir as mybir
from concourse.bass_utils import run_bass_kernel_spmd

F32 = mybir.dt.float32
BF16 = mybir.dt.bfloat16
I32 = mybir.dt.int32
AF = mybir.ActivationFunctionType
ALU = mybir.AluOpType
AX = mybir.AxisListType

D = 1024
NHEAD = 12
DH = 128
AW = 1536
DFF = 2816
INW = 8704
CONVK = 31
EPS = 1e-6
NEG = -30000.0
SCALE = DH ** -0.5
OWN_BLK = 32
SEQ = 16384
DILS = (1, 4, 16)
TWO_PI = 6.283185307179586
C1 = 6.28125
C2 = TWO_PI - C1


def geometry(L):
    Kb = [17 * i for i in range(L)]
    Mb = [k + 16 for k in Kb]
    own0 = 17 * L
    nb = own0 + OWN_BLK
    return Kb, Mb, own0, nb


def attn_units(L):
    Kb, Mb, own0, nb = geometry(L)
    out = []
    for l in range(L):
        q0, q1 = Mb[l] * 128, nb * 128
        lst = []
        for g, d in enumerate(DILS):
            span = 128 * d
            c0 = q0
            while c0 < q1:
                n = min(span, q1 - c0)
                for r in range(d):
                    lst.append((g, d, c0 + r, n // d))
                c0 += span
        out.append(lst)
    return out


def keyset_columns(L):
    cols = {}
    for lst in attn_units(L):
        for (g, d, base, nq) in lst:
            for ks in ((d, base - 128 * d), (d, base)):
                if ks not in cols:
                    cols[ks] = len(cols)
    return cols


class Tok:
    __slots__ = ("eng", "sem", "val")

    def __init__(self, eng, sem, val):
        self.eng, self.sem, self.val = eng, sem, val


class Buf:
    __slots__ = ("name", "w", "r")

    def __init__(self, name=""):
        self.name, self.w, self.r = name, None, []


class Eng:
    def __init__(self, name, e, sem, self_sync):
        self.name, self.e, self.sem, self.self_sync = name, e, sem, self_sync
        self.count = 0
        self.waited = {}
        self.pending = []
        self.slots = []
        self.slot_i = 0


class Sched:
    def __init__(self, nc, es, ndma=14):
        self.nc = nc

        def sem(n):
            return es.enter_context(nc.semaphore(n))

        self.PE = Eng("pe", nc.tensor, sem("s_pe"), False)
        self.ACT = Eng("act", nc.scalar, sem("s_act"), True)
        self.DVE = Eng("dve", nc.vector, sem("s_dve"), True)
        self.POOL = Eng("pool", nc.gpsimd, sem("s_pool"), True)
        self.SP = Eng("sp", nc.sync, sem("s_sp"), True)
        self.engs = [self.PE, self.ACT, self.DVE, self.POOL, self.SP]
        for q in (self.SP, self.POOL, self.ACT):
            q.slots = [[sem("d_%s%d" % (q.name, i)), 0] for i in range(ndma)]
        self.n_ins = 0

    def _wait(self, E, sem, val):
        key = id(sem)
        if E.waited.get(key, 0) < val:
            E.e.wait_ge(sem, val)
            E.waited[key] = val
            self.n_ins += 1

    def _deps(self, E, R, W):
        toks = []
        for b in R:
            if b.w is not None:
                toks.append(b.w)
        for b in W:
            if b.w is not None:
                toks.append(b.w)
            toks.extend(b.r)
        for t in toks:
            if t.eng is E and not E.self_sync:
                continue
            if t.val is None:
                if t.eng is E:
                    continue
                raise RuntimeError("dependency on unsignalled instruction (%s)" % t.eng.name)
            self._wait(E, t.sem, t.val)

    def op(self, E, fn, R=(), W=(), sig=True):
        self._deps(E, R, W)
        ins = fn(E.e)
        self.n_ins += 1
        tok = Tok(E, E.sem, None)
        E.pending.append(tok)
        if sig:
            E.count += 1
            ins.then_inc(E.sem, 1)
            for t in E.pending:
                t.val = E.count
            E.pending = []
        for b in W:
            b.w = tok
            b.r = []
        for b in R:
            b.r = [t for t in b.r if t.eng is not E or t.eng is None]
            b.r.append(tok)
        return ins

    def dma(self, Q, out, in_, R=(), W=()):
        self._deps(Q, R, W)
        slot = Q.slots[Q.slot_i % len(Q.slots)]
        Q.slot_i += 1
        if slot[1] > 0:
            self._wait(Q, slot[0], slot[1])
        slot[1] += 16
        Q.e.dma_start(out=out, in_=in_).then_inc(slot[0], 16)
        self.n_ins += 1
        tok = Tok(None, slot[0], slot[1])
        for b in W:
            b.w = tok
            b.r = []
        for b in R:
            b.r.append(tok)

    def barrier(self):
        for E in self.engs:
            if E.pending:
                raise RuntimeError("pending unsignalled instructions at barrier on " + E.name)
        for E in self.engs:
            for F in self.engs:
                if F is not E and F.count > 0:
                    self._wait(E, F.sem, F.count)
            for Q in (self.SP, self.POOL, self.ACT):
                for s in Q.slots:
                    if s[1] > 0:
                        self._wait(E, s[0], s[1])


class Ring:
    def __init__(self, items):
        self.items = items
        self.i = 0

    def next(self):
        it = self.items[self.i % len(self.items)]
        self.i += 1
        return it


def build_program(L=4, debug=False):
    Kb, Mb, OWN0, NB = geometry(L)
    NTOK = NB * 128
    units = attn_units(L)
    kcols = keyset_columns(L)
    NKS = len(kcols)

    nc = bass.Bass("TRN2", target_bir_lowering=False)

    def din(name, shape, dt=F32):
        return nc.dram_tensor(name, list(shape), dt, kind="ExternalInput").ap()

    dbg_kind = "ExternalOutput" if debug else "Internal"

    def dscr(name, shape, dt=F32):
        return nc.dram_tensor(name, list(shape), dt, kind=dbg_kind).ap()

    xw = din("xw", [NTOK, D])
    pos_t = din("pos_t", [128, NB], I32)
    vcol_d = din("vcol", [128, NB])
    kb_d = din("kbias", [128, NKS])
    c_t = din("c_t", [128, 8])
    invf_d = din("invf", [128, 16])
    ident_d = din("ident", [128, 128])
    maskp_d = din("maskp", [128, 128])
    maskc_d = din("maskc", [128, 128])
    w_ada = din("w_ada", [L, D, 6 * D])
    b_ada = din("b_ada", [L, 6 * D])
    g_norm1 = din("g_norm1", [L, D])
    w_in = din("w_in", [L, D, INW])
    g_q = din("g_q", [L, DH])
    g_k = din("g_k", [L, DH])
    w_attn_proj = din("w_attn_proj", [L, 512, D])
    wdw_t = din("wdw_t", [L, 128, 8, CONVK])
    bdw_t = din("bdw_t", [L, 128, 8])
    gln_t = din("gln_t", [L, 128, 8])
    bln_t = din("bln_t", [L, 128, 8])
    w_conv_out = din("w_conv_out", [L, D, D])
    w_o = din("w_o", [L, D, D])
    g_norm2 = din("g_norm2", [L, D])
    w_ffn_in = din("w_ffn_in", [L, D, 2 * DFF])
    wf3_t = din("wf3_t", [L, 128, 22, 3])
    bf_t = din("bf_t", [L, 128, 22])
    w_ffn_down = din("w_ffn_down", [L, DFF, D])
    y_out = nc.dram_tensor("y", [OWN_BLK * 128, D], F32, kind="ExternalOutput").ap()

    XA = dscr("XA", [NTOK, D])
    XM = dscr("XM", [NTOK, D])
    QS = dscr("QS", [NTOK, AW], BF16)
    KS = dscr("KS", [NTOK, AW], BF16)
    VS = dscr("VS", [NTOK, NHEAD * 129], BF16)
    ATT = [dscr("ATT%d" % g, [NTOK, 516]) for g in range(3)]
    MOD = dscr("MOD", [L, 128, 6 * D])
    ROPE = dscr("ROPE", [2, 128, NB * 16])

    uid = [0]

    def nm(p):
        uid[0] += 1
        return "%s_%d" % (p, uid[0])

    with contextlib.ExitStack() as top:
        S = Sched(nc, top)
        PE, ACT, DVE, POOL, SP = S.PE, S.ACT, S.DVE, S.POOL, S.SP

        def sbuf(es, shape, dt, p="t"):
            return es.enter_context(nc.sbuf_tensor(nm(p), list(shape), dt))

        def psum(es, shape, dt, p="ps"):
            return es.enter_context(nc.psum_tensor(nm(p), list(shape), dt))

        ident = sbuf(top, [128, 128], BF16, "ident")
        maskP = sbuf(top, [128, 128], BF16, "maskp")
        maskC = sbuf(top, [128, 128], BF16, "maskc")
        ones_bf = sbuf(top, [128, 128], BF16, "ones")
        vcol = sbuf(top, [128, NB], F32, "vcol")
        kb = sbuf(top, [128, NKS], F32, "kb")
        gB = Buf("glob")
        S.dma(POOL, ident[:], ident_d, W=[gB])
        S.dma(POOL, maskP[:], maskp_d, W=[gB])
        S.dma(POOL, maskC[:], maskc_d, W=[gB])
        S.dma(SP, vcol[:], vcol_d, W=[gB])
        S.dma(SP, kb[:], kb_d, W=[gB])
        S.op(DVE, lambda e: e.memset(ones_bf[:], 1.0), W=[Buf()])

        def load_weight(es_stage, dst, src, kc, ncol, rot):
            PIECE = 2048
            stg = [(sbuf(es_stage, [128, PIECE], F32, "stg"), Buf()) for _ in range(3)]
            ring = Ring(stg)
            for kk in range(kc):
                for c0 in range(0, ncol, PIECE):
                    cw = min(PIECE, ncol - c0)
                    st, sb_ = ring.next()
                    S.dma(SP, st[:, :cw], src[kk * 128:(kk + 1) * 128, c0:c0 + cw], W=[sb_])
                    E = rot.next()
                    if E is ACT:
                        S.op(E, lambda e, st=st, kk=kk, c0=c0, cw=cw: e.copy(out=dst[:, kk, c0:c0 + cw], in_=st[:, :cw]),
                             R=[sb_], W=[Buf()])
                    else:
                        S.op(E, lambda e, st=st, kk=kk, c0=c0, cw=cw: e.tensor_copy(out=dst[:, kk, c0:c0 + cw], in_=st[:, :cw]),
                             R=[sb_], W=[Buf()])

        def rsqrt_small(dst, src, scale, srcB, dstB, tmp, tmpB):
            S.op(ACT, lambda e: e.activation(out=tmp, in_=src, func=AF.Ln, scale=scale, bias=EPS), R=[srcB], W=[tmpB])
            S.op(ACT, lambda e: e.activation(out=dst, in_=tmp, func=AF.Exp, scale=-0.5), R=[tmpB], W=[dstB])

        class NormCtx:
            def __init__(self, es, Gm, SHb, constB):
                self.Gm, self.SHb, self.constB = Gm, SHb, constB
                self.junk = sbuf(es, [128, D], BF16, "junk")
                self.junkB = Buf()
                self.tmp = Ring([(sbuf(es, [128, D], F32, "ntmp"), Buf()) for _ in range(1)])
                self.h = Ring([(sbuf(es, [128, D], BF16, "h"), Buf()) for _ in range(2)])
                self.st = Ring([(sbuf(es, [128, 4], F32, "nst"), Buf()) for _ in range(2)])
                self.tp = Ring([(psum(es, [128, 8, 128], BF16, "tp"), Buf()) for _ in range(1)])

            def run(self, xt, xtB, blk, hT_dst, hTB, evac_eng):
                junk, junkB = self.junk, self.junkB
                tmp, tmpB = self.tmp.next()
                h, hB = self.h.next()
                st, stB = self.st.next()
                tp, tpB = self.tp.next()
                S.op(ACT, lambda e: e.activation(out=junk[:], in_=xt[:], func=AF.Square, accum_out=st[:, 0:1]),
                     R=[xtB], W=[junkB, stB])
                S.op(ACT, lambda e: e.activation(out=st[:, 1:2], in_=st[:, 0:1], func=AF.Ln, scale=1.0 / D, bias=EPS),
                     R=[stB], W=[stB])
                S.op(ACT, lambda e: e.activation(out=st[:, 2:3], in_=st[:, 1:2], func=AF.Exp, scale=-0.5),
                     R=[stB], W=[stB])
                S.op(DVE, lambda e: e.tensor_tensor(out=st[:, 3:4], in0=st[:, 2:3], in1=vcol[:, blk:blk + 1], op=ALU.mult),
                     R=[stB], W=[stB])
                S.op(DVE, lambda e: e.scalar_tensor_tensor(out=tmp[:], in0=xt[:], scalar=st[:, 3:4], in1=self.Gm[:],
                                                           op0=ALU.mult, op1=ALU.mult),
                     R=[xtB, stB, self.constB], W=[tmpB])
                S.op(DVE, lambda e: e.scalar_tensor_tensor(out=h[:], in0=self.SHb[:], scalar=vcol[:, blk:blk + 1], in1=tmp[:],
                                                           op0=ALU.mult, op1=ALU.add),
                     R=[tmpB, self.constB], W=[hB])
                for kk in range(8):
                    S.op(PE, lambda e, kk=kk: e.transpose(out=tp[:, kk, :], in_=h[:, kk * 128:(kk + 1) * 128], identity=ident[:]),
                         R=[hB], W=[tpB], sig=(kk == 7))
                if evac_eng is ACT:
                    S.op(ACT, lambda e: e.copy(out=hT_dst, in_=tp[:]), R=[tpB], W=[hTB])
                else:
                    S.op(evac_eng, lambda e: e.tensor_copy(out=hT_dst, in_=tp[:]), R=[tpB], W=[hTB])

        def load_bcast(es, dst, src_row, B):
            S.dma(SP, dst[:], src_row.partition_broadcast(128), W=[B])

        def prologue():
            with contextlib.ExitStack() as es:
                NE = NB * 16
                pi = sbuf(es, [128, NB], I32)
                pf = sbuf(es, [128, NB], F32)
                invf = sbuf(es, [128, 16], F32)
                ang = sbuf(es, [128, NB, 16], F32)
                kf = sbuf(es, [128, NB, 16], F32)
                ki = sbuf(es, [128, NB, 16], I32)
                r = sbuf(es, [128, NB, 16], F32)
                m = sbuf(es, [128, NB, 16], F32)
                r2 = sbuf(es, [128, NB, 16], F32)
                sn = sbuf(es, [128, NB, 16], F32)
                cs = sbuf(es, [128, NB, 16], F32)
                B = Buf()
                S.dma(SP, pi[:], pos_t, W=[B])
                S.dma(SP, invf[:], invf_d, W=[B])
                V = lambda fn, R=(B,), W=(B,): S.op(DVE, fn, R=list(R), W=list(W))
                V(lambda e: e.tensor_copy(out=pf[:], in_=pi[:]))
                V(lambda e: e.tensor_tensor(out=ang[:], in0=pf[:].unsqueeze(2).broadcast_to([128, NB, 16]),
                                            in1=invf[:].unsqueeze(1).broadcast_to([128, NB, 16]), op=ALU.mult))
                V(lambda e: e.tensor_scalar(out=kf[:], in0=ang[:], scalar1=1.0 / TWO_PI, scalar2=None, op0=ALU.mult))
                V(lambda e: e.tensor_copy(out=ki[:], in_=kf[:]))
                V(lambda e: e.tensor_copy(out=kf[:], in_=ki[:]))
                V(lambda e: e.scalar_tensor_tensor(out=r[:], in0=kf[:], scalar=-C1, in1=ang[:], op0=ALU.mult, op1=ALU.add))
                V(lambda e: e.scalar_tensor_tensor(out=r[:], in0=kf[:], scalar=-C2, in1=r[:], op0=ALU.mult, op1=ALU.add))

                def wrap(t):
                    V(lambda e: e.tensor_scalar(out=m[:], in0=t[:], scalar1=np.pi, scalar2=-TWO_PI, op0=ALU.is_gt, op1=ALU.mult))
                    V(lambda e: e.tensor_tensor(out=t[:], in0=t[:], in1=m[:], op=ALU.add))
                    V(lambda e: e.tensor_scalar(out=m[:], in0=t[:], scalar1=-np.pi, scalar2=TWO_PI, op0=ALU.is_lt, op1=ALU.mult))
                    V(lambda e: e.tensor_tensor(out=t[:], in0=t[:], in1=m[:], op=ALU.add))
                    V(lambda e: e.tensor_scalar(out=t[:], in0=t[:], scalar1=3.1415925, scalar2=-3.1415925, op0=ALU.min, op1=ALU.max))

                wrap(r)
                V(lambda e: e.tensor_scalar(out=r2[:], in0=r[:], scalar1=np.pi / 2, scalar2=None, op0=ALU.add))
                wrap(r2)
                S.op(ACT, lambda e: e.activation(out=sn[:], in_=r[:], func=AF.Sin), R=[B], W=[B])
                S.op(ACT, lambda e: e.activation(out=cs[:], in_=r2[:], func=AF.Sin), R=[B], W=[B])
                S.dma(SP, ROPE[0], cs[:].rearrange("p b i -> p (b i)"), R=[B])
                S.dma(SP, ROPE[1], sn[:].rearrange("p b i -> p (b i)"), R=[B])
                S.barrier()
            with contextlib.ExitStack() as es:
                ct = sbuf(es, [128, 8], F32)
                ca = sbuf(es, [128, 8], F32)
                crep = sbuf(es, [128, 8, 128], F32)
                B = Buf()
                S.dma(SP, ct[:], c_t, W=[B])
                S.op(ACT, lambda e: e.activation(out=ca[:], in_=ct[:], func=AF.Silu), R=[B], W=[B])
                S.op(DVE, lambda e: e.tensor_copy(out=crep[:], in_=ca[:].unsqueeze(2).broadcast_to([128, 8, 128])), R=[B], W=[B])
                stg = Ring([(sbuf(es, [128, 8, 512], F32, "astg"), Buf()) for _ in range(2)])
                pss = Ring([(psum(es, [128, 512], F32, "aps"), Buf()) for _ in range(2)])
                bada = sbuf(es, [128, 6 * D], F32)
                modt = sbuf(es, [128, 6 * D], F32)
                badaB, modB = Buf(), Buf()
                for l in range(L):
                    load_bcast(es, bada, b_ada[l], badaB)
                    wv = w_ada[l].rearrange("(k p) c -> p k c", p=128)
                    for ctile in range(12):
                        st, stB = stg.next()
                        ps, psB = pss.next()
                        S.dma(SP, st[:], wv[:, :, ctile * 512:(ctile + 1) * 512], W=[stB])
                        for kk in range(8):
                            S.op(PE, lambda e, kk=kk, st=st, ps=ps: e.matmul(ps[:], lhsT=crep[:, kk, :], rhs=st[:, kk, :],
                                                                             start=(kk == 0), stop=(kk == 7)),
                                 R=[stB, B], W=[psB], sig=(kk == 7))
                        S.op(DVE, lambda e, ps=ps, ctile=ctile: e.tensor_tensor(out=modt[:, ctile * 512:(ctile + 1) * 512], in0=ps[:],
                                                                                in1=bada[:, ctile * 512:(ctile + 1) * 512], op=ALU.add),
                             R=[psB, badaB], W=[modB])
                    S.dma(SP, MOD[l], modt[:], R=[modB])
                S.barrier()

        def phase1(l):
            Xin = xw if l == 0 else XA
            with contextlib.ExitStack() as es:
                wq = sbuf(es, [128, 8, 3 * AW], BF16, "wq")
                with contextlib.ExitStack() as es2:
                    load_weight(es2, wq, w_in[l][:, 0:3 * AW], 8, 3 * AW, Ring([DVE, POOL, ACT]))
                    S.barrier()
                cB = Buf()
                Gm = sbuf(es, [128, D], F32, "Gm")
                SHb = sbuf(es, [128, D], F32, "SHb")
                g1b = sbuf(es, [128, D], F32, "g1b")
                gq = sbuf(es, [128, DH], F32, "gq")
                gk = sbuf(es, [128, DH], F32, "gk")
                cosT = sbuf(es, [128, NB, 16], F32, "cosT")
                sinT = sbuf(es, [128, NB, 16], F32, "sinT")
                load_bcast(es, g1b, g_norm1[l], cB)
                load_bcast(es, gq, g_q[l], cB)
                load_bcast(es, gk, g_k[l], cB)
                S.dma(SP, Gm[:], MOD[l][:, D:2 * D], W=[cB])
                S.dma(SP, SHb[:], MOD[l][:, 0:D], W=[cB])
                S.dma(SP, cosT[:].rearrange("p b i -> p (b i)"), ROPE[0], W=[cB])
                S.dma(SP, sinT[:].rearrange("p b i -> p (b i)"), ROPE[1], W=[cB])
                S.op(DVE, lambda e: e.scalar_tensor_tensor(out=Gm[:], in0=Gm[:], scalar=1.0, in1=g1b[:], op0=ALU.add, op1=ALU.mult),
                     R=[cB], W=[cB])
                norm = NormCtx(es, Gm, SHb, cB)
                xts = Ring([(sbuf(es, [128, D], F32, "xt"), Buf()) for _ in range(3)])
                hTs = Ring([(sbuf(es, [128, 8, 128], BF16, "hT"), Buf()) for _ in range(2)])
                pss = Ring([(psum(es, [128, 512], F32, "p1ps"), Buf()) for _ in range(4)])
                sqs = Ring([(sbuf(es, [128, 512], F32, "sq"), Buf()) for _ in range(2)])
                s4s = Ring([(sbuf(es, [128, 12], F32, "s4"), Buf()) for _ in range(2)])
                qns = Ring([(sbuf(es, [128, 512], F32, "qn"), Buf()) for _ in range(2)])
                qos = Ring([(sbuf(es, [128, 512], BF16, "qo"), Buf()) for _ in range(3)])
                rts = Ring([(sbuf(es, [128, 4, 4, 16], F32, "rt"), Buf()) for _ in range(2)])
                vos = []
                for _ in range(2):
                    vo = sbuf(es, [128, 4, 129], BF16, "vo")
                    vB = Buf()
                    S.op(POOL, lambda e, vo=vo: e.memset(vo[:], 1.0), W=[vB])
                    vos.append((vo, vB))
                vos = Ring(vos)
                for blk in range(Kb[l], NB):
                    xt, xtB = xts.next()
                    hT, hTB = hTs.next()
                    S.dma(SP, xt[:], Xin[blk * 128:(blk + 1) * 128, :], W=[xtB])
                    norm.run(xt, xtB, blk, hT[:], hTB, DVE)
                    for t in range(9):
                        ps, psB = pss.next()
                        for kk in range(8):
                            S.op(PE, lambda e, kk=kk, ps=ps, t=t: e.matmul(ps[:], lhsT=hT[:, kk, :], rhs=wq[:, kk, t * 512:(t + 1) * 512],
                                                                           start=(kk == 0), stop=(kk == 7)),
                                 R=[hTB], W=[psB], sig=(kk == 7))
                        if t < 6:
                            gvec = gq if t < 3 else gk
                            dstD = QS if t < 3 else KS
                            tt = t % 3
                            sq, sqB = sqs.next()
                            s4, s4B = s4s.next()
                            qn, qnB = qns.next()
                            qo, qoB = qos.next()
                            rt, rtB = rts.next()
                            S.op(ACT, lambda e, sq=sq, ps=ps: e.activation(out=sq[:], in_=ps[:], func=AF.Square), R=[psB], W=[sqB])
                            S.op(DVE, lambda e, s4=s4, sq=sq: e.tensor_reduce(out=s4[:, 0:4], in_=sq[:].rearrange("p (h d) -> p h d", h=4),
                                                                              axis=AX.X, op=ALU.add), R=[sqB], W=[s4B])
                            rsqrt_small(s4[:, 8:12], s4[:, 0:4], 1.0 / DH, s4B, s4B, s4[:, 4:8], s4B)
                            for j in range(4):
                                S.op(DVE, lambda e, j=j, qn=qn, ps=ps, s4=s4, gvec=gvec: e.scalar_tensor_tensor(
                                    out=qn[:, j * 128:(j + 1) * 128], in0=ps[:, j * 128:(j + 1) * 128], scalar=s4[:, 8 + j:9 + j],
                                    in1=gvec[:], op0=ALU.mult, op1=ALU.mult), R=[psB, s4B, cB], W=[qnB])
                            S.op(POOL, lambda e, qo=qo, qn=qn: e.tensor_copy(out=qo[:], in_=qn[:]), R=[qnB], W=[qoB])
                            qn3 = qn[:].rearrange("p (h d) -> p h d", h=4)
                            qo3 = qo[:].rearrange("p (h d) -> p h d", h=4)
                            cb = cosT[:, blk, :].unsqueeze(1).broadcast_to([128, 4, 16])
                            sb_ = sinT[:, blk, :].unsqueeze(1).broadcast_to([128, 4, 16])
                            t1, t2 = qn3[:, :, 0:16], qn3[:, :, 16:32]
                            S.op(DVE, lambda e, rt=rt, t1=t1, cb=cb: e.tensor_tensor(out=rt[:, 0], in0=t1, in1=cb, op=ALU.mult), R=[qnB, cB], W=[rtB])
                            S.op(DVE, lambda e, rt=rt, t2=t2, sb_=sb_: e.tensor_tensor(out=rt[:, 1], in0=t2, in1=sb_, op=ALU.mult), R=[qnB, cB], W=[rtB])
                            S.op(DVE, lambda e, rt=rt, t2=t2, cb=cb: e.tensor_tensor(out=rt[:, 2], in0=t2, in1=cb, op=ALU.mult), R=[qnB, cB], W=[rtB])
                            S.op(DVE, lambda e, rt=rt, t1=t1, sb_=sb_: e.tensor_tensor(out=rt[:, 3], in0=t1, in1=sb_, op=ALU.mult), R=[qnB, cB], W=[rtB])
                            S.op(DVE, lambda e, rt=rt, qo3=qo3: e.tensor_tensor(out=qo3[:, :, 0:16], in0=rt[:, 0], in1=rt[:, 1], op=ALU.subtract),
                                 R=[rtB], W=[qoB])
                            S.op(DVE, lambda e, rt=rt, qo3=qo3: e.tensor_tensor(out=qo3[:, :, 16:32], in0=rt[:, 2], in1=rt[:, 3], op=ALU.add),
                                 R=[rtB], W=[qoB])
                            S.dma(POOL, dstD[blk * 128:(blk + 1) * 128, tt * 512:(tt + 1) * 512], qo[:], R=[qoB])
                        else:
                            tt = t - 6
                            vo, vB = vos.next()
                            S.op(ACT, lambda e, vo=vo, ps=ps: e.copy(out=vo[:, :, 0:128], in_=ps[:].rearrange("p (h d) -> p h d", h=4)),
                                 R=[psB], W=[vB])
                            S.dma(POOL, VS[blk * 128:(blk + 1) * 128, tt * 516:(tt + 1) * 516], vo[:].rearrange("p h d -> p (h d)"), R=[vB])
                S.barrier()

        def phase2(l):
            with contextlib.ExitStack() as es:
                def ring(n, shape, dt, p, ps=False):
                    return Ring([((psum if ps else sbuf)(es, shape, dt, p), Buf()) for _ in range(n)])
                Qts = ring(3, [128, 512], BF16, "Qt")
                Kcs = ring(3, [128, 512], BF16, "Kc")
                Kps = ring(3, [128, 512], BF16, "Kp")
                Vcs = ring(3, [128, 4, 129], BF16, "Vc")
                Vps = ring(3, [128, 4, 129], BF16, "Vp")
                Tps = ring(2, [128, 6, 128], BF16, "Tps", ps=True)
                qkTs = ring(2, [128, 6, 128], BF16, "qkT")
                Sps = ring(2, [128, 2, 2, 128], F32, "Sps", ps=True)
                PTs = ring(2, [128, 2, 2, 128], BF16, "PT")
                Ops = ring(2, [128, 2, 129], F32, "Ops", ps=True)
                Ots = ring(2, [128, 4, 129], F32, "Ot")
                for rg in (Qts, Kcs, Vcs, qkTs, PTs):
                    for (t_, b_) in rg.items:
                        S.op(POOL, lambda e, t_=t_: e.memset(t_[:], 0.0), W=[b_])
                hu = 0
                for (g, d, base, nq) in units[l]:
                    m0, r = divmod(base, d)
                    qv = QS.rearrange("(m d) c -> m d c", d=d)
                    kv = KS.rearrange("(m d) c -> m d c", d=d)
                    vv = VS.rearrange("(m d) c -> m d c", d=d)
                    av = ATT[g].rearrange("(m d) c -> m d c", d=d)
                    Qt, QtB = Qts.next()
                    Kc, KcB = Kcs.next()
                    Kp, KpB = Kps.next()
                    Vc, VcB = Vcs.next()
                    Vp, VpB = Vps.next()
                    cs_ = slice(512 * g, 512 * g + 512)
                    vs_ = slice(516 * g, 516 * g + 516)
                    S.dma(SP, Qt[:nq, :], qv[m0:m0 + nq, r, cs_], W=[QtB])
                    S.dma(SP, Kp[:, :], kv[m0 - 128:m0, r, cs_], W=[KpB])
                    S.dma(SP, Kc[:nq, :], kv[m0:m0 + nq, r, cs_], W=[KcB])
                    S.dma(SP, Vp[:].rearrange("p h d -> p (h d)"), vv[m0 - 128:m0, r, vs_], W=[VpB])
                    S.dma(SP, Vc[:nq].rearrange("p h d -> p (h d)"), vv[m0:m0 + nq, r, vs_], W=[VcB])
                    colp = kcols[(d, base - 128 * d)]
                    colc = kcols[(d, base)]
                    Ot, OtB = Ots.next()
                    for hp in range(2):
                        T, TB = Tps.next()
                        qkT, qkTB = qkTs.next()
                        Sp, SpB = Sps.next()
                        PT, PTB = PTs.next()
                        Op, OpB = Ops.next()
                        for jj in range(2):
                            j = 2 * hp + jj
                            S.op(PE, lambda e, jj=jj, j=j: e.transpose(out=T[:, jj, :nq], in_=Qt[:nq, j * 128:(j + 1) * 128], identity=ident[:nq, :nq]),
                                 R=[QtB], W=[TB], sig=False)
                            S.op(PE, lambda e, jj=jj, j=j: e.transpose(out=T[:, 2 + jj, :], in_=Kp[:, j * 128:(j + 1) * 128], identity=ident[:]),
                                 R=[KpB], W=[TB], sig=False)
                            S.op(PE, lambda e, jj=jj, j=j: e.transpose(out=T[:, 4 + jj, :nq], in_=Kc[:nq, j * 128:(j + 1) * 128], identity=ident[:nq, :nq]),
                                 R=[KcB], W=[TB], sig=(jj == 1))
                        ev = ACT if (hu % 2 == 0) else DVE
                        hu += 1
                        if nq == 128:
                            if ev is ACT:
                                S.op(ACT, lambda e: e.copy(out=qkT[:], in_=T[:]), R=[TB], W=[qkTB])
                            else:
                                S.op(DVE, lambda e: e.tensor_copy(out=qkT[:], in_=T[:]), R=[TB], W=[qkTB])
                        else:
                            S.op(DVE, lambda e: e.tensor_copy(out=qkT[:, 0:2, :nq], in_=T[:, 0:2, :nq]), R=[TB], W=[qkTB])
                            S.op(DVE, lambda e: e.tensor_copy(out=qkT[:, 2:4, :], in_=T[:, 2:4, :]), R=[TB], W=[qkTB])
                            S.op(DVE, lambda e: e.tensor_copy(out=qkT[:, 4:6, :nq], in_=T[:, 4:6, :nq]), R=[TB], W=[qkTB])
                        for jj in range(2):
                            S.op(PE, lambda e, jj=jj: e.matmul(Sp[:, 0, jj, :nq], lhsT=ident[:], rhs=maskP[:, :nq], start=True, stop=False),
                                 R=[], W=[SpB], sig=False)
                            S.op(PE, lambda e, jj=jj: e.matmul(Sp[:, 0, jj, :nq], lhsT=qkT[:, 2 + jj, :], rhs=qkT[:, jj, :nq], start=False, stop=True),
                                 R=[qkTB], W=[SpB], sig=False)
                            S.op(PE, lambda e, jj=jj: e.matmul(Sp[:nq, 1, jj, :nq], lhsT=ident[:nq, :nq], rhs=maskC[:nq, :nq], start=True, stop=False),
                                 R=[], W=[SpB], sig=False)
                            S.op(PE, lambda e, jj=jj: e.matmul(Sp[:nq, 1, jj, :nq], lhsT=qkT[:, 4 + jj, :nq], rhs=qkT[:, jj, :nq], start=False, stop=True),
                                 R=[qkTB], W=[SpB], sig=(jj == 1))
                        S.op(ACT, lambda e: e.activation(out=PT[:, 0, :, :nq], in_=Sp[:, 0, :, :nq], func=AF.Exp, scale=SCALE,
                                                         bias=kb[:, colp:colp + 1]), R=[SpB], W=[PTB])
                        S.op(ACT, lambda e: e.activation(out=PT[:nq, 1, :, :nq], in_=Sp[:nq, 1, :, :nq], func=AF.Exp, scale=SCALE,
                                                         bias=kb[:nq, colc:colc + 1]), R=[SpB], W=[PTB])
                        for jj in range(2):
                            j = 2 * hp + jj
                            S.op(PE, lambda e, jj=jj, j=j: e.matmul(Op[:nq, jj, :], lhsT=PT[:, 0, jj, :nq], rhs=Vp[:, j, :], start=True, stop=False),
                                 R=[PTB, VpB], W=[OpB], sig=False)
                            S.op(PE, lambda e, jj=jj, j=j: e.matmul(Op[:nq, jj, :], lhsT=PT[:nq, 1, jj, :nq], rhs=Vc[:nq, j, :], start=False, stop=True),
                                 R=[PTB, VcB], W=[OpB], sig=(jj == 1))
                        S.op(DVE, lambda e, hp=hp: e.tensor_copy(out=Ot[:nq, 2 * hp:2 * hp + 2, :], in_=Op[:nq, :, :]), R=[OpB], W=[OtB])
                    S.dma(POOL, av[m0:m0 + nq, r, :], Ot[:nq].rearrange("p h d -> p (h d)"), R=[OtB])
                S.barrier()

        def phase3(l):
            Xin = xw if l == 0 else XA
            WT = 256
            with contextlib.ExitStack() as es:
                wcg = sbuf(es, [128, 8, 4096], BF16, "wcg")
                wco = sbuf(es, [128, 8, D], BF16, "wco")
                wap = sbuf(es, [128, 4, D], BF16, "wap")
                wo = sbuf(es, [128, 8, D], BF16, "wo")
                with contextlib.ExitStack() as es2:
                    rot = Ring([DVE, POOL, ACT])
                    load_weight(es2, wcg, w_in[l][:, 3 * AW:INW], 8, 4096, rot)
                    load_weight(es2, wco, w_conv_out[l], 8, D, rot)
                    load_weight(es2, wap, w_attn_proj[l], 4, D, rot)
                    load_weight(es2, wo, w_o[l], 8, D, rot)
                    S.barrier()
                cB = Buf()
                Gm = sbuf(es, [128, D], F32, "Gm")
                SHb = sbuf(es, [128, D], F32, "SHb")
                gtb = sbuf(es, [128, D], F32, "gtb")
                wdw = sbuf(es, [128, 8, CONVK], F32, "wdw")
                bdw = sbuf(es, [128, 8], F32, "bdw")
                gln = sbuf(es, [128, 8], F32, "gln")
                bln = sbuf(es, [128, 8], F32, "bln")
                xm = sbuf(es, [128, D], F32, "xm")
                load_bcast(es, xm, g_norm1[l], cB)
                S.dma(SP, Gm[:], MOD[l][:, D:2 * D], W=[cB])
                S.dma(SP, SHb[:], MOD[l][:, 0:D], W=[cB])
                S.dma(SP, gtb[:], MOD[l][:, 2 * D:3 * D], W=[cB])
                S.dma(SP, wdw[:], wdw_t[l], W=[cB])
                S.dma(SP, bdw[:], bdw_t[l], W=[cB])
                S.dma(SP, gln[:], gln_t[l], W=[cB])
                S.dma(SP, bln[:], bln_t[l], W=[cB])
                S.op(DVE, lambda e: e.scalar_tensor_tensor(out=Gm[:], in0=Gm[:], scalar=1.0, in1=xm[:], op0=ALU.add, op1=ALU.mult),
                     R=[cB], W=[cB])
                xmB = Buf()
                xmB.r.append(cB.w)
                norm = NormCtx(es, Gm, SHb, cB)
                xts = Ring([(sbuf(es, [128, D], F32, "xt"), Buf()) for _ in range(2)])
                hTs = Ring([(sbuf(es, [128, 8, WT], BF16, "hT"), [Buf(), Buf()]) for _ in range(2)])
                As = [(sbuf(es, [128, 516], F32, "A"), Buf()) for _ in range(3)]
                abf = sbuf(es, [128, 512], BF16, "abf"); abfB = Buf()
                ast = sbuf(es, [128, 8], F32, "ast"); astB = Buf()
                attnTs = Ring([(sbuf(es, [128, 4, WT], BF16, "attnT"), [Buf(), Buf()]) for _ in range(2)])
                tpa = norm.tp.items[0]
                uT = sbuf(es, [128, 8, 30 + WT], BF16, "uT")
                uTB = [Buf() for _ in range(8)]
                S.op(POOL, lambda e: e.memset(uT[:], 0.0), W=uTB)
                sgt = Ring([(sbuf(es, [128, WT], F32, "sgt"), Buf()) for _ in range(2)])
                yv = sbuf(es, [128, 8, WT], F32, "yv"); yB = [Buf() for _ in range(8)]
                ybf = sbuf(es, [128, 8, WT], BF16, "ybf"); ybfB = Buf()
                ysq = sbuf(es, [128, 8, WT], BF16, "ysq"); ysqB = Buf()
                stt_ = sbuf(es, [128, 5, WT], F32, "lnst"); stB = Buf()
                actT = sbuf(es, [128, 8, WT], BF16, "actT"); actB = [Buf() for _ in range(8)]
                sg2 = Ring([(sbuf(es, [128, 2, WT], F32, "sg2"), Buf()) for _ in range(2)])
                m1 = Ring([(sbuf(es, [128, 2, WT], F32, "m1"), Buf()) for _ in range(2)])
                mT = sbuf(es, [128, 8, WT], BF16, "mT"); mTB = [Buf() for _ in range(8)]
                psv = Ring([(psum(es, [128, 2, WT], F32, "psv"), Buf()) for _ in range(2)])
                psS = (psum(es, [128, 2, WT], F32, "psS"), Buf())
                psA = (psum(es, [128, 2, WT], F32, "psA"), Buf())
                psB_ = (psum(es, [128, 2, WT], F32, "psB"), Buf())
                pso = Ring([(psum(es, [128, 512], F32, "pso"), Buf()) for _ in range(2)])

                blk0 = Mb[l]
                while blk0 < NB:
                    nbk = min(2, NB - blk0)
                    W_ = 128 * nbk
                    hT, hTBs = hTs.next()
                    attnT, attnTBs = attnTs.next()
                    xl = []
                    for bi in range(nbk):
                        blk = blk0 + bi
                        xt, xtB = xts.next()
                        xl.append((xt, xtB))
                        S.dma(SP, xt[:], Xin[blk * 128:(blk + 1) * 128, :], W=[xtB])
                        norm.run(xt, xtB, blk, hT[:, :, bi * 128:(bi + 1) * 128], hTBs[bi], ACT)
                        for g in range(3):
                            S.dma(SP, As[g][0][:], ATT[g][blk * 128:(blk + 1) * 128, :], W=[As[g][1]])
                        A0, A1, A2 = As[0][0], As[1][0], As[2][0]
                        S.op(POOL, lambda e: e.tensor_tensor(out=A0[:], in0=A0[:], in1=A1[:], op=ALU.add), R=[As[1][1]], W=[As[0][1]])
                        S.op(POOL, lambda e: e.tensor_tensor(out=A0[:], in0=A0[:], in1=A2[:], op=ALU.add), R=[As[2][1]], W=[As[0][1]])
                        A3 = A0[:].rearrange("p (h d) -> p h d", h=4)
                        S.op(DVE, lambda e: e.tensor_scalar(out=ast[:, 0:4], in0=A3[:, :, 128], scalar1=1e-30, scalar2=None, op0=ALU.max),
                             R=[As[0][1]], W=[astB])
                        S.op(DVE, lambda e: e.reciprocal(out=ast[:, 4:8], in_=ast[:, 0:4]), R=[astB], W=[astB])
                        for j in range(4):
                            S.op(DVE, lambda e, j=j: e.tensor_scalar(out=abf[:, j * 128:(j + 1) * 128], in0=A3[:, j, 0:128],
                                                                     scalar1=ast[:, 4 + j:5 + j], scalar2=None, op0=ALU.mult),
                                 R=[As[0][1], astB], W=[abfB])
                        tp_, tpB_ = tpa
                        for j in range(4):
                            S.op(PE, lambda e, j=j: e.transpose(out=tp_[:, j, :], in_=abf[:, j * 128:(j + 1) * 128], identity=ident[:]),
                                 R=[abfB], W=[tpB_], sig=(j == 3))
                        S.op(DVE, lambda e, bi=bi: e.tensor_copy(out=attnT[:, :, bi * 128:(bi + 1) * 128], in_=tp_[:, 0:4, :]),
                             R=[tpB_], W=[attnTBs[bi]])
                    hR = hTBs[:nbk]
                    aR = attnTBs[:nbk]
                    for cp in range(4):
                        pair = []
                        for ci in range(2):
                            c = 2 * cp + ci
                            pv, pvB = psv.next()
                            sg, sgB = sgt.next()
                            for half in range(2):
                                col0 = half * D + c * 128
                                for kk in range(8):
                                    S.op(PE, lambda e, kk=kk, pv=pv, half=half, col0=col0: e.matmul(
                                        pv[:, half, :W_], lhsT=wcg[:, kk, col0:col0 + 128], rhs=hT[:, kk, :W_],
                                        start=(kk == 0), stop=(kk == 7)), R=hR, W=[pvB], sig=(kk == 7 and half == 1))
                            S.op(ACT, lambda e, pv=pv, sg=sg: e.activation(out=sg[:, :W_], in_=pv[:, 1, :W_], func=AF.Sigmoid), R=[pvB], W=[sgB])
                            S.op(DVE, lambda e, pv=pv, sg=sg, c=c: e.tensor_tensor(out=uT[:, c, 30:30 + W_], in0=pv[:, 0, :W_], in1=sg[:, :W_], op=ALU.mult),
                                 R=[pvB, sgB], W=[uTB[c]])
                            pair.append(c)
                        for c in pair:
                            S.op(DVE, lambda e, c=c: e.tensor_scalar(out=yv[:, c, :W_], in0=uT[:, c, 0:W_], scalar1=wdw[:, c, 0:1],
                                                                     scalar2=bdw[:, c:c + 1], op0=ALU.mult, op1=ALU.add),
                                 R=[uTB[c], cB], W=[yB[c]])
                        for tap in range(1, CONVK):
                            for c in pair:
                                S.op(DVE, lambda e, c=c, tap=tap: e.scalar_tensor_tensor(
                                    out=yv[:, c, :W_], in0=uT[:, c, tap:tap + W_], scalar=wdw[:, c, tap:tap + 1], in1=yv[:, c, :W_],
                                    op0=ALU.mult, op1=ALU.add), R=[uTB[c], yB[c]], W=[yB[c]])
                        for c in pair:
                            S.op(POOL, lambda e, c=c: e.tensor_copy(out=uT[:, c, 0:30], in_=uT[:, c, W_:W_ + 30]), R=[uTB[c]], W=[uTB[c]])
                    S.op(POOL, lambda e: e.tensor_copy(out=ybf[:, :, :W_], in_=yv[:, :, :W_]), R=yB, W=[ybfB])
                    S.op(ACT, lambda e: e.activation(out=ysq[:, :, :W_], in_=yv[:, :, :W_], func=AF.Square), R=yB, W=[ysqB])
                    pS, pSB = psS
                    for c in range(8):
                        S.op(PE, lambda e, c=c: e.matmul(pS[:, 0, :W_], lhsT=ones_bf[:], rhs=ybf[:, c, :W_], start=(c == 0), stop=(c == 7)),
                             R=[ybfB], W=[pSB], sig=False)
                    for c in range(8):
                        S.op(PE, lambda e, c=c: e.matmul(pS[:, 1, :W_], lhsT=ones_bf[:], rhs=ysq[:, c, :W_], start=(c == 0), stop=(c == 7)),
                             R=[ysqB], W=[pSB], sig=(c == 7))
                    mean_, msq_, var_, lnv_, rstd_ = (stt_[:, i, :W_] for i in range(5))
                    S.op(ACT, lambda e: e.activation(out=mean_, in_=pS[:, 0, :W_], func=AF.Copy, scale=1.0 / D), R=[pSB], W=[stB])
                    S.op(ACT, lambda e: e.activation(out=msq_, in_=mean_, func=AF.Square), R=[stB], W=[stB])
                    S.op(DVE, lambda e: e.scalar_tensor_tensor(out=var_, in0=pS[:, 1, :W_], scalar=1.0 / D, in1=msq_, op0=ALU.mult, op1=ALU.subtract),
                         R=[pSB, stB], W=[stB])
                    S.op(DVE, lambda e: e.tensor_scalar(out=var_, in0=var_, scalar1=0.0, scalar2=None, op0=ALU.max), R=[stB], W=[stB])
                    S.op(ACT, lambda e: e.activation(out=lnv_, in_=var_, func=AF.Ln, bias=EPS), R=[stB], W=[stB])
                    S.op(ACT, lambda e: e.activation(out=rstd_, in_=lnv_, func=AF.Exp, scale=-0.5), R=[stB], W=[stB])
                    S.op(DVE, lambda e: e.tensor_tensor(out=yv[:, :, :W_], in0=yv[:, :, :W_],
                                                        in1=mean_.unsqueeze(1).broadcast_to([128, 8, W_]), op=ALU.subtract), R=yB + [stB], W=yB)
                    S.op(DVE, lambda e: e.tensor_tensor(out=yv[:, :, :W_], in0=yv[:, :, :W_],
                                                        in1=rstd_.unsqueeze(1).broadcast_to([128, 8, W_]), op=ALU.mult), R=yB + [stB], W=yB)
                    for c in range(8):
                        S.op(ACT, lambda e, c=c: e.activation(out=actT[:, c, :W_], in_=yv[:, c, :W_], func=AF.Silu,
                                                              scale=gln[:, c:c + 1], bias=bln[:, c:c + 1]), R=[yB[c], cB], W=[actB[c]])
                    for oc in range(8):
                        pA, pAB = psA
                        pB, pBB = psB_
                        s2, s2B = sg2.next()
                        mm1, m1B = m1.next()
                        for gi in range(2):
                            col0 = 2 * D + gi * D + oc * 128
                            for kk in range(8):
                                S.op(PE, lambda e, kk=kk, gi=gi, col0=col0: e.matmul(pA[:, gi, :W_], lhsT=wcg[:, kk, col0:col0 + 128], rhs=hT[:, kk, :W_],
                                                                                     start=(kk == 0), stop=(kk == 7)), R=hR, W=[pAB],
                                     sig=(kk == 7 and gi == 1))
                        for j in range(4):
                            S.op(PE, lambda e, j=j: e.matmul(pB[:, 0, :W_], lhsT=wap[:, j, oc * 128:(oc + 1) * 128], rhs=attnT[:, j, :W_],
                                                             start=(j == 0), stop=(j == 3)), R=aR, W=[pBB], sig=False)
                        for kk in range(8):
                            S.op(PE, lambda e, kk=kk: e.matmul(pB[:, 1, :W_], lhsT=wco[:, kk, oc * 128:(oc + 1) * 128], rhs=actT[:, kk, :W_],
                                                               start=(kk == 0), stop=(kk == 7)), R=actB, W=[pBB], sig=(kk == 7))
                        S.op(ACT, lambda e, s2=s2: e.activation(out=s2[:, :, :W_], in_=pA[:, :, :W_], func=AF.Sigmoid), R=[pAB], W=[s2B])
                        S.op(DVE, lambda e, s2=s2, mm1=mm1: e.tensor_tensor(out=mm1[:, :, :W_], in0=pB[:, :, :W_], in1=s2[:, :, :W_], op=ALU.mult),
                             R=[pBB, s2B], W=[m1B])
                        S.op(POOL, lambda e, mm1=mm1, oc=oc: e.tensor_tensor(out=mT[:, oc, :W_], in0=mm1[:, 0, :W_], in1=mm1[:, 1, :W_], op=ALU.add),
                             R=[m1B], W=[mTB[oc]])
                    for bi in range(nbk):
                        blk = blk0 + bi
                        xt, xtB = xl[bi]
                        for hf in range(2):
                            po, poB = pso.next()
                            for kk in range(8):
                                S.op(PE, lambda e, kk=kk, po=po, bi=bi, hf=hf: e.matmul(po[:], lhsT=mT[:, kk, bi * 128:(bi + 1) * 128],
                                                                                        rhs=wo[:, kk, hf * 512:(hf + 1) * 512],
                                                                                        start=(kk == 0), stop=(kk == 7)), R=mTB, W=[poB], sig=(kk == 7))
                            S.op(DVE, lambda e, po=po, hf=hf: e.tensor_tensor(out=xm[:, hf * 512:(hf + 1) * 512], in0=po[:],
                                                                              in1=gtb[:, hf * 512:(hf + 1) * 512], op=ALU.mult), R=[poB, cB], W=[xmB])
                        S.op(POOL, lambda e, xt=xt: e.tensor_tensor(out=xm[:], in0=xm[:], in1=xt[:], op=ALU.add), R=[xtB, xmB], W=[xmB])
                        S.dma(POOL, XM[blk * 128:(blk + 1) * 128, :], xm[:], R=[xmB])
                    blk0 += nbk
                S.barrier()

        def phase4(l):
            WT = 256
            last = (l == L - 1)
            with contextlib.ExitStack() as es:
                wfi = sbuf(es, [128, 8, 2 * DFF], BF16, "wfi")
                wfd = sbuf(es, [128, 22, D], BF16, "wfd")
                with contextlib.ExitStack() as es2:
                    rot = Ring([DVE, POOL, ACT])
                    load_weight(es2, wfi, w_ffn_in[l], 8, 2 * DFF, rot)
                    load_weight(es2, wfd, w_ffn_down[l], 22, D, rot)
                    S.barrier()
                cB = Buf()
                Gm = sbuf(es, [128, D], F32, "Gm")
                SHb = sbuf(es, [128, D], F32, "SHb")
                gtb = sbuf(es, [128, D], F32, "gtb")
                wf3 = sbuf(es, [128, 22, 3], F32, "wf3")
                bfv = sbuf(es, [128, 22], F32, "bfv")
                xo = sbuf(es, [128, D], F32, "xo")
                load_bcast(es, xo, g_norm2[l], cB)
                S.dma(SP, Gm[:], MOD[l][:, 4 * D:5 * D], W=[cB])
                S.dma(SP, SHb[:], MOD[l][:, 3 * D:4 * D], W=[cB])
                S.dma(SP, gtb[:], MOD[l][:, 5 * D:6 * D], W=[cB])
                S.dma(SP, wf3[:], wf3_t[l], W=[cB])
                S.dma(SP, bfv[:], bf_t[l], W=[cB])
                S.op(DVE, lambda e: e.scalar_tensor_tensor(out=Gm[:], in0=Gm[:], scalar=1.0, in1=xo[:], op0=ALU.add, op1=ALU.mult),
                     R=[cB], W=[cB])
                xoB = Buf()
                xoB.r.append(cB.w)
                norm = NormCtx(es, Gm, SHb, cB)
                xts = Ring([(sbuf(es, [128, D], F32, "xt"), Buf()) for _ in range(2)])
                hTs = Ring([(sbuf(es, [128, 8, WT], BF16, "hT"), [Buf(), Buf()]) for _ in range(2)])
                carry = sbuf(es, [128, 22, 2], F32, "carry"); carB = [Buf() for _ in range(22)]
                S.op(POOL, lambda e: e.memset(carry[:], 0.0), W=carB)
                gbs = Ring([(sbuf(es, [128, 2 + WT], F32, "gb"), Buf()) for _ in range(4)])
                accs = Ring([(sbuf(es, [128, WT], F32, "acc"), Buf()) for _ in range(4)])
                sils = Ring([(sbuf(es, [128, WT], F32, "sil"), Buf()) for _ in range(2)])
                actT = sbuf(es, [128, 22, WT], BF16, "actT"); actB = [Buf() for _ in range(22)]
                psg = Ring([(psum(es, [128, 2, WT], F32, "psg"), Buf()) for _ in range(4)])
                pso = Ring([(psum(es, [128, 512], F32, "pso"), Buf()) for _ in range(2)])
                blk0 = Mb[l]
                while blk0 < NB:
                    nbk = min(2, NB - blk0)
                    W_ = 128 * nbk
                    hT, hTBs = hTs.next()
                    xl = []
                    for bi in range(nbk):
                        blk = blk0 + bi
                        xt, xtB = xts.next()
                        xl.append((xt, xtB))
                        S.dma(SP, xt[:], XM[blk * 128:(blk + 1) * 128, :], W=[xtB])
                        norm.run(xt, xtB, blk, hT[:, :, bi * 128:(bi + 1) * 128], hTBs[bi], ACT)
                    hR = hTBs[:nbk]
                    for fp in range(11):
                        items = []
                        for fi in range(2):
                            f = 2 * fp + fi
                            pg, pgB = psg.next()
                            gb, gbB = gbs.next()
                            acc, accB = accs.next()
                            for half in range(2):
                                col0 = half * DFF + f * 128
                                for kk in range(8):
                                    S.op(PE, lambda e, kk=kk, pg=pg, half=half, col0=col0: e.matmul(
                                        pg[:, half, :W_], lhsT=wfi[:, kk, col0:col0 + 128], rhs=hT[:, kk, :W_],
                                        start=(kk == 0), stop=(kk == 7)), R=hR, W=[pgB], sig=(kk == 7 and half == 1))
                            S.op(POOL, lambda e, gb=gb, f=f: e.tensor_copy(out=gb[:, 0:2], in_=carry[:, f, :]), R=[carB[f]], W=[gbB])
                            S.op(ACT, lambda e, gb=gb, pg=pg: e.copy(out=gb[:, 2:2 + W_], in_=pg[:, 0, :W_]), R=[pgB], W=[gbB])
                            S.op(POOL, lambda e, gb=gb, f=f: e.tensor_copy(out=carry[:, f, :], in_=gb[:, W_:W_ + 2]), R=[gbB], W=[carB[f]])
                            items.append((f, pg, pgB, gb, gbB, acc, accB))
                        for (f, pg, pgB, gb, gbB, acc, accB) in items:
                            S.op(DVE, lambda e, f=f, gb=gb, acc=acc: e.tensor_scalar(out=acc[:, :W_], in0=gb[:, 0:W_], scalar1=wf3[:, f, 0:1],
                                                                                     scalar2=bfv[:, f:f + 1], op0=ALU.mult, op1=ALU.add),
                                 R=[gbB, cB], W=[accB])
                        for tap in (1, 2):
                            for (f, pg, pgB, gb, gbB, acc, accB) in items:
                                S.op(DVE, lambda e, f=f, gb=gb, acc=acc, tap=tap: e.scalar_tensor_tensor(
                                    out=acc[:, :W_], in0=gb[:, tap:tap + W_], scalar=wf3[:, f, tap:tap + 1], in1=acc[:, :W_],
                                    op0=ALU.mult, op1=ALU.add), R=[gbB, accB], W=[accB])
                        for (f, pg, pgB, gb, gbB, acc, accB) in items:
                            sl, slB = sils.next()
                            S.op(ACT, lambda e, sl=sl, acc=acc: e.activation(out=sl[:, :W_], in_=acc[:, :W_], func=AF.Silu), R=[accB], W=[slB])
                            S.op(DVE, lambda e, sl=sl, pg=pg, f=f: e.tensor_tensor(out=actT[:, f, :W_], in0=pg[:, 1, :W_], in1=sl[:, :W_], op=ALU.mult),
                                 R=[pgB, slB], W=[actB[f]])
                    for bi in range(nbk):
                        blk = blk0 + bi
                        xt, xtB = xl[bi]
                        for hf in range(2):
                            po, poB = pso.next()
                            for f in range(22):
                                S.op(PE, lambda e, f=f, po=po, bi=bi, hf=hf: e.matmul(po[:], lhsT=actT[:, f, bi * 128:(bi + 1) * 128],
                                                                                      rhs=wfd[:, f, hf * 512:(hf + 1) * 512],
                                                                                      start=(f == 0), stop=(f == 21)), R=actB, W=[poB], sig=(f == 21))
                            S.op(DVE, lambda e, po=po, hf=hf: e.tensor_tensor(out=xo[:, hf * 512:(hf + 1) * 512], in0=po[:],
                                                                              in1=gtb[:, hf * 512:(hf + 1) * 512], op=ALU.mult), R=[poB, cB], W=[xoB])
                        S.op(POOL, lambda e, xt=xt: e.tensor_tensor(out=xo[:], in0=xo[:], in1=xt[:], op=ALU.add), R=[xtB, xoB], W=[xoB])
                        if last:
                            if blk >= OWN0:
                                S.dma(POOL, y_out[(blk - OWN0) * 128:(blk - OWN0 + 1) * 128, :], xo[:], R=[xoB])
                        else:
                            S.dma(POOL, XA[blk * 128:(blk + 1) * 128, :], xo[:], R=[xoB])
                    blk0 += nbk
                S.barrier()

        S.barrier()
        prologue()
        for l in range(L):
            phase1(l)
            phase2(l)
            phase3(l)
            phase4(l)
        S.barrier()
    return nc, S.n_ins


def make_in_maps(inputs, L=4, cores=range(8)):
    Kb, Mb, OWN0, NB = geometry(L)
    NTOK = NB * 128
    kcols = keyset_columns(L)
    x = np.asarray(inputs["x"], dtype=np.float32)
    c = np.asarray(inputs["c"], dtype=np.float32)
    positions = np.asarray(inputs["positions"], dtype=np.int32)
    f32 = lambda k: np.ascontiguousarray(np.asarray(inputs[k], dtype=np.float32)[:L])
    shared = {k: f32(k) for k in ("w_ada", "b_ada", "g_norm1", "w_in", "g_q", "g_k", "w_attn_proj", "w_conv_out",
                                  "w_o", "g_norm2", "w_ffn_in", "w_ffn_down")}
    wdw = f32("w_conv_dw")
    shared["wdw_t"] = np.ascontiguousarray(wdw.reshape(L, CONVK, 8, 128).transpose(0, 3, 2, 1))
    per_ch = lambda a, n: np.ascontiguousarray(a.reshape(L, n, 128).transpose(0, 2, 1))
    shared["bdw_t"] = per_ch(f32("b_conv_dw"), 8)
    shared["gln_t"] = per_ch(f32("g_conv_ln"), 8)
    shared["bln_t"] = per_ch(f32("b_conv_ln"), 8)
    wf = f32("w_ffn_dw")
    shared["wf3_t"] = np.ascontiguousarray(wf.reshape(L, 3, 22, 128).transpose(0, 3, 2, 1))
    shared["bf_t"] = per_ch(f32("b_ffn_dw"), 22)
    inv = (np.float32(500000.0) ** (-np.arange(0, 32, 2, dtype=np.float32) / np.float32(32))).astype(np.float32)
    shared["invf"] = np.ascontiguousarray(np.broadcast_to(inv[None, :], (128, 16))).astype(np.float32)
    shared["ident"] = np.eye(128, dtype=np.float32)
    kk, qq = np.meshgrid(np.arange(128), np.arange(128), indexing="ij")
    shared["maskp"] = np.where(kk >= qq, 0.0, NEG).astype(np.float32)
    shared["maskc"] = np.where(kk <= qq, 0.0, NEG).astype(np.float32)
    maps = []
    for core in cores:
        b, j = core // 4, core % 4
        off = 4096 * j - OWN0 * 128
        gpos = off + np.arange(NTOK)
        valid = gpos >= 0
        gsafe = np.clip(gpos, 0, SEQ - 1)
        xw = np.where(valid[:, None], x[b, gsafe, :], np.float32(0.0)).astype(np.float32)
        pw = np.where(valid, positions[b, gsafe], 0).astype(np.int32)
        vb = valid.reshape(NB, 128)[:, 0].astype(np.float32)
        kbt = np.zeros((128, len(kcols)), dtype=np.float32)
        for (d, start), col in kcols.items():
            toks = np.clip(start + d * np.arange(128), 0, NTOK - 1)
            kbt[:, col] = np.where(valid[toks], 0.0, NEG)
        m = dict(shared)
        m["xw"] = np.ascontiguousarray(xw)
        m["pos_t"] = np.ascontiguousarray(pw.reshape(NB, 128).T)
        m["vcol"] = np.ascontiguousarray(np.broadcast_to(vb[None, :], (128, NB))).astype(np.float32)
        m["kbias"] = kbt
        m["c_t"] = np.ascontiguousarray(c[b].reshape(8, 128).T)
        maps.append(m)
    return maps


_CACHE = {}


def kernel(**inputs):
    L = 4
    if L not in _CACHE:
        _CACHE[L] = build_program(L)[0]
    nc = _CACHE[L]
    maps = make_in_maps(inputs, L)
    res = run_bass_kernel_spmd(nc, maps, core_ids=list(range(8)))
    out = np.empty((2, SEQ, D), dtype=np.float32)
    for core in range(8):
        b, j = core // 4, core % 4
        out[b, 4096 * j:4096 * (j + 1), :] = res.results[core]["y"]
    return out
```

```python
import contextlib
import os
import numpy as np
import ml_dtypes

import concourse.bass as bass
import concourse.mybir as mybir
from concourse.bass_utils import run_bass_kernel_spmd

F32 = mybir.dt.float32
BF16 = mybir.dt.bfloat16
I32 = mybir.dt.int32
AF = mybir.ActivationFunctionType
ALU = mybir.AluOpType
AX = mybir.AxisListType

D = 1024
NHEAD = 12
DH = 128
AW = 1536
DFF = 2816
INW = 8704
CONVK = 31
EPS = 1e-6
NEG = -30000.0
SCALE = DH ** -0.5
OWN_BLK = 32
SEQ = 16384
DILS = (1, 4, 16)
TWO_PI = 6.283185307179586
C1 = 6.28125
C2 = TWO_PI - C1


def geometry(L):
    Kb = [17 * i for i in range(L)]
    Mb = [k + 16 for k in Kb]
    own0 = 17 * L
    nb = own0 + OWN_BLK
    return Kb, Mb, own0, nb


def attn_units(L):
    Kb, Mb, own0, nb = geometry(L)
    out = []
    for l in range(L):
        q0, q1 = Mb[l] * 128, nb * 128
        lst = []
        for g, d in enumerate(DILS):
            span = 128 * d
            c0 = q0
            while c0 < q1:
                n = min(span, q1 - c0)
                for r in range(d):
                    lst.append((g, d, c0 + r, n // d))
                c0 += span
        out.append(lst)
    return out


def keyset_columns(L):
    cols = {}
    for lst in attn_units(L):
        for (g, d, base, nq) in lst:
            for ks in ((d, base - 128 * d), (d, base)):
                if ks not in cols:
                    cols[ks] = len(cols)
    return cols


class Tok:
    __slots__ = ("eng", "sem", "val")

    def __init__(self, eng, sem, val):
        self.eng, self.sem, self.val = eng, sem, val


class Buf:
    __slots__ = ("name", "w", "r")

    def __init__(self, name=""):
        self.name, self.w, self.r = name, None, []


class Eng:
    def __init__(self, name, e, sem, self_sync):
        self.name, self.e, self.sem, self.self_sync = name, e, sem, self_sync
        self.count = 0
        self.waited = {}
        self.pending = []
        self.slots = []
        self.slot_i = 0


class Sched:
    def __init__(self, nc, es, ndma=14):
        self.nc = nc

        def sem(n):
            return es.enter_context(nc.semaphore(n))

        self.PE = Eng("pe", nc.tensor, sem("s_pe"), False)
        self.ACT = Eng("act", nc.scalar, sem("s_act"), True)
        self.DVE = Eng("dve", nc.vector, sem("s_dve"), True)
        self.POOL = Eng("pool", nc.gpsimd, sem("s_pool"), True)
        self.SP = Eng("sp", nc.sync, sem("s_sp"), True)
        self.engs = [self.PE, self.ACT, self.DVE, self.POOL, self.SP]
        for q in (self.SP, self.POOL, self.ACT):
            nq_ = int(os.environ.get("POOLDMA", "14")) if q is self.POOL else ndma
            q.slots = [[sem("d_%s%d" % (q.name, i)), 0] for i in range(nq_)]
        self.n_ins = 0

    def _wait(self, E, sem, val):
        key = id(sem)
        if E.waited.get(key, 0) < val:
            E.e.wait_ge(sem, val)
            E.waited[key] = val
            self.n_ins += 1

    def _deps(self, E, R, W):
        toks = []
        for b in R:
            if b.w is not None:
                toks.append(b.w)
        for b in W:
            if b.w is not None:
                toks.append(b.w)
            toks.extend(b.r)
        for t in toks:
            if t.eng is E and not E.self_sync:
                continue
            if t.val is None:
                if t.eng is E:
                    continue
                raise RuntimeError("dependency on unsignalled instruction (%s)" % t.eng.name)
            self._wait(E, t.sem, t.val)

    def op(self, E, fn, R=(), W=(), sig=True):
        self._deps(E, R, W)
        ins = fn(E.e)
        self.n_ins += 1
        tok = Tok(E, E.sem, None)
        E.pending.append(tok)
        if sig:
            E.count += 1
            ins.then_inc(E.sem, 1)
            for t in E.pending:
                t.val = E.count
            E.pending = []
        for b in W:
            b.w = tok
            b.r = []
        for b in R:
            b.r = [t for t in b.r if t.eng is not E or t.eng is None]
            b.r.append(tok)
        return ins

    def dma(self, Q, out, in_, R=(), W=()):
        self._deps(Q, R, W)
        slot = Q.slots[Q.slot_i % len(Q.slots)]
        Q.slot_i += 1
        if slot[1] > 0:
            self._wait(Q, slot[0], slot[1])
        slot[1] += 16
        Q.e.dma_start(out=out, in_=in_).then_inc(slot[0], 16)
        self.n_ins += 1
        tok = Tok(None, slot[0], slot[1])
        for b in W:
            b.w = tok
            b.r = []
        for b in R:
            b.r.append(tok)

    def barrier(self):
        for E in self.engs:
            if E.pending:
                raise RuntimeError("pending unsignalled instructions at barrier on " + E.name)
        for E in self.engs:
            for F in self.engs:
                if F is not E and F.count > 0:
                    self._wait(E, F.sem, F.count)
            for Q in (self.SP, self.POOL, self.ACT):
                for s in Q.slots:
                    if s[1] > 0:
                        self._wait(E, s[0], s[1])


class Ring:
    def __init__(self, items):
        self.items = items
        self.i = 0

    def next(self):
        it = self.items[self.i % len(self.items)]
        self.i += 1
        return it


def build_program(L=4, debug=False, stop_after=99, maxblk=None):
    Kb, Mb, OWN0, NB = geometry(L)
    NTOK = NB * 128
    units = attn_units(L)
    kcols = keyset_columns(L)
    NKS = len(kcols)

    nc = bass.Bass("TRN2", target_bir_lowering=False)

    def din(name, shape, dt=F32):
        return nc.dram_tensor(name, list(shape), dt, kind="ExternalInput").ap()

    dbg_kind = "ExternalOutput" if debug else "Internal"

    def dscr(name, shape, dt=F32):
        return nc.dram_tensor(name, list(shape), dt, kind=dbg_kind).ap()

    xw = din("xw", [NTOK, D])
    pos_t = din("pos_t", [128, NB], I32)
    vcol_d = din("vcol", [128, NB])
    kb_d = din("kbias", [128, NKS])
    c_t = din("c_t", [128, 8])
    invf_d = din("invf", [128, 16])
    ident_d = din("ident", [128, 128])
    maskp_d = din("maskp", [128, 128])
    maskc_d = din("maskc", [128, 128])
    w_ada = din("w_ada", [L, D, 6 * D])
    b_ada = din("b_ada", [L, 6 * D])
    g_norm1 = din("g_norm1", [L, D])
    w_in = din("w_in", [L, D, INW])
    g_q = din("g_q", [L, DH])
    g_k = din("g_k", [L, DH])
    w_attn_proj = din("w_attn_proj", [L, 512, D])
    wdw_t = din("wdw_t", [L, 128, 8, CONVK])
    bdw_t = din("bdw_t", [L, 128, 8])
    gln_t = din("gln_t", [L, 128, 8])
    bln_t = din("bln_t", [L, 128, 8])
    w_conv_out = din("w_conv_out", [L, D, D])
    w_o = din("w_o", [L, D, D])
    g_norm2 = din("g_norm2", [L, D])
    w_ffn_in = din("w_ffn_in", [L, D, 2 * DFF])
    wf3_t = din("wf3_t", [L, 128, 22, 3])
    bf_t = din("bf_t", [L, 128, 22])
    w_ffn_down = din("w_ffn_down", [L, DFF, D])
    y_out = nc.dram_tensor("y", [OWN_BLK * 128, D], F32, kind="ExternalOutput").ap()

    QS = dscr("QS", [NTOK, AW], BF16)
    KS = dscr("KS", [NTOK, AW], BF16)
    XA = dscr("XA", [NTOK, D])
    XM = dscr("XM", [NTOK, D])
    VS = dscr("VS", [NTOK, NHEAD * 129], BF16)
    ATT = [dscr("ATT%d" % g, [NTOK, 516]) for g in range(3)]
    MOD = dscr("MOD", [L, 128, 6 * D])
    ROPE = dscr("ROPE", [2, 128, NB * 16])
    YB = dscr("YB", [8, 128, NTOK], BF16)

    uid = [0]

    def nm(p):
        uid[0] += 1
        return "%s_%d" % (p, uid[0])

    with contextlib.ExitStack() as top:
        S = Sched(nc, top)
        PE, ACT, DVE, POOL, SP = S.PE, S.ACT, S.DVE, S.POOL, S.SP

        def sbuf(es, shape, dt, p="t"):
            return es.enter_context(nc.sbuf_tensor(nm(p), list(shape), dt))

        def psum(es, shape, dt, p="ps"):
            return es.enter_context(nc.psum_tensor(nm(p), list(shape), dt))

        ident = sbuf(top, [128, 128], BF16, "ident")
        maskP = sbuf(top, [128, 128], BF16, "maskp")
        maskC = sbuf(top, [128, 128], BF16, "maskc")
        ones_bf = sbuf(top, [128, 128], BF16, "ones")
        vcol = sbuf(top, [128, NB], F32, "vcol")
        kb = sbuf(top, [128, NKS], F32, "kb")
        gB = Buf("glob")
        S.dma(POOL, ident[:], ident_d, W=[gB])
        S.dma(POOL, maskP[:], maskp_d, W=[gB])
        S.dma(POOL, maskC[:], maskc_d, W=[gB])
        S.dma(SP, vcol[:], vcol_d, W=[gB])
        S.dma(SP, kb[:], kb_d, W=[gB])
        S.op(DVE, lambda e: e.memset(ones_bf[:], 1.0), W=[Buf()])

        def load_weight(es_stage, dst, src, kc, ncol, rot):
            PIECE = 2048
            stg = [(sbuf(es_stage, [128, PIECE], F32, "stg"), Buf()) for _ in range(3)]
            ring = Ring(stg)
            for kk in range(kc):
                for c0 in range(0, ncol, PIECE):
                    cw = min(PIECE, ncol - c0)
                    st, sb_ = ring.next()
                    S.dma(SP, st[:, :cw], src[kk * 128:(kk + 1) * 128, c0:c0 + cw], W=[sb_])
                    E = rot.next()
                    if E is ACT:
                        S.op(E, lambda e, st=st, kk=kk, c0=c0, cw=cw: e.copy(out=dst[:, kk, c0:c0 + cw], in_=st[:, :cw]),
                             R=[sb_], W=[Buf()])
                    else:
                        S.op(E, lambda e, st=st, kk=kk, c0=c0, cw=cw: e.tensor_copy(out=dst[:, kk, c0:c0 + cw], in_=st[:, :cw]),
                             R=[sb_], W=[Buf()])

        def rsqrt_small(dst, src, scale, srcB, dstB, tmp, tmpB):
            S.op(ACT, lambda e: e.activation(out=tmp, in_=src, func=AF.Ln, scale=scale, bias=EPS), R=[srcB], W=[tmpB])
            S.op(ACT, lambda e: e.activation(out=dst, in_=tmp, func=AF.Exp, scale=-0.5), R=[tmpB], W=[dstB])

        class NormCtx:
            def __init__(self, es, Gm, SHb, constB):
                self.Gm, self.SHb, self.constB = Gm, SHb, constB
                self.junk = sbuf(es, [128, D], BF16, "junk")
                self.junkB = Buf()
                self.tmp = Ring([(sbuf(es, [128, D], F32, "ntmp"), Buf()) for _ in range(1)])
                self.h = Ring([(sbuf(es, [128, D], BF16, "h"), Buf()) for _ in range(2)])
                self.st = Ring([(sbuf(es, [128, 4], F32, "nst"), Buf()) for _ in range(2)])
                self.tp = Ring([(psum(es, [128, 8, 128], BF16, "tp"), Buf()) for _ in range(1)])

            def run(self, xt, xtB, blk, hT_dst, hTB, evac_eng):
                junk, junkB = self.junk, self.junkB
                tmp, tmpB = self.tmp.next()
                h, hB = self.h.next()
                st, stB = self.st.next()
                tp, tpB = self.tp.next()
                S.op(ACT, lambda e: e.activation(out=junk[:], in_=xt[:], func=AF.Square, accum_out=st[:, 0:1]),
                     R=[xtB], W=[junkB, stB])
                S.op(ACT, lambda e: e.activation(out=st[:, 1:2], in_=st[:, 0:1], func=AF.Ln, scale=1.0 / D, bias=EPS),
                     R=[stB], W=[stB])
                S.op(ACT, lambda e: e.activation(out=st[:, 2:3], in_=st[:, 1:2], func=AF.Exp, scale=-0.5),
                     R=[stB], W=[stB])
                S.op(DVE, lambda e: e.tensor_tensor(out=st[:, 3:4], in0=st[:, 2:3], in1=vcol[:, blk:blk + 1], op=ALU.mult),
                     R=[stB], W=[stB])
                S.op(DVE, lambda e: e.scalar_tensor_tensor(out=tmp[:], in0=xt[:], scalar=st[:, 3:4], in1=self.Gm[:],
                                                           op0=ALU.mult, op1=ALU.mult),
                     R=[xtB, stB, self.constB], W=[tmpB])
                S.op(DVE, lambda e: e.scalar_tensor_tensor(out=h[:], in0=self.SHb[:], scalar=vcol[:, blk:blk + 1], in1=tmp[:],
                                                           op0=ALU.mult, op1=ALU.add),
                     R=[tmpB, self.constB], W=[hB])
                for kk in range(8):
                    S.op(PE, lambda e, kk=kk: e.transpose(out=tp[:, kk, :], in_=h[:, kk * 128:(kk + 1) * 128], identity=ident[:]),
                         R=[hB], W=[tpB], sig=(kk == 7))
                if evac_eng is ACT:
                    S.op(ACT, lambda e: e.copy(out=hT_dst, in_=tp[:]), R=[tpB], W=[hTB])
                else:
                    S.op(evac_eng, lambda e: e.tensor_copy(out=hT_dst, in_=tp[:]), R=[tpB], W=[hTB])

        def load_bcast(es, dst, src_row, B):
            S.dma(SP, dst[:], src_row.partition_broadcast(128), W=[B])

        def prologue():
            with contextlib.ExitStack() as es:
                NE = NB * 16
                pi = sbuf(es, [128, NB], I32)
                pf = sbuf(es, [128, NB], F32)
                invf = sbuf(es, [128, 16], F32)
                ang = sbuf(es, [128, NB, 16], F32)
                kf = sbuf(es, [128, NB, 16], F32)
                ki = sbuf(es, [128, NB, 16], I32)
                r = sbuf(es, [128, NB, 16], F32)
                m = sbuf(es, [128, NB, 16], F32)
                r2 = sbuf(es, [128, NB, 16], F32)
                sn = sbuf(es, [128, NB, 16], F32)
                cs = sbuf(es, [128, NB, 16], F32)
                B = Buf()
                S.dma(SP, pi[:], pos_t, W=[B])
                S.dma(SP, invf[:], invf_d, W=[B])
                V = lambda fn, R=(B,), W=(B,): S.op(DVE, fn, R=list(R), W=list(W))
                V(lambda e: e.tensor_copy(out=pf[:], in_=pi[:]))
                V(lambda e: e.tensor_tensor(out=ang[:], in0=pf[:].unsqueeze(2).broadcast_to([128, NB, 16]),
                                            in1=invf[:].unsqueeze(1).broadcast_to([128, NB, 16]), op=ALU.mult))
                V(lambda e: e.tensor_scalar(out=kf[:], in0=ang[:], scalar1=1.0 / TWO_PI, scalar2=None, op0=ALU.mult))
                V(lambda e: e.tensor_copy(out=ki[:], in_=kf[:]))
                V(lambda e: e.tensor_copy(out=kf[:], in_=ki[:]))
                V(lambda e: e.scalar_tensor_tensor(out=r[:], in0=kf[:], scalar=-C1, in1=ang[:], op0=ALU.mult, op1=ALU.add))
                V(lambda e: e.scalar_tensor_tensor(out=r[:], in0=kf[:], scalar=-C2, in1=r[:], op0=ALU.mult, op1=ALU.add))

                def wrap(t):
                    V(lambda e: e.tensor_scalar(out=m[:], in0=t[:], scalar1=np.pi, scalar2=-TWO_PI, op0=ALU.is_gt, op1=ALU.mult))
                    V(lambda e: e.tensor_tensor(out=t[:], in0=t[:], in1=m[:], op=ALU.add))
                    V(lambda e: e.tensor_scalar(out=m[:], in0=t[:], scalar1=-np.pi, scalar2=TWO_PI, op0=ALU.is_lt, op1=ALU.mult))
                    V(lambda e: e.tensor_tensor(out=t[:], in0=t[:], in1=m[:], op=ALU.add))
                    V(lambda e: e.tensor_scalar(out=t[:], in0=t[:], scalar1=3.1415925, scalar2=-3.1415925, op0=ALU.min, op1=ALU.max))

                wrap(r)
                V(lambda e: e.tensor_scalar(out=r2[:], in0=r[:], scalar1=np.pi / 2, scalar2=None, op0=ALU.add))
                wrap(r2)
                S.op(ACT, lambda e: e.activation(out=sn[:], in_=r[:], func=AF.Sin), R=[B], W=[B])
                S.op(ACT, lambda e: e.activation(out=cs[:], in_=r2[:], func=AF.Sin), R=[B], W=[B])
                S.dma(SP, ROPE[0], cs[:].rearrange("p b i -> p (b i)"), R=[B])
                S.dma(SP, ROPE[1], sn[:].rearrange("p b i -> p (b i)"), R=[B])
                S.barrier()
            with contextlib.ExitStack() as es:
                ct = sbuf(es, [128, 8], F32)
                ca = sbuf(es, [128, 8], F32)
                crep = sbuf(es, [128, 8, 128], F32)
                B = Buf()
                S.dma(SP, ct[:], c_t, W=[B])
                S.op(ACT, lambda e: e.activation(out=ca[:], in_=ct[:], func=AF.Silu), R=[B], W=[B])
                S.op(DVE, lambda e: e.tensor_copy(out=crep[:], in_=ca[:].unsqueeze(2).broadcast_to([128, 8, 128])), R=[B], W=[B])
                stg = Ring([(sbuf(es, [128, 8, 512], F32, "astg"), Buf()) for _ in range(2)])
                pss = Ring([(psum(es, [128, 512], F32, "aps"), Buf()) for _ in range(2)])
                bada = sbuf(es, [128, 6 * D], F32)
                modt = sbuf(es, [128, 6 * D], F32)
                badaB, modB = Buf(), Buf()
                for l in range(L):
                    load_bcast(es, bada, b_ada[l], badaB)
                    wv = w_ada[l].rearrange("(k p) c -> p k c", p=128)
                    for ctile in range(12):
                        st, stB = stg.next()
                        ps, psB = pss.next()
                        S.dma(SP, st[:], wv[:, :, ctile * 512:(ctile + 1) * 512], W=[stB])
                        for kk in range(8):
                            S.op(PE, lambda e, kk=kk, st=st, ps=ps: e.matmul(ps[:], lhsT=crep[:, kk, :], rhs=st[:, kk, :],
                                                                             start=(kk == 0), stop=(kk == 7)),
                                 R=[stB, B], W=[psB], sig=(kk == 7))
                        S.op(DVE, lambda e, ps=ps, ctile=ctile: e.tensor_tensor(out=modt[:, ctile * 512:(ctile + 1) * 512], in0=ps[:],
                                                                                in1=bada[:, ctile * 512:(ctile + 1) * 512], op=ALU.add),
                             R=[psB, badaB], W=[modB])
                    S.dma(SP, MOD[l], modt[:], R=[modB])
                S.barrier()

        def phase1(l):
            Xin = xw if l == 0 else XA
            with contextlib.ExitStack() as es:
                wq = sbuf(es, [128, 8, 3 * AW], BF16, "wq")
                with contextlib.ExitStack() as es2:
                    load_weight(es2, wq, w_in[l][:, 0:3 * AW], 8, 3 * AW, Ring([DVE, POOL, ACT]))
                    S.barrier()
                cB = Buf()
                Gm = sbuf(es, [128, D], F32, "Gm")
                SHb = sbuf(es, [128, D], F32, "SHb")
                g1b = sbuf(es, [128, D], F32, "g1b")
                gq = sbuf(es, [128, DH], F32, "gq")
                gk = sbuf(es, [128, DH], F32, "gk")
                cosT = sbuf(es, [128, NB, 16], F32, "cosT")
                sinT = sbuf(es, [128, NB, 16], F32, "sinT")
                load_bcast(es, g1b, g_norm1[l], cB)
                load_bcast(es, gq, g_q[l], cB)
                load_bcast(es, gk, g_k[l], cB)
                S.dma(SP, Gm[:], MOD[l][:, D:2 * D], W=[cB])
                S.dma(SP, SHb[:], MOD[l][:, 0:D], W=[cB])
                S.dma(SP, cosT[:].rearrange("p b i -> p (b i)"), ROPE[0], W=[cB])
                S.dma(SP, sinT[:].rearrange("p b i -> p (b i)"), ROPE[1], W=[cB])
                S.op(DVE, lambda e: e.scalar_tensor_tensor(out=Gm[:], in0=Gm[:], scalar=1.0, in1=g1b[:], op0=ALU.add, op1=ALU.mult),
                     R=[cB], W=[cB])
                norm = NormCtx(es, Gm, SHb, cB)
                xts = Ring([(sbuf(es, [128, D], F32, "xt"), Buf()) for _ in range(3)])
                hTs = Ring([(sbuf(es, [128, 8, 128], BF16, "hT"), Buf()) for _ in range(2)])
                pss = Ring([(psum(es, [128, 512], F32, "p1ps"), Buf()) for _ in range(4)])
                sqs = Ring([(sbuf(es, [128, 512], F32, "sq"), Buf()) for _ in range(2)])
                s4s = Ring([(sbuf(es, [128, 12], F32, "s4"), Buf()) for _ in range(2)])
                qns = Ring([(sbuf(es, [128, 512], F32, "qn"), Buf()) for _ in range(2)])
                qos = Ring([(sbuf(es, [128, 512], BF16, "qo"), Buf()) for _ in range(3)])
                rts = Ring([(sbuf(es, [128, 4, 4, 16], F32, "rt"), Buf()) for _ in range(2)])
                vos = []
                for _ in range(2):
                    vo = sbuf(es, [128, 4, 129], BF16, "vo")
                    vB = Buf()
                    S.op(POOL, lambda e, vo=vo: e.memset(vo[:], 1.0), W=[vB])
                    vos.append((vo, vB))
                vos = Ring(vos)
                for blk in range(Kb[l], NB):
                    xt, xtB = xts.next()
                    hT, hTB = hTs.next()
                    S.dma(SP, xt[:], Xin[blk * 128:(blk + 1) * 128, :], W=[xtB])
                    norm.run(xt, xtB, blk, hT[:], hTB, DVE)
                    for t in range(9):
                        ps, psB = pss.next()
                        for kk in range(8):
                            S.op(PE, lambda e, kk=kk, ps=ps, t=t: e.matmul(ps[:], lhsT=hT[:, kk, :], rhs=wq[:, kk, t * 512:(t + 1) * 512],
                                                                           start=(kk == 0), stop=(kk == 7)),
                                 R=[hTB], W=[psB], sig=(kk == 7))
                        if t < 6:
                            gvec = gq if t < 3 else gk
                            dstD = QS if t < 3 else KS
                            tt = t % 3
                            sq, sqB = sqs.next()
                            s4, s4B = s4s.next()
                            qn, qnB = qns.next()
                            qo, qoB = qos.next()
                            rt, rtB = rts.next()
                            S.op(ACT, lambda e, sq=sq, ps=ps: e.activation(out=sq[:], in_=ps[:], func=AF.Square), R=[psB], W=[sqB])
                            S.op(DVE, lambda e, s4=s4, sq=sq: e.tensor_reduce(out=s4[:, 0:4], in_=sq[:].rearrange("p (h d) -> p h d", h=4),
                                                                              axis=AX.X, op=ALU.add), R=[sqB], W=[s4B])
                            rsqrt_small(s4[:, 8:12], s4[:, 0:4], 1.0 / DH, s4B, s4B, s4[:, 4:8], s4B)
                            for j in range(4):
                                S.op(DVE, lambda e, j=j, qn=qn, ps=ps, s4=s4, gvec=gvec: e.scalar_tensor_tensor(
                                    out=qn[:, j * 128:(j + 1) * 128], in0=ps[:, j * 128:(j + 1) * 128], scalar=s4[:, 8 + j:9 + j],
                                    in1=gvec[:], op0=ALU.mult, op1=ALU.mult), R=[psB, s4B, cB], W=[qnB])
                            S.op(POOL, lambda e, qo=qo, qn=qn: e.tensor_copy(out=qo[:], in_=qn[:]), R=[qnB], W=[qoB])
                            qn3 = qn[:].rearrange("p (h d) -> p h d", h=4)
                            qo3 = qo[:].rearrange("p (h d) -> p h d", h=4)
                            cb = cosT[:, blk, :].unsqueeze(1).broadcast_to([128, 4, 16])
                            sb_ = sinT[:, blk, :].unsqueeze(1).broadcast_to([128, 4, 16])
                            t1, t2 = qn3[:, :, 0:16], qn3[:, :, 16:32]
                            S.op(DVE, lambda e, rt=rt, t1=t1, cb=cb: e.tensor_tensor(out=rt[:, 0], in0=t1, in1=cb, op=ALU.mult), R=[qnB, cB], W=[rtB])
                            S.op(DVE, lambda e, rt=rt, t2=t2, sb_=sb_: e.tensor_tensor(out=rt[:, 1], in0=t2, in1=sb_, op=ALU.mult), R=[qnB, cB], W=[rtB])
                            S.op(DVE, lambda e, rt=rt, t2=t2, cb=cb: e.tensor_tensor(out=rt[:, 2], in0=t2, in1=cb, op=ALU.mult), R=[qnB, cB], W=[rtB])
                            S.op(DVE, lambda e, rt=rt, t1=t1, sb_=sb_: e.tensor_tensor(out=rt[:, 3], in0=t1, in1=sb_, op=ALU.mult), R=[qnB, cB], W=[rtB])
                            S.op(DVE, lambda e, rt=rt, qo3=qo3: e.tensor_tensor(out=qo3[:, :, 0:16], in0=rt[:, 0], in1=rt[:, 1], op=ALU.subtract),
                                 R=[rtB], W=[qoB])
                            S.op(DVE, lambda e, rt=rt, qo3=qo3: e.tensor_tensor(out=qo3[:, :, 16:32], in0=rt[:, 2], in1=rt[:, 3], op=ALU.add),
                                 R=[rtB], W=[qoB])
                            S.dma(POOL, dstD[blk * 128:(blk + 1) * 128, tt * 512:(tt + 1) * 512], qo[:], R=[qoB])
                        else:
                            tt = t - 6
                            vo, vB = vos.next()
                            S.op(ACT, lambda e, vo=vo, ps=ps: e.copy(out=vo[:, :, 0:128], in_=ps[:].rearrange("p (h d) -> p h d", h=4)),
                                 R=[psB], W=[vB])
                            S.dma(POOL, VS[blk * 128:(blk + 1) * 128, tt * 516:(tt + 1) * 516], vo[:].rearrange("p h d -> p (h d)"), R=[vB])
                S.barrier()

        def phase2(l):
            with contextlib.ExitStack() as es:
                def ring(n, shape, dt, p, ps=False):
                    return Ring([((psum if ps else sbuf)(es, shape, dt, p), Buf()) for _ in range(n)])
                cB = Buf()
                gqp = sbuf(es, [128, 1], F32, "gqp")
                gkp = sbuf(es, [128, 1], F32, "gkp")
                S.dma(SP, gqp[:], g_q[l].rearrange("(p o) -> p o", o=1), W=[cB])
                S.dma(SP, gkp[:], g_k[l].rearrange("(p o) -> p o", o=1), W=[cB])
                S.op(DVE, lambda e: e.memset(gqp[0:32, :], 1.0), R=[cB], W=[cB])
                S.op(DVE, lambda e: e.memset(gkp[0:32, :], 1.0), R=[cB], W=[cB])
                Qts = ring(3, [128, 512], BF16, "Qt")
                Kcs = ring(3, [128, 512], BF16, "Kc")
                Kps = ring(3, [128, 512], BF16, "Kp")
                Vcs = ring(3, [128, 4, 129], BF16, "Vc")
                Vps = ring(3, [128, 4, 129], BF16, "Vp")
                Tps = ring(2, [128, 6, 128], BF16, "Tps", ps=True)
                qkTs = ring(3, [128, 6, 128], BF16, "qkT")
                Sps = ring(3, [128, 2, 2, 128], F32, "Sps", ps=True)
                PTs = ring(3, [128, 2, 2, 128], BF16, "PT")
                Ops = ring(2, [128, 2, 129], F32, "Ops", ps=True)
                Ots = ring(3, [128, 4, 129], F32, "Ot")
                for rg in (Qts, Kcs, Vcs, qkTs, PTs):
                    for (t_, b_) in rg.items:
                        S.op(POOL, lambda e, t_=t_: e.memset(t_[:], 0.0), W=[b_])
                ul = units[l]
                loaded = {}

                def loads(ui):
                    (g, d, base, nq) = ul[ui]
                    m0, r = divmod(base, d)
                    qv = QS.rearrange("(m d) c -> m d c", d=d)
                    kv = KS.rearrange("(m d) c -> m d c", d=d)
                    vv = VS.rearrange("(m d) c -> m d c", d=d)
                    Qt, QtB = Qts.next()
                    Kc, KcB = Kcs.next()
                    Kp, KpB = Kps.next()
                    Vc, VcB = Vcs.next()
                    Vp, VpB = Vps.next()
                    cs_ = slice(512 * g, 512 * g + 512)
                    vs_ = slice(516 * g, 516 * g + 516)
                    S.dma(SP, Qt[:nq, :], qv[m0:m0 + nq, r, cs_], W=[QtB])
                    S.dma(SP, Kp[:, :], kv[m0 - 128:m0, r, cs_], W=[KpB])
                    S.dma(SP, Kc[:nq, :], kv[m0:m0 + nq, r, cs_], W=[KcB])
                    S.dma(SP, Vp[:].rearrange("p h d -> p (h d)"), vv[m0 - 128:m0, r, vs_], W=[VpB])
                    S.dma(SP, Vc[:nq].rearrange("p h d -> p (h d)"), vv[m0:m0 + nq, r, vs_], W=[VcB])
                    loaded[ui] = (Qt, QtB, Kc, KcB, Kp, KpB, Vc, VcB, Vp, VpB, Ots.next())

                def stageA(ui, hp):
                    (g, d, base, nq) = ul[ui]
                    (Qt, QtB, Kc, KcB, Kp, KpB, Vc, VcB, Vp, VpB, (Ot, OtB)) = loaded[ui]
                    colp = kcols[(d, base - 128 * d)]
                    colc = kcols[(d, base)]
                    T, TB = Tps.next()
                    qkT, qkTB = qkTs.next()
                    Sp, SpB = Sps.next()
                    PT, PTB = PTs.next()
                    for jj in range(2):
                        j = 2 * hp + jj
                        S.op(PE, lambda e, jj=jj, j=j: e.transpose(out=T[:, jj, :nq], in_=Qt[:nq, j * 128:(j + 1) * 128], identity=ident[:nq, :nq]),
                             R=[QtB], W=[TB], sig=False)
                        S.op(PE, lambda e, jj=jj, j=j: e.transpose(out=T[:, 2 + jj, :], in_=Kp[:, j * 128:(j + 1) * 128], identity=ident[:]),
                             R=[KpB], W=[TB], sig=False)
                        S.op(PE, lambda e, jj=jj, j=j: e.transpose(out=T[:, 4 + jj, :nq], in_=Kc[:nq, j * 128:(j + 1) * 128], identity=ident[:nq, :nq]),
                             R=[KcB], W=[TB], sig=(jj == 1))
                    S.op(ACT, lambda e: e.copy(out=qkT[:, 0:2, :nq], in_=T[:, 0:2, :nq]), R=[TB], W=[qkTB])
                    if nq == 128:
                        S.op(DVE, lambda e: e.tensor_copy(out=qkT[:, 2:6, :], in_=T[:, 2:6, :]), R=[TB], W=[qkTB])
                    else:
                        S.op(DVE, lambda e: e.tensor_copy(out=qkT[:, 2:4, :], in_=T[:, 2:4, :]), R=[TB], W=[qkTB])
                        S.op(DVE, lambda e: e.tensor_copy(out=qkT[:, 4:6, :nq], in_=T[:, 4:6, :nq]), R=[TB], W=[qkTB])
                    for jj in range(2):
                        S.op(PE, lambda e, jj=jj: e.matmul(Sp[:, 0, jj, :nq], lhsT=ident[:], rhs=maskP[:, :nq], start=True, stop=False),
                             R=[], W=[SpB], sig=False)
                        S.op(PE, lambda e, jj=jj: e.matmul(Sp[:, 0, jj, :nq], lhsT=qkT[:, 2 + jj, :], rhs=qkT[:, jj, :nq], start=False, stop=True),
                             R=[qkTB], W=[SpB], sig=False)
                        S.op(PE, lambda e, jj=jj: e.matmul(Sp[:nq, 1, jj, :nq], lhsT=ident[:nq, :nq], rhs=maskC[:nq, :nq], start=True, stop=False),
                             R=[], W=[SpB], sig=False)
                        S.op(PE, lambda e, jj=jj: e.matmul(Sp[:nq, 1, jj, :nq], lhsT=qkT[:, 4 + jj, :nq], rhs=qkT[:, jj, :nq], start=False, stop=True),
                             R=[qkTB], W=[SpB], sig=(jj == 1))
                    S.op(ACT, lambda e: e.activation(out=PT[:, 0, :, :nq], in_=Sp[:, 0, :, :nq], func=AF.Exp, scale=SCALE,
                                                     bias=kb[:, colp:colp + 1]), R=[SpB], W=[PTB])
                    S.op(ACT, lambda e: e.activation(out=PT[:nq, 1, :, :nq], in_=Sp[:nq, 1, :, :nq], func=AF.Exp, scale=SCALE,
                                                     bias=kb[:nq, colc:colc + 1]), R=[SpB], W=[PTB])
                    return (PT, PTB)

                def stageB(ui, hp, PT, PTB):
                    (g, d, base, nq) = ul[ui]
                    (Qt, QtB, Kc, KcB, Kp, KpB, Vc, VcB, Vp, VpB, (Ot, OtB)) = loaded[ui]
                    m0, r = divmod(base, d)
                    Op, OpB = Ops.next()
                    for jj in range(2):
                        j = 2 * hp + jj
                        S.op(PE, lambda e, jj=jj, j=j: e.matmul(Op[:nq, jj, :], lhsT=PT[:, 0, jj, :nq], rhs=Vp[:, j, :], start=True, stop=False),
                             R=[PTB, VpB], W=[OpB], sig=False)
                        S.op(PE, lambda e, jj=jj, j=j: e.matmul(Op[:nq, jj, :], lhsT=PT[:nq, 1, jj, :nq], rhs=Vc[:nq, j, :], start=False, stop=True),
                             R=[PTB, VcB], W=[OpB], sig=(jj == 1))
                    S.op(DVE, lambda e: e.tensor_copy(out=Ot[:nq, 2 * hp:2 * hp + 2, :], in_=Op[:nq, :, :]), R=[OpB], W=[OtB])
                    if hp == 1:
                        av = ATT[g].rearrange("(m d) c -> m d c", d=d)
                        S.dma(POOL, av[m0:m0 + nq, r, :], Ot[:nq].rearrange("p h d -> p (h d)"), R=[OtB])
                        del loaded[ui]

                nu = len(ul)
                loads(0)
                if nu > 1:
                    loads(1)
                prev = None
                for ui in range(nu):
                    for hp in range(2):
                        cur = (ui, hp) + stageA(ui, hp)
                        if prev is not None:
                            stageB(*prev)
                        if hp == 0 and ui + 2 < nu:
                            loads(ui + 2)
                        prev = cur
                stageB(*prev)
                S.barrier()

        def phase3(l):
            Xin = xw if l == 0 else XA
            WT = 256
            with contextlib.ExitStack() as es:
                wcg = sbuf(es, [128, 8, 4096], BF16, "wcg")
                wco = sbuf(es, [128, 8, D], BF16, "wco")
                wap = sbuf(es, [128, 4, D], BF16, "wap")
                wo = sbuf(es, [128, 8, D], BF16, "wo")
                with contextlib.ExitStack() as es2:
                    rot = Ring([DVE, POOL, ACT])
                    load_weight(es2, wcg, w_in[l][:, 3 * AW:INW], 8, 4096, rot)
                    load_weight(es2, wco, w_conv_out[l], 8, D, rot)
                    load_weight(es2, wap, w_attn_proj[l], 4, D, rot)
                    load_weight(es2, wo, w_o[l], 8, D, rot)
                    S.barrier()
                cB = Buf()
                Gm = sbuf(es, [128, D], F32, "Gm")
                SHb = sbuf(es, [128, D], F32, "SHb")
                gtb = sbuf(es, [128, D], F32, "gtb")
                wdw = sbuf(es, [128, 8, CONVK], F32, "wdw")
                bdw = sbuf(es, [128, 8], F32, "bdw")
                gln = sbuf(es, [128, 8], F32, "gln")
                bln = sbuf(es, [128, 8], F32, "bln")
                xm = sbuf(es, [128, D], F32, "xm")
                load_bcast(es, xm, g_norm1[l], cB)
                S.dma(SP, Gm[:], MOD[l][:, D:2 * D], W=[cB])
                S.dma(SP, SHb[:], MOD[l][:, 0:D], W=[cB])
                S.dma(SP, gtb[:], MOD[l][:, 2 * D:3 * D], W=[cB])
                S.dma(SP, wdw[:], wdw_t[l], W=[cB])
                S.dma(SP, bdw[:], bdw_t[l], W=[cB])
                S.dma(SP, gln[:], gln_t[l], W=[cB])
                S.dma(SP, bln[:], bln_t[l], W=[cB])
                S.op(DVE, lambda e: e.scalar_tensor_tensor(out=Gm[:], in0=Gm[:], scalar=1.0, in1=xm[:], op0=ALU.add, op1=ALU.mult),
                     R=[cB], W=[cB])
                xmB = Buf()
                xmB.r.append(cB.w)
                norm = NormCtx(es, Gm, SHb, cB)
                xts = Ring([(sbuf(es, [128, D], F32, "xt"), Buf()) for _ in range(2)])
                hTs = Ring([(sbuf(es, [128, 8, WT], BF16, "hT"), [Buf(), Buf()]) for _ in range(2)])
                As = [(sbuf(es, [128, 516], F32, "A"), Buf()) for _ in range(3)]
                abf = sbuf(es, [128, 512], BF16, "abf"); abfB = Buf()
                ast = sbuf(es, [128, 8], F32, "ast"); astB = Buf()
                attnTs = Ring([(sbuf(es, [128, 4, WT], BF16, "attnT"), [Buf(), Buf()]) for _ in range(2)])
                tpa = norm.tp.items[0]
                uT = sbuf(es, [128, 8, 30 + WT], BF16, "uT")
                uTB = [Buf() for _ in range(8)]
                S.op(POOL, lambda e: e.memset(uT[:], 0.0), W=uTB)
                sgt = Ring([(sbuf(es, [128, WT], F32, "sgt"), Buf()) for _ in range(2)])
                yv = sbuf(es, [128, 8, WT], F32, "yv"); yB = [Buf() for _ in range(8)]
                ybf = sbuf(es, [128, 8, WT], BF16, "ybf"); ybfB = Buf()
                ysq = sbuf(es, [128, 8, WT], BF16, "ysq"); ysqB = Buf()
                stt_ = sbuf(es, [128, 5, WT], F32, "lnst"); stB = Buf()
                actT = sbuf(es, [128, 8, WT], BF16, "actT"); actB = [Buf() for _ in range(8)]
                sg2 = Ring([(sbuf(es, [128, 2, WT], F32, "sg2"), Buf()) for _ in range(2)])
                m1 = Ring([(sbuf(es, [128, 2, WT], F32, "m1"), Buf()) for _ in range(2)])
                mT = sbuf(es, [128, 8, WT], BF16, "mT"); mTB = [Buf() for _ in range(8)]
                psv = Ring([(psum(es, [128, 2, WT], F32, "psv"), Buf()) for _ in range(2)])
                psS = (psum(es, [128, 2, WT], F32, "psS"), Buf())
                psA = (psum(es, [128, 2, WT], F32, "psA"), Buf())
                psB_ = (psum(es, [128, 2, WT], F32, "psB"), Buf())
                pso = Ring([(psum(es, [128, 512], F32, "pso"), Buf()) for _ in range(2)])

                blk0 = Mb[l]
                while blk0 < NB:
                    nbk = min(2, NB - blk0)
                    W_ = 128 * nbk
                    hT, hTBs = hTs.next()
                    attnT, attnTBs = attnTs.next()
                    xl = []
                    for bi in range(nbk):
                        blk = blk0 + bi
                        xt, xtB = xts.next()
                        xl.append((xt, xtB))
                        S.dma(SP, xt[:], Xin[blk * 128:(blk + 1) * 128, :], W=[xtB])
                        norm.run(xt, xtB, blk, hT[:, :, bi * 128:(bi + 1) * 128], hTBs[bi], ACT)
                        for g in range(3):
                            S.dma(SP, As[g][0][:], ATT[g][blk * 128:(blk + 1) * 128, :], W=[As[g][1]])
                        A0, A1, A2 = As[0][0], As[1][0], As[2][0]
                        S.op(POOL, lambda e: e.tensor_tensor(out=A0[:], in0=A0[:], in1=A1[:], op=ALU.add), R=[As[1][1]], W=[As[0][1]])
                        S.op(POOL, lambda e: e.tensor_tensor(out=A0[:], in0=A0[:], in1=A2[:], op=ALU.add), R=[As[2][1]], W=[As[0][1]])
                        A3 = A0[:].rearrange("p (h d) -> p h d", h=4)
                        S.op(DVE, lambda e: e.tensor_scalar(out=ast[:, 0:4], in0=A3[:, :, 128], scalar1=1e-30, scalar2=None, op0=ALU.max),
                             R=[As[0][1]], W=[astB])
                        S.op(DVE, lambda e: e.reciprocal(out=ast[:, 4:8], in_=ast[:, 0:4]), R=[astB], W=[astB])
                        for j in range(4):
                            S.op(DVE, lambda e, j=j: e.tensor_scalar(out=abf[:, j * 128:(j + 1) * 128], in0=A3[:, j, 0:128],
                                                                     scalar1=ast[:, 4 + j:5 + j], scalar2=None, op0=ALU.mult),
                                 R=[As[0][1], astB], W=[abfB])
                        tp_, tpB_ = tpa
                        for j in range(4):
                            S.op(PE, lambda e, j=j: e.transpose(out=tp_[:, j, :], in_=abf[:, j * 128:(j + 1) * 128], identity=ident[:]),
                                 R=[abfB], W=[tpB_], sig=(j == 3))
                        S.op(DVE, lambda e, bi=bi: e.tensor_copy(out=attnT[:, :, bi * 128:(bi + 1) * 128], in_=tp_[:, 0:4, :]),
                             R=[tpB_], W=[attnTBs[bi]])
                    hR = hTBs[:nbk]
                    aR = attnTBs[:nbk]
                    for cp in range(4):
                        pair = []
                        for ci in range(2):
                            c = 2 * cp + ci
                            pv, pvB = psv.next()
                            sg, sgB = sgt.next()
                            for half in range(2):
                                col0 = half * D + c * 128
                                for kk in range(8):
                                    S.op(PE, lambda e, kk=kk, pv=pv, half=half, col0=col0: e.matmul(
                                        pv[:, half, :W_], lhsT=wcg[:, kk, col0:col0 + 128], rhs=hT[:, kk, :W_],
                                        start=(kk == 0), stop=(kk == 7)), R=hR, W=[pvB], sig=(kk == 7 and half == 1))
                            S.op(ACT, lambda e, pv=pv, sg=sg: e.activation(out=sg[:, :W_], in_=pv[:, 1, :W_], func=AF.Sigmoid), R=[pvB], W=[sgB])
                            S.op(DVE, lambda e, pv=pv, sg=sg, c=c: e.tensor_tensor(out=uT[:, c, 30:30 + W_], in0=pv[:, 0, :W_], in1=sg[:, :W_], op=ALU.mult),
                                 R=[pvB, sgB], W=[uTB[c]])
                            pair.append(c)
                        for c in pair:
                            S.op(DVE, lambda e, c=c: e.tensor_scalar(out=yv[:, c, :W_], in0=uT[:, c, 0:W_], scalar1=wdw[:, c, 0:1],
                                                                     scalar2=bdw[:, c:c + 1], op0=ALU.mult, op1=ALU.add),
                                 R=[uTB[c], cB], W=[yB[c]])
                        for tap in range(1, CONVK):
                            for c in pair:
                                S.op(DVE, lambda e, c=c, tap=tap: e.scalar_tensor_tensor(
                                    out=yv[:, c, :W_], in0=uT[:, c, tap:tap + W_], scalar=wdw[:, c, tap:tap + 1], in1=yv[:, c, :W_],
                                    op0=ALU.mult, op1=ALU.add), R=[uTB[c], yB[c]], W=[yB[c]])
                        for c in pair:
                            S.op(POOL, lambda e, c=c: e.tensor_copy(out=uT[:, c, 0:30], in_=uT[:, c, W_:W_ + 30]), R=[uTB[c]], W=[uTB[c]])
                    S.op(POOL, lambda e: e.tensor_copy(out=ybf[:, :, :W_], in_=yv[:, :, :W_]), R=yB, W=[ybfB])
                    S.op(ACT, lambda e: e.activation(out=ysq[:, :, :W_], in_=yv[:, :, :W_], func=AF.Square), R=yB, W=[ysqB])
                    pS, pSB = psS
                    for c in range(8):
                        S.op(PE, lambda e, c=c: e.matmul(pS[:, 0, :W_], lhsT=ones_bf[:], rhs=ybf[:, c, :W_], start=(c == 0), stop=(c == 7)),
                             R=[ybfB], W=[pSB], sig=False)
                    for c in range(8):
                        S.op(PE, lambda e, c=c: e.matmul(pS[:, 1, :W_], lhsT=ones_bf[:], rhs=ysq[:, c, :W_], start=(c == 0), stop=(c == 7)),
                             R=[ysqB], W=[pSB], sig=(c == 7))
                    mean_, msq_, var_, lnv_, rstd_ = (stt_[:, i, :W_] for i in range(5))
                    S.op(ACT, lambda e: e.activation(out=mean_, in_=pS[:, 0, :W_], func=AF.Copy, scale=1.0 / D), R=[pSB], W=[stB])
                    S.op(ACT, lambda e: e.activation(out=msq_, in_=mean_, func=AF.Square), R=[stB], W=[stB])
                    S.op(DVE, lambda e: e.scalar_tensor_tensor(out=var_, in0=pS[:, 1, :W_], scalar=1.0 / D, in1=msq_, op0=ALU.mult, op1=ALU.subtract),
                         R=[pSB, stB], W=[stB])
                    S.op(DVE, lambda e: e.tensor_scalar(out=var_, in0=var_, scalar1=0.0, scalar2=None, op0=ALU.max), R=[stB], W=[stB])
                    S.op(ACT, lambda e: e.activation(out=lnv_, in_=var_, func=AF.Ln, bias=EPS), R=[stB], W=[stB])
                    S.op(ACT, lambda e: e.activation(out=rstd_, in_=lnv_, func=AF.Exp, scale=-0.5), R=[stB], W=[stB])
                    S.op(DVE, lambda e: e.tensor_tensor(out=yv[:, :, :W_], in0=yv[:, :, :W_],
                                                        in1=mean_.unsqueeze(1).broadcast_to([128, 8, W_]), op=ALU.subtract), R=yB + [stB], W=yB)
                    S.op(DVE, lambda e: e.tensor_tensor(out=yv[:, :, :W_], in0=yv[:, :, :W_],
                                                        in1=rstd_.unsqueeze(1).broadcast_to([128, 8, W_]), op=ALU.mult), R=yB + [stB], W=yB)
                    for c in range(8):
                        S.op(ACT, lambda e, c=c: e.activation(out=actT[:, c, :W_], in_=yv[:, c, :W_], func=AF.Silu,
                                                              scale=gln[:, c:c + 1], bias=bln[:, c:c + 1]), R=[yB[c], cB], W=[actB[c]])
                    for oc in range(8):
                        pA, pAB = psA
                        pB, pBB = psB_
                        s2, s2B = sg2.next()
                        mm1, m1B = m1.next()
                        for gi in range(2):
                            col0 = 2 * D + gi * D + oc * 128
                            for kk in range(8):
                                S.op(PE, lambda e, kk=kk, gi=gi, col0=col0: e.matmul(pA[:, gi, :W_], lhsT=wcg[:, kk, col0:col0 + 128], rhs=hT[:, kk, :W_],
                                                                                     start=(kk == 0), stop=(kk == 7)), R=hR, W=[pAB],
                                     sig=(kk == 7 and gi == 1))
                        for j in range(4):
                            S.op(PE, lambda e, j=j: e.matmul(pB[:, 0, :W_], lhsT=wap[:, j, oc * 128:(oc + 1) * 128], rhs=attnT[:, j, :W_],
                                                             start=(j == 0), stop=(j == 3)), R=aR, W=[pBB], sig=False)
                        for kk in range(8):
                            S.op(PE, lambda e, kk=kk: e.matmul(pB[:, 1, :W_], lhsT=wco[:, kk, oc * 128:(oc + 1) * 128], rhs=actT[:, kk, :W_],
                                                               start=(kk == 0), stop=(kk == 7)), R=actB, W=[pBB], sig=(kk == 7))
                        S.op(ACT, lambda e, s2=s2: e.activation(out=s2[:, :, :W_], in_=pA[:, :, :W_], func=AF.Sigmoid), R=[pAB], W=[s2B])
                        S.op(DVE, lambda e, s2=s2, mm1=mm1: e.tensor_tensor(out=mm1[:, :, :W_], in0=pB[:, :, :W_], in1=s2[:, :, :W_], op=ALU.mult),
                             R=[pBB, s2B], W=[m1B])
                        S.op(POOL, lambda e, mm1=mm1, oc=oc: e.tensor_tensor(out=mT[:, oc, :W_], in0=mm1[:, 0, :W_], in1=mm1[:, 1, :W_], op=ALU.add),
                             R=[m1B], W=[mTB[oc]])
                    for bi in range(nbk):
                        blk = blk0 + bi
                        xt, xtB = xl[bi]
                        for hf in range(2):
                            po, poB = pso.next()
                            for kk in range(8):
                                S.op(PE, lambda e, kk=kk, po=po, bi=bi, hf=hf: e.matmul(po[:], lhsT=mT[:, kk, bi * 128:(bi + 1) * 128],
                                                                                        rhs=wo[:, kk, hf * 512:(hf + 1) * 512],
                                                                                        start=(kk == 0), stop=(kk == 7)), R=mTB, W=[poB], sig=(kk == 7))
                            S.op(DVE, lambda e, po=po, hf=hf: e.tensor_tensor(out=xm[:, hf * 512:(hf + 1) * 512], in0=po[:],
                                                                              in1=gtb[:, hf * 512:(hf + 1) * 512], op=ALU.mult), R=[poB, cB], W=[xmB])
                        S.op(POOL, lambda e, xt=xt: e.tensor_tensor(out=xm[:], in0=xm[:], in1=xt[:], op=ALU.add), R=[xtB, xmB], W=[xmB])
                        S.dma(POOL, XM[blk * 128:(blk + 1) * 128, :], xm[:], R=[xmB])
                    blk0 += nbk
                S.barrier()

        def phase4(l):
            WT = 256
            last = (l == L - 1)
            with contextlib.ExitStack() as es:
                wfi = sbuf(es, [128, 8, 2 * DFF], BF16, "wfi")
                wfd = sbuf(es, [128, 22, D], BF16, "wfd")
                with contextlib.ExitStack() as es2:
                    rot = Ring([DVE, POOL, ACT])
                    load_weight(es2, wfi, w_ffn_in[l], 8, 2 * DFF, rot)
                    load_weight(es2, wfd, w_ffn_down[l], 22, D, rot)
                    S.barrier()
                cB = Buf()
                Gm = sbuf(es, [128, D], F32, "Gm")
                SHb = sbuf(es, [128, D], F32, "SHb")
                gtb = sbuf(es, [128, D], F32, "gtb")
                wf3 = sbuf(es, [128, 22, 3], F32, "wf3")
                bfv = sbuf(es, [128, 22], F32, "bfv")
                xo = sbuf(es, [128, D], F32, "xo")
                load_bcast(es, xo, g_norm2[l], cB)
                S.dma(SP, Gm[:], MOD[l][:, 4 * D:5 * D], W=[cB])
                S.dma(SP, SHb[:], MOD[l][:, 3 * D:4 * D], W=[cB])
                S.dma(SP, gtb[:], MOD[l][:, 5 * D:6 * D], W=[cB])
                S.dma(SP, wf3[:], wf3_t[l], W=[cB])
                S.dma(SP, bfv[:], bf_t[l], W=[cB])
                S.op(DVE, lambda e: e.scalar_tensor_tensor(out=Gm[:], in0=Gm[:], scalar=1.0, in1=xo[:], op0=ALU.add, op1=ALU.mult),
                     R=[cB], W=[cB])
                xoB = Buf()
                xoB.r.append(cB.w)
                norm = NormCtx(es, Gm, SHb, cB)
                xts = Ring([(sbuf(es, [128, D], F32, "xt"), Buf()) for _ in range(2)])
                hTs = Ring([(sbuf(es, [128, 8, WT], BF16, "hT"), [Buf(), Buf()]) for _ in range(2)])
                carry = sbuf(es, [128, 22, 2], F32, "carry"); carB = [Buf() for _ in range(22)]
                S.op(POOL, lambda e: e.memset(carry[:], 0.0), W=carB)
                gbs = Ring([(sbuf(es, [128, 2 + WT], F32, "gb"), Buf()) for _ in range(4)])
                accs = Ring([(sbuf(es, [128, WT], F32, "acc"), Buf()) for _ in range(4)])
                sils = Ring([(sbuf(es, [128, WT], F32, "sil"), Buf()) for _ in range(2)])
                actT = sbuf(es, [128, 22, WT], BF16, "actT"); actB = [Buf() for _ in range(22)]
                psg = Ring([(psum(es, [128, 2, WT], F32, "psg"), Buf()) for _ in range(4)])
                pso = Ring([(psum(es, [128, 512], F32, "pso"), Buf()) for _ in range(2)])
                blk0 = Mb[l]
                while blk0 < NB:
                    nbk = min(2, NB - blk0)
                    W_ = 128 * nbk
                    hT, hTBs = hTs.next()
                    xl = []
                    for bi in range(nbk):
                        blk = blk0 + bi
                        xt, xtB = xts.next()
                        xl.append((xt, xtB))
                        S.dma(SP, xt[:], XM[blk * 128:(blk + 1) * 128, :], W=[xtB])
                        norm.run(xt, xtB, blk, hT[:, :, bi * 128:(bi + 1) * 128], hTBs[bi], ACT)
                    hR = hTBs[:nbk]
                    for fp in range(11):
                        items = []
                        for fi in range(2):
                            f = 2 * fp + fi
                            pg, pgB = psg.next()
                            gb, gbB = gbs.next()
                            acc, accB = accs.next()
                            for half in range(2):
                                col0 = half * DFF + f * 128
                                for kk in range(8):
                                    S.op(PE, lambda e, kk=kk, pg=pg, half=half, col0=col0: e.matmul(
                                        pg[:, half, :W_], lhsT=wfi[:, kk, col0:col0 + 128], rhs=hT[:, kk, :W_],
                                        start=(kk == 0), stop=(kk == 7)), R=hR, W=[pgB], sig=(kk == 7 and half == 1))
                            S.op(POOL, lambda e, gb=gb, f=f: e.tensor_copy(out=gb[:, 0:2], in_=carry[:, f, :]), R=[carB[f]], W=[gbB])
                            S.op(ACT, lambda e, gb=gb, pg=pg: e.copy(out=gb[:, 2:2 + W_], in_=pg[:, 0, :W_]), R=[pgB], W=[gbB])
                            S.op(POOL, lambda e, gb=gb, f=f: e.tensor_copy(out=carry[:, f, :], in_=gb[:, W_:W_ + 2]), R=[gbB], W=[carB[f]])
                            items.append((f, pg, pgB, gb, gbB, acc, accB))
                        for (f, pg, pgB, gb, gbB, acc, accB) in items:
                            S.op(DVE, lambda e, f=f, gb=gb, acc=acc: e.tensor_scalar(out=acc[:, :W_], in0=gb[:, 0:W_], scalar1=wf3[:, f, 0:1],
                                                                                     scalar2=bfv[:, f:f + 1], op0=ALU.mult, op1=ALU.add),
                                 R=[gbB, cB], W=[accB])
                        for tap in (1, 2):
                            for (f, pg, pgB, gb, gbB, acc, accB) in items:
                                S.op(DVE, lambda e, f=f, gb=gb, acc=acc, tap=tap: e.scalar_tensor_tensor(
                                    out=acc[:, :W_], in0=gb[:, tap:tap + W_], scalar=wf3[:, f, tap:tap + 1], in1=acc[:, :W_],
                                    op0=ALU.mult, op1=ALU.add), R=[gbB, accB], W=[accB])
                        for (f, pg, pgB, gb, gbB, acc, accB) in items:
                            sl, slB = sils.next()
                            S.op(ACT, lambda e, sl=sl, acc=acc: e.activation(out=sl[:, :W_], in_=acc[:, :W_], func=AF.Silu), R=[accB], W=[slB])
                            S.op(DVE, lambda e, sl=sl, pg=pg, f=f: e.tensor_tensor(out=actT[:, f, :W_], in0=pg[:, 1, :W_], in1=sl[:, :W_], op=ALU.mult),
                                 R=[pgB, slB], W=[actB[f]])
                    for bi in range(nbk):
                        blk = blk0 + bi
                        xt, xtB = xl[bi]
                        for hf in range(2):
                            po, poB = pso.next()
                            for f in range(22):
                                S.op(PE, lambda e, f=f, po=po, bi=bi, hf=hf: e.matmul(po[:], lhsT=actT[:, f, bi * 128:(bi + 1) * 128],
                                                                                      rhs=wfd[:, f, hf * 512:(hf + 1) * 512],
                                                                                      start=(f == 0), stop=(f == 21)), R=actB, W=[poB], sig=(f == 21))
                            S.op(DVE, lambda e, po=po, hf=hf: e.tensor_tensor(out=xo[:, hf * 512:(hf + 1) * 512], in0=po[:],
                                                                              in1=gtb[:, hf * 512:(hf + 1) * 512], op=ALU.mult), R=[poB, cB], W=[xoB])
                        S.op(POOL, lambda e, xt=xt: e.tensor_tensor(out=xo[:], in0=xo[:], in1=xt[:], op=ALU.add), R=[xtB, xoB], W=[xoB])
                        if last:
                            if blk >= OWN0:
                                S.dma(POOL, y_out[(blk - OWN0) * 128:(blk - OWN0 + 1) * 128, :], xo[:], R=[xoB])
                        else:
                            S.dma(POOL, XA[blk * 128:(blk + 1) * 128, :], xo[:], R=[xoB])
                    blk0 += nbk
                S.barrier()

        S.barrier()
        prologue()
        for l in range(L):
            for pi, ph in enumerate((phase1, phase2, phase3, phase4)):
                if pi < stop_after:
                    ph(l)
        S.barrier()
    return nc, S.n_ins


def make_in_maps(inputs, L=4, cores=range(8)):
    Kb, Mb, OWN0, NB = geometry(L)
    NTOK = NB * 128
    kcols = keyset_columns(L)
    x = np.asarray(inputs["x"], dtype=np.float32)
    c = np.asarray(inputs["c"], dtype=np.float32)
    positions = np.asarray(inputs["positions"], dtype=np.int32)
    f32 = lambda k: np.ascontiguousarray(np.asarray(inputs[k], dtype=np.float32)[:L])
    shared = {k: f32(k) for k in ("w_ada", "b_ada", "g_norm1", "w_in", "g_q", "g_k", "w_attn_proj", "w_conv_out",
                                  "w_o", "g_norm2", "w_ffn_in", "w_ffn_down")}
    wdw = f32("w_conv_dw")
    shared["wdw_t"] = np.ascontiguousarray(wdw.reshape(L, CONVK, 8, 128).transpose(0, 3, 2, 1))
    per_ch = lambda a, n: np.ascontiguousarray(a.reshape(L, n, 128).transpose(0, 2, 1))
    shared["bdw_t"] = per_ch(f32("b_conv_dw"), 8)
    shared["gln_t"] = per_ch(f32("g_conv_ln"), 8)
    shared["bln_t"] = per_ch(f32("b_conv_ln"), 8)
    wf = f32("w_ffn_dw")
    shared["wf3_t"] = np.ascontiguousarray(wf.reshape(L, 3, 22, 128).transpose(0, 3, 2, 1))
    shared["bf_t"] = per_ch(f32("b_ffn_dw"), 22)
    inv = (np.float32(500000.0) ** (-np.arange(0, 32, 2, dtype=np.float32) / np.float32(32))).astype(np.float32)
    shared["invf"] = np.ascontiguousarray(np.broadcast_to(inv[None, :], (128, 16))).astype(np.float32)
    shared["ident"] = np.eye(128, dtype=np.float32)
    kk, qq = np.meshgrid(np.arange(128), np.arange(128), indexing="ij")
    shared["maskp"] = np.where(kk >= qq, 0.0, NEG).astype(np.float32)
    shared["maskc"] = np.where(kk <= qq, 0.0, NEG).astype(np.float32)
    maps = []
    for core in cores:
        b, j = core // 4, core % 4
        off = 4096 * j - OWN0 * 128
        gpos = off + np.arange(NTOK)
        valid = gpos >= 0
        gsafe = np.clip(gpos, 0, SEQ - 1)
        xw = np.where(valid[:, None], x[b, gsafe, :], np.float32(0.0)).astype(np.float32)
        pw = np.where(valid, positions[b, gsafe], 0).astype(np.int32)
        vb = valid.reshape(NB, 128)[:, 0].astype(np.float32)
        kbt = np.zeros((128, len(kcols)), dtype=np.float32)
        for (d, start), col in kcols.items():
            toks = np.clip(start + d * np.arange(128), 0, NTOK - 1)
            kbt[:, col] = np.where(valid[toks], 0.0, NEG)
        m = dict(shared)
        m["xw"] = np.ascontiguousarray(xw)
        m["pos_t"] = np.ascontiguousarray(pw.reshape(NB, 128).T)
        m["vcol"] = np.ascontiguousarray(np.broadcast_to(vb[None, :], (128, NB))).astype(np.float32)
        m["kbias"] = kbt
        m["c_t"] = np.ascontiguousarray(c[b].reshape(8, 128).T)
        maps.append(m)
    return maps


_CACHE = {}


def kernel(**inputs):
    L = 4
    if L not in _CACHE:
        _CACHE[L] = build_program(L)[0]
    nc = _CACHE[L]
    maps = make_in_maps(inputs, L)
    res = run_bass_kernel_spmd(nc, maps, core_ids=list(range(8)))
    out = np.empty((2, SEQ, D), dtype=np.float32)
    for core in range(8):
        b, j = core // 4, core % 4
        out[b, 4096 * j:4096 * (j + 1), :] = res.results[core]["y"]
    return out
```

```python
import contextlib
import os
import numpy as np
import ml_dtypes

import concourse.bass as bass
import concourse.mybir as mybir
from concourse.bass_utils import run_bass_kernel_spmd

F32 = mybir.dt.float32
BF16 = mybir.dt.bfloat16
I32 = mybir.dt.int32
AF = mybir.ActivationFunctionType
ALU = mybir.AluOpType
AX = mybir.AxisListType

D = 1024
NHEAD = 12
DH = 128
AW = 1536
DFF = 2816
INW = 8704
CONVK = 31
EPS = 1e-6
NEG = -30000.0
SCALE = DH ** -0.5
OWN_BLK = 32
SEQ = 16384
DILS = (1, 4, 16)
TWO_PI = 6.283185307179586
C1 = 6.28125
C2 = TWO_PI - C1


def geometry(L):
    Kb = [17 * i for i in range(L)]
    Mb = [k + 16 for k in Kb]
    own0 = 17 * L
    nb = own0 + OWN_BLK
    return Kb, Mb, own0, nb


def attn_units(L):
    Kb, Mb, own0, nb = geometry(L)
    out = []
    for l in range(L):
        q0, q1 = Mb[l] * 128, nb * 128
        lst = []
        for g, d in enumerate(DILS):
            span = 128 * d
            c0 = q0
            while c0 < q1:
                n = min(span, q1 - c0)
                for r in range(d):
                    lst.append((g, d, c0 + r, n // d))
                c0 += span
        out.append(lst)
    return out


def keyset_columns(L):
    cols = {}
    for lst in attn_units(L):
        for (g, d, base, nq) in lst:
            for ks in ((d, base - 128 * d), (d, base)):
                if ks not in cols:
                    cols[ks] = len(cols)
    return cols


class Tok:
    __slots__ = ("eng", "sem", "val")

    def __init__(self, eng, sem, val):
        self.eng, self.sem, self.val = eng, sem, val


class Buf:
    __slots__ = ("name", "w", "r")

    def __init__(self, name=""):
        self.name, self.w, self.r = name, None, []


class Eng:
    def __init__(self, name, e, sem, self_sync):
        self.name, self.e, self.sem, self.self_sync = name, e, sem, self_sync
        self.count = 0
        self.waited = {}
        self.pending = []
        self.slots = []
        self.slot_i = 0


class Sched:
    def __init__(self, nc, es, ndma=14):
        self.nc = nc

        def sem(n):
            return es.enter_context(nc.semaphore(n))

        self.PE = Eng("pe", nc.tensor, sem("s_pe"), False)
        self.ACT = Eng("act", nc.scalar, sem("s_act"), True)
        self.DVE = Eng("dve", nc.vector, sem("s_dve"), True)
        self.POOL = Eng("pool", nc.gpsimd, sem("s_pool"), True)
        self.SP = Eng("sp", nc.sync, sem("s_sp"), True)
        self.engs = [self.PE, self.ACT, self.DVE, self.POOL, self.SP]
        for q in (self.SP, self.POOL, self.ACT):
            nq_ = int(os.environ.get("POOLDMA", "14")) if q is self.POOL else ndma
            q.slots = [[sem("d_%s%d" % (q.name, i)), 0] for i in range(nq_)]
        self.n_ins = 0

    def _wait(self, E, sem, val):
        key = id(sem)
        if E.waited.get(key, 0) < val:
            E.e.wait_ge(sem, val)
            E.waited[key] = val
            self.n_ins += 1

    def _deps(self, E, R, W):
        toks = []
        for b in R:
            if b.w is not None:
                toks.append(b.w)
        for b in W:
            if b.w is not None:
                toks.append(b.w)
            toks.extend(b.r)
        for t in toks:
            if t.eng is E and not E.self_sync:
                continue
            if t.val is None:
                if t.eng is E:
                    continue
                raise RuntimeError("dependency on unsignalled instruction (%s)" % t.eng.name)
            self._wait(E, t.sem, t.val)

    def op(self, E, fn, R=(), W=(), sig=True):
        self._deps(E, R, W)
        ins = fn(E.e)
        self.n_ins += 1
        tok = Tok(E, E.sem, None)
        E.pending.append(tok)
        if sig:
            E.count += 1
            ins.then_inc(E.sem, 1)
            for t in E.pending:
                t.val = E.count
            E.pending = []
        for b in W:
            b.w = tok
            b.r = []
        for b in R:
            b.r = [t for t in b.r if t.eng is not E or t.eng is None]
            b.r.append(tok)
        return ins

    def dma(self, Q, out, in_, R=(), W=()):
        self._deps(Q, R, W)
        slot = Q.slots[Q.slot_i % len(Q.slots)]
        Q.slot_i += 1
        if slot[1] > 0:
            self._wait(Q, slot[0], slot[1])
        slot[1] += 16
        Q.e.dma_start(out=out, in_=in_).then_inc(slot[0], 16)
        self.n_ins += 1
        tok = Tok(None, slot[0], slot[1])
        for b in W:
            b.w = tok
            b.r = []
        for b in R:
            b.r.append(tok)

    def barrier(self):
        for E in self.engs:
            if E.pending:
                raise RuntimeError("pending unsignalled instructions at barrier on " + E.name)
        for E in self.engs:
            for F in self.engs:
                if F.count > 0 and (F is not E or E.self_sync):
                    self._wait(E, F.sem, F.count)
            for Q in (self.SP, self.POOL, self.ACT):
                for s in Q.slots:
                    if s[1] > 0:
                        self._wait(E, s[0], s[1])


class Ring:
    def __init__(self, items):
        self.items = items
        self.i = 0

    def next(self):
        it = self.items[self.i % len(self.items)]
        self.i += 1
        return it


def build_program(L=4, debug=False, stop_after=99, maxblk=None):
    Kb, Mb, OWN0, NB = geometry(L)
    NTOK = NB * 128
    units = attn_units(L)
    kcols = keyset_columns(L)
    NKS = len(kcols)

    nc = bass.Bass("TRN2", target_bir_lowering=False)

    def din(name, shape, dt=F32):
        return nc.dram_tensor(name, list(shape), dt, kind="ExternalInput").ap()

    dbg_kind = "ExternalOutput" if debug else "Internal"

    def dscr(name, shape, dt=F32):
        return nc.dram_tensor(name, list(shape), dt, kind=dbg_kind).ap()

    xw = din("xw", [NTOK, D])
    pos_t = din("pos_t", [128, NB], I32)
    vcol_d = din("vcol", [128, NB])
    kb_d = din("kbias", [128, NKS])
    c_t = din("c_t", [128, 8])
    invf_d = din("invf", [128, 16])
    ident_d = din("ident", [128, 128])
    maskp_d = din("maskp", [128, 128])
    maskc_d = din("maskc", [128, 128])
    w_ada = din("w_ada", [L, D, 6 * D])
    b_ada = din("b_ada", [L, 6 * D])
    g_norm1 = din("g_norm1", [L, D])
    w_in = din("w_in", [L, D, INW])
    g_q = din("g_q", [L, DH])
    g_k = din("g_k", [L, DH])
    w_attn_proj = din("w_attn_proj", [L, 512, D])
    wdw_t = din("wdw_t", [L, 128, 8, CONVK])
    bdw_t = din("bdw_t", [L, 128, 8])
    gln_t = din("gln_t", [L, 128, 8])
    bln_t = din("bln_t", [L, 128, 8])
    w_conv_out = din("w_conv_out", [L, D, D])
    w_o = din("w_o", [L, D, D])
    g_norm2 = din("g_norm2", [L, D])
    w_ffn_in = din("w_ffn_in", [L, D, 2 * DFF])
    wf3_t = din("wf3_t", [L, 128, 22, 3])
    bf_t = din("bf_t", [L, 128, 22])
    w_ffn_down = din("w_ffn_down", [L, DFF, D])
    y_out = nc.dram_tensor("y", [OWN_BLK * 128, D], F32, kind="ExternalOutput").ap()

    QS = dscr("QS", [NTOK, AW], BF16)
    KS = dscr("KS", [NTOK, AW], BF16)
    XA = dscr("XA", [NTOK, D])
    XM = dscr("XM", [NTOK, D])
    VS = dscr("VS", [NTOK, NHEAD * 129], BF16)
    ATT = [dscr("ATT%d" % g, [NTOK, 516]) for g in range(3)]
    MOD = dscr("MOD", [L, 128, 6 * D])
    ROPE = dscr("ROPE", [2, 128, NB * 16])
    YB = dscr("YB", [8, 128, NTOK], BF16)

    uid = [0]

    def nm(p):
        uid[0] += 1
        return "%s_%d" % (p, uid[0])

    with contextlib.ExitStack() as top:
        S = Sched(nc, top)
        PE, ACT, DVE, POOL, SP = S.PE, S.ACT, S.DVE, S.POOL, S.SP

        def sbuf(es, shape, dt, p="t"):
            return es.enter_context(nc.sbuf_tensor(nm(p), list(shape), dt))

        def psum(es, shape, dt, p="ps"):
            return es.enter_context(nc.psum_tensor(nm(p), list(shape), dt))

        ident = sbuf(top, [128, 128], BF16, "ident")
        maskP = sbuf(top, [128, 128], BF16, "maskp")
        maskC = sbuf(top, [128, 128], BF16, "maskc")
        ones_bf = sbuf(top, [128, 128], BF16, "ones")
        vcol = sbuf(top, [128, NB], F32, "vcol")
        kb = sbuf(top, [128, NKS], F32, "kb")
        gB = Buf("glob")
        S.dma(POOL, ident[:], ident_d, W=[gB])
        S.dma(POOL, maskP[:], maskp_d, W=[gB])
        S.dma(POOL, maskC[:], maskc_d, W=[gB])
        S.dma(SP, vcol[:], vcol_d, W=[gB])
        S.dma(SP, kb[:], kb_d, W=[gB])
        S.op(DVE, lambda e: e.memset(ones_bf[:], 1.0), W=[Buf()])

        def load_weight(es_stage, dst, src, kc, ncol, rot):
            PIECE = 2048
            stg = [(sbuf(es_stage, [128, PIECE], F32, "stg"), Buf()) for _ in range(3)]
            ring = Ring(stg)
            for kk in range(kc):
                for c0 in range(0, ncol, PIECE):
                    cw = min(PIECE, ncol - c0)
                    st, sb_ = ring.next()
                    S.dma(SP, st[:, :cw], src[kk * 128:(kk + 1) * 128, c0:c0 + cw], W=[sb_])
                    E = rot.next()
                    if E is ACT:
                        S.op(E, lambda e, st=st, kk=kk, c0=c0, cw=cw: e.copy(out=dst[:, kk, c0:c0 + cw], in_=st[:, :cw]),
                             R=[sb_], W=[Buf()])
                    else:
                        S.op(E, lambda e, st=st, kk=kk, c0=c0, cw=cw: e.tensor_copy(out=dst[:, kk, c0:c0 + cw], in_=st[:, :cw]),
                             R=[sb_], W=[Buf()])

        def rsqrt_small(dst, src, scale, srcB, dstB, tmp, tmpB):
            S.op(ACT, lambda e: e.activation(out=tmp, in_=src, func=AF.Ln, scale=scale, bias=EPS), R=[srcB], W=[tmpB])
            S.op(ACT, lambda e: e.activation(out=dst, in_=tmp, func=AF.Exp, scale=-0.5), R=[tmpB], W=[dstB])

        class NormCtx:
            def __init__(self, es, Gm, SHb, constB):
                self.Gm, self.SHb, self.constB = Gm, SHb, constB
                self.junk = sbuf(es, [128, D], BF16, "junk")
                self.junkB = Buf()
                self.tmp = Ring([(sbuf(es, [128, D], F32, "ntmp"), Buf()) for _ in range(1)])
                self.h = Ring([(sbuf(es, [128, D], BF16, "h"), Buf()) for _ in range(2)])
                self.st = Ring([(sbuf(es, [128, 4], F32, "nst"), Buf()) for _ in range(2)])
                self.tp = Ring([(psum(es, [128, 8, 128], BF16, "tp"), Buf()) for _ in range(1)])

            def run(self, xt, xtB, blk, hT_dst, hTB, evac_eng):
                junk, junkB = self.junk, self.junkB
                tmp, tmpB = self.tmp.next()
                h, hB = self.h.next()
                st, stB = self.st.next()
                tp, tpB = self.tp.next()
                S.op(ACT, lambda e: e.activation(out=junk[:], in_=xt[:], func=AF.Square, accum_out=st[:, 0:1]),
                     R=[xtB], W=[junkB, stB])
                S.op(ACT, lambda e: e.activation(out=st[:, 1:2], in_=st[:, 0:1], func=AF.Ln, scale=1.0 / D, bias=EPS),
                     R=[stB], W=[stB])
                S.op(ACT, lambda e: e.activation(out=st[:, 2:3], in_=st[:, 1:2], func=AF.Exp, scale=-0.5),
                     R=[stB], W=[stB])
                S.op(DVE, lambda e: e.tensor_tensor(out=st[:, 3:4], in0=st[:, 2:3], in1=vcol[:, blk:blk + 1], op=ALU.mult),
                     R=[stB], W=[stB])
                S.op(DVE, lambda e: e.scalar_tensor_tensor(out=tmp[:], in0=xt[:], scalar=st[:, 3:4], in1=self.Gm[:],
                                                           op0=ALU.mult, op1=ALU.mult),
                     R=[xtB, stB, self.constB], W=[tmpB])
                S.op(DVE, lambda e: e.scalar_tensor_tensor(out=h[:], in0=self.SHb[:], scalar=vcol[:, blk:blk + 1], in1=tmp[:],
                                                           op0=ALU.mult, op1=ALU.add),
                     R=[tmpB, self.constB], W=[hB])
                for kk in range(8):
                    S.op(PE, lambda e, kk=kk: e.transpose(out=tp[:, kk, :], in_=h[:, kk * 128:(kk + 1) * 128], identity=ident[:]),
                         R=[hB], W=[tpB], sig=(kk == 7))
                if evac_eng is ACT:
                    S.op(ACT, lambda e: e.copy(out=hT_dst, in_=tp[:]), R=[tpB], W=[hTB])
                else:
                    S.op(evac_eng, lambda e: e.tensor_copy(out=hT_dst, in_=tp[:]), R=[tpB], W=[hTB])

        def load_bcast(es, dst, src_row, B):
            S.dma(SP, dst[:], src_row.partition_broadcast(128), W=[B])

        def prologue():
            with contextlib.ExitStack() as es:
                NE = NB * 16
                pi = sbuf(es, [128, NB], I32)
                pf = sbuf(es, [128, NB], F32)
                invf = sbuf(es, [128, 16], F32)
                ang = sbuf(es, [128, NB, 16], F32)
                kf = sbuf(es, [128, NB, 16], F32)
                ki = sbuf(es, [128, NB, 16], I32)
                r = sbuf(es, [128, NB, 16], F32)
                m = sbuf(es, [128, NB, 16], F32)
                r2 = sbuf(es, [128, NB, 16], F32)
                sn = sbuf(es, [128, NB, 16], F32)
                cs = sbuf(es, [128, NB, 16], F32)
                B = Buf()
                S.dma(SP, pi[:], pos_t, W=[B])
                S.dma(SP, invf[:], invf_d, W=[B])
                V = lambda fn, R=(B,), W=(B,): S.op(DVE, fn, R=list(R), W=list(W))
                V(lambda e: e.tensor_copy(out=pf[:], in_=pi[:]))
                V(lambda e: e.tensor_tensor(out=ang[:], in0=pf[:].unsqueeze(2).broadcast_to([128, NB, 16]),
                                            in1=invf[:].unsqueeze(1).broadcast_to([128, NB, 16]), op=ALU.mult))
                V(lambda e: e.tensor_scalar(out=kf[:], in0=ang[:], scalar1=1.0 / TWO_PI, scalar2=None, op0=ALU.mult))
                V(lambda e: e.tensor_copy(out=ki[:], in_=kf[:]))
                V(lambda e: e.tensor_copy(out=kf[:], in_=ki[:]))
                V(lambda e: e.scalar_tensor_tensor(out=r[:], in0=kf[:], scalar=-C1, in1=ang[:], op0=ALU.mult, op1=ALU.add))
                V(lambda e: e.scalar_tensor_tensor(out=r[:], in0=kf[:], scalar=-C2, in1=r[:], op0=ALU.mult, op1=ALU.add))

                def wrap(t):
                    V(lambda e: e.tensor_scalar(out=m[:], in0=t[:], scalar1=np.pi, scalar2=-TWO_PI, op0=ALU.is_gt, op1=ALU.mult))
                    V(lambda e: e.tensor_tensor(out=t[:], in0=t[:], in1=m[:], op=ALU.add))
                    V(lambda e: e.tensor_scalar(out=m[:], in0=t[:], scalar1=-np.pi, scalar2=TWO_PI, op0=ALU.is_lt, op1=ALU.mult))
                    V(lambda e: e.tensor_tensor(out=t[:], in0=t[:], in1=m[:], op=ALU.add))
                    V(lambda e: e.tensor_scalar(out=t[:], in0=t[:], scalar1=3.1415925, scalar2=-3.1415925, op0=ALU.min, op1=ALU.max))

                wrap(r)
                V(lambda e: e.tensor_scalar(out=r2[:], in0=r[:], scalar1=np.pi / 2, scalar2=None, op0=ALU.add))
                wrap(r2)
                S.op(ACT, lambda e: e.activation(out=sn[:], in_=r[:], func=AF.Sin), R=[B], W=[B])
                S.op(ACT, lambda e: e.activation(out=cs[:], in_=r2[:], func=AF.Sin), R=[B], W=[B])
                S.dma(SP, ROPE[0], cs[:].rearrange("p b i -> p (b i)"), R=[B])
                S.dma(SP, ROPE[1], sn[:].rearrange("p b i -> p (b i)"), R=[B])
                S.barrier()
            with contextlib.ExitStack() as es:
                ct = sbuf(es, [128, 8], F32)
                ca = sbuf(es, [128, 8], F32)
                crep = sbuf(es, [128, 8, 128], F32)
                B = Buf()
                S.dma(SP, ct[:], c_t, W=[B])
                S.op(ACT, lambda e: e.activation(out=ca[:], in_=ct[:], func=AF.Silu), R=[B], W=[B])
                S.op(DVE, lambda e: e.tensor_copy(out=crep[:], in_=ca[:].unsqueeze(2).broadcast_to([128, 8, 128])), R=[B], W=[B])
                stg = Ring([(sbuf(es, [128, 8, 512], F32, "astg"), Buf()) for _ in range(2)])
                pss = Ring([(psum(es, [128, 512], F32, "aps"), Buf()) for _ in range(2)])
                bada = sbuf(es, [128, 6 * D], F32)
                modt = sbuf(es, [128, 6 * D], F32)
                badaB, modB = Buf(), Buf()
                for l in range(L):
                    load_bcast(es, bada, b_ada[l], badaB)
                    wv = w_ada[l].rearrange("(k p) c -> p k c", p=128)
                    for ctile in range(12):
                        st, stB = stg.next()
                        ps, psB = pss.next()
                        S.dma(SP, st[:], wv[:, :, ctile * 512:(ctile + 1) * 512], W=[stB])
                        for kk in range(8):
                            S.op(PE, lambda e, kk=kk, st=st, ps=ps: e.matmul(ps[:], lhsT=crep[:, kk, :], rhs=st[:, kk, :],
                                                                             start=(kk == 0), stop=(kk == 7)),
                                 R=[stB, B], W=[psB], sig=(kk == 7))
                        S.op(DVE, lambda e, ps=ps, ctile=ctile: e.tensor_tensor(out=modt[:, ctile * 512:(ctile + 1) * 512], in0=ps[:],
                                                                                in1=bada[:, ctile * 512:(ctile + 1) * 512], op=ALU.add),
                             R=[psB, badaB], W=[modB])
                    S.dma(SP, MOD[l], modt[:], R=[modB])
                S.barrier()

        def phase1(l):
            Xin = xw if l == 0 else XA
            with contextlib.ExitStack() as es:
                wq = sbuf(es, [128, 8, 3 * AW], BF16, "wq")
                with contextlib.ExitStack() as es2:
                    load_weight(es2, wq, w_in[l][:, 0:3 * AW], 8, 3 * AW, Ring([DVE, POOL, ACT]))
                    S.barrier()
                cB = Buf()
                Gm = sbuf(es, [128, D], F32, "Gm")
                SHb = sbuf(es, [128, D], F32, "SHb")
                g1b = sbuf(es, [128, D], F32, "g1b")
                gq = sbuf(es, [128, DH], F32, "gq")
                gk = sbuf(es, [128, DH], F32, "gk")
                cosT = sbuf(es, [128, NB, 16], F32, "cosT")
                sinT = sbuf(es, [128, NB, 16], F32, "sinT")
                load_bcast(es, g1b, g_norm1[l], cB)
                load_bcast(es, gq, g_q[l], cB)
                load_bcast(es, gk, g_k[l], cB)
                S.dma(SP, Gm[:], MOD[l][:, D:2 * D], W=[cB])
                S.dma(SP, SHb[:], MOD[l][:, 0:D], W=[cB])
                S.dma(SP, cosT[:].rearrange("p b i -> p (b i)"), ROPE[0], W=[cB])
                S.dma(SP, sinT[:].rearrange("p b i -> p (b i)"), ROPE[1], W=[cB])
                S.op(DVE, lambda e: e.scalar_tensor_tensor(out=Gm[:], in0=Gm[:], scalar=1.0, in1=g1b[:], op0=ALU.add, op1=ALU.mult),
                     R=[cB], W=[cB])
                norm = NormCtx(es, Gm, SHb, cB)
                xts = Ring([(sbuf(es, [128, D], F32, "xt"), Buf()) for _ in range(3)])
                hTs = Ring([(sbuf(es, [128, 8, 128], BF16, "hT"), Buf()) for _ in range(2)])
                pss = Ring([(psum(es, [128, 512], F32, "p1ps"), Buf()) for _ in range(4)])
                sqs = Ring([(sbuf(es, [128, 512], F32, "sq"), Buf()) for _ in range(2)])
                s4s = Ring([(sbuf(es, [128, 12], F32, "s4"), Buf()) for _ in range(2)])
                qns = Ring([(sbuf(es, [128, 512], F32, "qn"), Buf()) for _ in range(2)])
                qos = Ring([(sbuf(es, [128, 512], BF16, "qo"), Buf()) for _ in range(3)])
                rts = Ring([(sbuf(es, [128, 4, 4, 16], F32, "rt"), Buf()) for _ in range(2)])
                vos = []
                for _ in range(2):
                    vo = sbuf(es, [128, 4, 129], BF16, "vo")
                    vB = Buf()
                    S.op(POOL, lambda e, vo=vo: e.memset(vo[:], 1.0), W=[vB])
                    vos.append((vo, vB))
                vos = Ring(vos)
                for blk in range(Kb[l], NB):
                    xt, xtB = xts.next()
                    hT, hTB = hTs.next()
                    S.dma(SP, xt[:], Xin[blk * 128:(blk + 1) * 128, :], W=[xtB])
                    norm.run(xt, xtB, blk, hT[:], hTB, DVE)
                    for t in range(9):
                        ps, psB = pss.next()
                        for kk in range(8):
                            S.op(PE, lambda e, kk=kk, ps=ps, t=t: e.matmul(ps[:], lhsT=hT[:, kk, :], rhs=wq[:, kk, t * 512:(t + 1) * 512],
                                                                           start=(kk == 0), stop=(kk == 7)),
                                 R=[hTB], W=[psB], sig=(kk == 7))
                        if t < 6:
                            gvec = gq if t < 3 else gk
                            dstD = QS if t < 3 else KS
                            tt = t % 3
                            sq, sqB = sqs.next()
                            s4, s4B = s4s.next()
                            qn, qnB = qns.next()
                            qo, qoB = qos.next()
                            rt, rtB = rts.next()
                            S.op(ACT, lambda e, sq=sq, ps=ps: e.activation(out=sq[:], in_=ps[:], func=AF.Square), R=[psB], W=[sqB])
                            S.op(DVE, lambda e, s4=s4, sq=sq: e.tensor_reduce(out=s4[:, 0:4], in_=sq[:].rearrange("p (h d) -> p h d", h=4),
                                                                              axis=AX.X, op=ALU.add), R=[sqB], W=[s4B])
                            rsqrt_small(s4[:, 8:12], s4[:, 0:4], 1.0 / DH, s4B, s4B, s4[:, 4:8], s4B)
                            for j in range(4):
                                S.op(DVE, lambda e, j=j, qn=qn, ps=ps, s4=s4, gvec=gvec: e.scalar_tensor_tensor(
                                    out=qn[:, j * 128:(j + 1) * 128], in0=ps[:, j * 128:(j + 1) * 128], scalar=s4[:, 8 + j:9 + j],
                                    in1=gvec[:], op0=ALU.mult, op1=ALU.mult), R=[psB, s4B, cB], W=[qnB])
                            S.op(POOL, lambda e, qo=qo, qn=qn: e.tensor_copy(out=qo[:], in_=qn[:]), R=[qnB], W=[qoB])
                            qn3 = qn[:].rearrange("p (h d) -> p h d", h=4)
                            qo3 = qo[:].rearrange("p (h d) -> p h d", h=4)
                            cb = cosT[:, blk, :].unsqueeze(1).broadcast_to([128, 4, 16])
                            sb_ = sinT[:, blk, :].unsqueeze(1).broadcast_to([128, 4, 16])
                            t1, t2 = qn3[:, :, 0:16], qn3[:, :, 16:32]
                            S.op(DVE, lambda e, rt=rt, t1=t1, cb=cb: e.tensor_tensor(out=rt[:, 0], in0=t1, in1=cb, op=ALU.mult), R=[qnB, cB], W=[rtB])
                            S.op(DVE, lambda e, rt=rt, t2=t2, sb_=sb_: e.tensor_tensor(out=rt[:, 1], in0=t2, in1=sb_, op=ALU.mult), R=[qnB, cB], W=[rtB])
                            S.op(DVE, lambda e, rt=rt, t2=t2, cb=cb: e.tensor_tensor(out=rt[:, 2], in0=t2, in1=cb, op=ALU.mult), R=[qnB, cB], W=[rtB])
                            S.op(DVE, lambda e, rt=rt, t1=t1, sb_=sb_: e.tensor_tensor(out=rt[:, 3], in0=t1, in1=sb_, op=ALU.mult), R=[qnB, cB], W=[rtB])
                            S.op(DVE, lambda e, rt=rt, qo3=qo3: e.tensor_tensor(out=qo3[:, :, 0:16], in0=rt[:, 0], in1=rt[:, 1], op=ALU.subtract),
                                 R=[rtB], W=[qoB])
                            S.op(DVE, lambda e, rt=rt, qo3=qo3: e.tensor_tensor(out=qo3[:, :, 16:32], in0=rt[:, 2], in1=rt[:, 3], op=ALU.add),
                                 R=[rtB], W=[qoB])
                            S.dma(POOL, dstD[blk * 128:(blk + 1) * 128, tt * 512:(tt + 1) * 512], qo[:], R=[qoB])
                        else:
                            tt = t - 6
                            vo, vB = vos.next()
                            S.op(ACT, lambda e, vo=vo, ps=ps: e.copy(out=vo[:, :, 0:128], in_=ps[:].rearrange("p (h d) -> p h d", h=4)),
                                 R=[psB], W=[vB])
                            S.dma(POOL, VS[blk * 128:(blk + 1) * 128, tt * 516:(tt + 1) * 516], vo[:].rearrange("p h d -> p (h d)"), R=[vB])
                S.barrier()

        def phase2(l):
            with contextlib.ExitStack() as es:
                def ring(n, shape, dt, p, ps=False):
                    return Ring([((psum if ps else sbuf)(es, shape, dt, p), Buf()) for _ in range(n)])
                cB = Buf()
                gqp = sbuf(es, [128, 1], F32, "gqp")
                gkp = sbuf(es, [128, 1], F32, "gkp")
                S.dma(SP, gqp[:], g_q[l].rearrange("(p o) -> p o", o=1), W=[cB])
                S.dma(SP, gkp[:], g_k[l].rearrange("(p o) -> p o", o=1), W=[cB])
                S.op(DVE, lambda e: e.memset(gqp[0:32, :], 1.0), R=[cB], W=[cB])
                S.op(DVE, lambda e: e.memset(gkp[0:32, :], 1.0), R=[cB], W=[cB])
                Qts = ring(3, [128, 512], BF16, "Qt")
                Kcs = ring(3, [128, 512], BF16, "Kc")
                Kps = ring(3, [128, 512], BF16, "Kp")
                Vcs = ring(3, [128, 4, 129], BF16, "Vc")
                Vps = ring(3, [128, 4, 129], BF16, "Vp")
                Tps = ring(2, [128, 6, 128], BF16, "Tps", ps=True)
                qkTs = ring(3, [128, 6, 128], BF16, "qkT")
                Sps = ring(3, [128, 2, 2, 128], F32, "Sps", ps=True)
                PTs = ring(3, [128, 2, 2, 128], BF16, "PT")
                Ops = ring(2, [128, 2, 129], F32, "Ops", ps=True)
                Ots = ring(3, [128, 4, 129], F32, "Ot")
                for rg in (Qts, Kcs, Vcs, qkTs, PTs):
                    for (t_, b_) in rg.items:
                        S.op(POOL, lambda e, t_=t_: e.memset(t_[:], 0.0), W=[b_])
                ul = units[l]
                loaded = {}

                def loads(ui):
                    (g, d, base, nq) = ul[ui]
                    m0, r = divmod(base, d)
                    qv = QS.rearrange("(m d) c -> m d c", d=d)
                    kv = KS.rearrange("(m d) c -> m d c", d=d)
                    vv = VS.rearrange("(m d) c -> m d c", d=d)
                    Qt, QtB = Qts.next()
                    Kc, KcB = Kcs.next()
                    Kp, KpB = Kps.next()
                    Vc, VcB = Vcs.next()
                    Vp, VpB = Vps.next()
                    cs_ = slice(512 * g, 512 * g + 512)
                    vs_ = slice(516 * g, 516 * g + 516)
                    S.dma(SP, Qt[:nq, :], qv[m0:m0 + nq, r, cs_], W=[QtB])
                    S.dma(SP, Kp[:, :], kv[m0 - 128:m0, r, cs_], W=[KpB])
                    S.dma(SP, Kc[:nq, :], kv[m0:m0 + nq, r, cs_], W=[KcB])
                    S.dma(SP, Vp[:].rearrange("p h d -> p (h d)"), vv[m0 - 128:m0, r, vs_], W=[VpB])
                    S.dma(SP, Vc[:nq].rearrange("p h d -> p (h d)"), vv[m0:m0 + nq, r, vs_], W=[VcB])
                    loaded[ui] = (Qt, QtB, Kc, KcB, Kp, KpB, Vc, VcB, Vp, VpB, Ots.next())

                def stageA(ui, hp):
                    (g, d, base, nq) = ul[ui]
                    (Qt, QtB, Kc, KcB, Kp, KpB, Vc, VcB, Vp, VpB, (Ot, OtB)) = loaded[ui]
                    colp = kcols[(d, base - 128 * d)]
                    colc = kcols[(d, base)]
                    T, TB = Tps.next()
                    qkT, qkTB = qkTs.next()
                    Sp, SpB = Sps.next()
                    PT, PTB = PTs.next()
                    for jj in range(2):
                        j = 2 * hp + jj
                        S.op(PE, lambda e, jj=jj, j=j: e.transpose(out=T[:, jj, :nq], in_=Qt[:nq, j * 128:(j + 1) * 128], identity=ident[:nq, :nq]),
                             R=[QtB], W=[TB], sig=False)
                        S.op(PE, lambda e, jj=jj, j=j: e.transpose(out=T[:, 2 + jj, :], in_=Kp[:, j * 128:(j + 1) * 128], identity=ident[:]),
                             R=[KpB], W=[TB], sig=False)
                        S.op(PE, lambda e, jj=jj, j=j: e.transpose(out=T[:, 4 + jj, :nq], in_=Kc[:nq, j * 128:(j + 1) * 128], identity=ident[:nq, :nq]),
                             R=[KcB], W=[TB], sig=(jj == 1))
                    S.op(ACT, lambda e: e.copy(out=qkT[:, 0:2, :nq], in_=T[:, 0:2, :nq]), R=[TB], W=[qkTB])
                    if nq == 128:
                        S.op(DVE, lambda e: e.tensor_copy(out=qkT[:, 2:6, :], in_=T[:, 2:6, :]), R=[TB], W=[qkTB])
                    else:
                        S.op(DVE, lambda e: e.tensor_copy(out=qkT[:, 2:4, :], in_=T[:, 2:4, :]), R=[TB], W=[qkTB])
                        S.op(DVE, lambda e: e.tensor_copy(out=qkT[:, 4:6, :nq], in_=T[:, 4:6, :nq]), R=[TB], W=[qkTB])
                    for jj in range(2):
                        S.op(PE, lambda e, jj=jj: e.matmul(Sp[:, 0, jj, :nq], lhsT=ident[:], rhs=maskP[:, :nq], start=True, stop=False),
                             R=[], W=[SpB], sig=False)
                        S.op(PE, lambda e, jj=jj: e.matmul(Sp[:, 0, jj, :nq], lhsT=qkT[:, 2 + jj, :], rhs=qkT[:, jj, :nq], start=False, stop=True),
                             R=[qkTB], W=[SpB], sig=False)
                        S.op(PE, lambda e, jj=jj: e.matmul(Sp[:nq, 1, jj, :nq], lhsT=ident[:nq, :nq], rhs=maskC[:nq, :nq], start=True, stop=False),
                             R=[], W=[SpB], sig=False)
                        S.op(PE, lambda e, jj=jj: e.matmul(Sp[:nq, 1, jj, :nq], lhsT=qkT[:, 4 + jj, :nq], rhs=qkT[:, jj, :nq], start=False, stop=True),
                             R=[qkTB], W=[SpB], sig=(jj == 1))
                    S.op(ACT, lambda e: e.activation(out=PT[:, 0, :, :nq], in_=Sp[:, 0, :, :nq], func=AF.Exp, scale=SCALE,
                                                     bias=kb[:, colp:colp + 1]), R=[SpB], W=[PTB])
                    S.op(ACT, lambda e: e.activation(out=PT[:nq, 1, :, :nq], in_=Sp[:nq, 1, :, :nq], func=AF.Exp, scale=SCALE,
                                                     bias=kb[:nq, colc:colc + 1]), R=[SpB], W=[PTB])
                    return (PT, PTB)

                def stageB(ui, hp, PT, PTB):
                    (g, d, base, nq) = ul[ui]
                    (Qt, QtB, Kc, KcB, Kp, KpB, Vc, VcB, Vp, VpB, (Ot, OtB)) = loaded[ui]
                    m0, r = divmod(base, d)
                    Op, OpB = Ops.next()
                    for jj in range(2):
                        j = 2 * hp + jj
                        S.op(PE, lambda e, jj=jj, j=j: e.matmul(Op[:nq, jj, :], lhsT=PT[:, 0, jj, :nq], rhs=Vp[:, j, :], start=True, stop=False),
                             R=[PTB, VpB], W=[OpB], sig=False)
                        S.op(PE, lambda e, jj=jj, j=j: e.matmul(Op[:nq, jj, :], lhsT=PT[:nq, 1, jj, :nq], rhs=Vc[:nq, j, :], start=False, stop=True),
                             R=[PTB, VcB], W=[OpB], sig=(jj == 1))
                    S.op(DVE, lambda e: e.tensor_copy(out=Ot[:nq, 2 * hp:2 * hp + 2, :], in_=Op[:nq, :, :]), R=[OpB], W=[OtB])
                    if hp == 1:
                        av = ATT[g].rearrange("(m d) c -> m d c", d=d)
                        S.dma(POOL, av[m0:m0 + nq, r, :], Ot[:nq].rearrange("p h d -> p (h d)"), R=[OtB])
                        del loaded[ui]

                nu = len(ul)
                loads(0)
                if nu > 1:
                    loads(1)
                prev = None
                for ui in range(nu):
                    for hp in range(2):
                        cur = (ui, hp) + stageA(ui, hp)
                        if prev is not None:
                            stageB(*prev)
                        if hp == 0 and ui + 2 < nu:
                            loads(ui + 2)
                        prev = cur
                stageB(*prev)
                S.barrier()

        def phase3a(l):
            Xin = xw if l == 0 else XA
            WT = 256
            with contextlib.ExitStack() as es:
                wcv = sbuf(es, [128, 8, 2048], BF16, "wcv")
                wco = sbuf(es, [128, 8, D], BF16, "wco")
                dg = sbuf(es, [128, 8, CONVK, 128], BF16, "dg")
                bdw = sbuf(es, [128, 8], F32, "bdw")
                gln = sbuf(es, [128, 8], F32, "gln")
                bln = sbuf(es, [128, 8], F32, "bln")
                cB = Buf()
                with contextlib.ExitStack() as es2:
                    rot = Ring([DVE, POOL, ACT])
                    load_weight(es2, wcv, w_in[l][:, 3 * AW:3 * AW + 2048], 8, 2048, rot)
                    load_weight(es2, wco, w_conv_out[l], 8, D, rot)
                    wdw = sbuf(es2, [128, 8, CONVK], F32, "wdw")
                    idf = sbuf(es2, [128, 128], F32, "idf")
                    dB = Buf()
                    S.dma(SP, wdw[:], wdw_t[l], W=[dB])
                    S.dma(SP, idf[:], ident_d, W=[dB])
                    for c in range(8):
                        for k in range(CONVK):
                            S.op(ACT, lambda e, c=c, k=k: e.activation(out=dg[:, c, k, :], in_=idf[:], func=AF.Identity, scale=wdw[:, c, k:k + 1]),
                                 R=[dB], W=[Buf()])
                    S.barrier()
                Gm = sbuf(es, [128, D], F32, "Gm")
                SHb = sbuf(es, [128, D], F32, "SHb")
                g1b = sbuf(es, [128, D], F32, "g1b")
                load_bcast(es, g1b, g_norm1[l], cB)
                S.dma(SP, Gm[:], MOD[l][:, D:2 * D], W=[cB])
                S.dma(SP, SHb[:], MOD[l][:, 0:D], W=[cB])
                S.dma(SP, bdw[:], bdw_t[l], W=[cB])
                S.dma(SP, gln[:], gln_t[l], W=[cB])
                S.dma(SP, bln[:], bln_t[l], W=[cB])
                S.op(DVE, lambda e: e.scalar_tensor_tensor(out=Gm[:], in0=Gm[:], scalar=1.0, in1=g1b[:], op0=ALU.add, op1=ALU.mult),
                     R=[cB], W=[cB])
                norm = NormCtx(es, Gm, SHb, cB)
                xts = Ring([(sbuf(es, [128, D], F32, "xt"), Buf()) for _ in range(3)])
                hTs = Ring([(sbuf(es, [128, 8, WT], BF16, "hT"), [Buf(), Buf()]) for _ in range(2)])
                uT = sbuf(es, [128, 8, 30 + WT], BF16, "uT")
                uTB = [Buf() for _ in range(8)]
                S.op(POOL, lambda e: e.memset(uT[:], 0.0), W=uTB)
                sgt = Ring([(sbuf(es, [128, WT], F32, "sgt"), Buf()) for _ in range(2)])
                yv = sbuf(es, [128, 8, WT], F32, "yv"); yB = [Buf() for _ in range(8)]
                ybf = sbuf(es, [128, 8, WT], BF16, "ybf"); ybfB = Buf()
                ysq = sbuf(es, [128, 8, WT], BF16, "ysq"); ysqB = Buf()
                stt_ = sbuf(es, [128, 5, WT], F32, "lnst"); stB = Buf()
                actT = sbuf(es, [128, 8, WT], BF16, "actT"); actB = [Buf() for _ in range(8)]
                ybos = Ring([(sbuf(es, [128, 8, WT], BF16, "ybo"), Buf()) for _ in range(2)])
                psv = Ring([(psum(es, [128, 2, WT], F32, "psv"), Buf()) for _ in range(2)])
                psc = Ring([(psum(es, [128, 2, WT], F32, "psc"), [Buf(), Buf()]) for _ in range(2)])
                psS = (psum(es, [128, 2, WT], F32, "psS"), Buf())
                psY = Ring([(psum(es, [128, 2, WT], F32, "psY"), Buf()) for _ in range(2)])
                ybv = YB.rearrange("c p t -> p c t")

                blk0 = Mb[l]
                while blk0 < NB:
                    nbk = min(2, NB - blk0)
                    W_ = 128 * nbk
                    hT, hTBs = hTs.next()
                    for bi in range(nbk):
                        blk = blk0 + bi
                        xt, xtB = xts.next()
                        S.dma(SP, xt[:], Xin[blk * 128:(blk + 1) * 128, :], W=[xtB])
                        norm.run(xt, xtB, blk, hT[:, :, bi * 128:(bi + 1) * 128], hTBs[bi], ACT)
                    hR = hTBs[:nbk]
                    for cp in range(4):
                        pc, pcBs = psc.next()
                        for ci in range(2):
                            c = 2 * cp + ci
                            pv, pvB = psv.next()
                            sg, sgB = sgt.next()
                            for half in range(2):
                                col0 = half * D + c * 128
                                for kk in range(8):
                                    S.op(PE, lambda e, kk=kk, pv=pv, half=half, col0=col0: e.matmul(
                                        pv[:, half, :W_], lhsT=wcv[:, kk, col0:col0 + 128], rhs=hT[:, kk, :W_],
                                        start=(kk == 0), stop=(kk == 7)), R=hR, W=[pvB], sig=(kk == 7 and half == 1))
                            S.op(ACT, lambda e, pv=pv, sg=sg: e.activation(out=sg[:, :W_], in_=pv[:, 1, :W_], func=AF.Sigmoid), R=[pvB], W=[sgB])
                            S.op(DVE, lambda e, pv=pv, sg=sg, c=c: e.tensor_tensor(out=uT[:, c, 30:30 + W_], in0=pv[:, 0, :W_], in1=sg[:, :W_], op=ALU.mult),
                                 R=[pvB, sgB], W=[uTB[c]])
                        for ci in range(2):
                            c = 2 * cp + ci
                            for tap in range(CONVK):
                                S.op(PE, lambda e, c=c, ci=ci, tap=tap, pc=pc: e.matmul(pc[:, ci, :W_], lhsT=dg[:, c, tap, :], rhs=uT[:, c, tap:tap + W_],
                                                                                        start=(tap == 0), stop=(tap == CONVK - 1)),
                                     R=[uTB[c]], W=[pcBs[0]], sig=(tap == CONVK - 1 and ci == 1))
                        for ci in range(2):
                            c = 2 * cp + ci
                            S.op(ACT, lambda e, c=c, ci=ci, pc=pc: e.activation(out=yv[:, c, :W_], in_=pc[:, ci, :W_], func=AF.Identity, bias=bdw[:, c:c + 1]),
                                 R=[pcBs[0], cB], W=[yB[c]])
                            S.op(POOL, lambda e, c=c: e.tensor_copy(out=uT[:, c, 0:30], in_=uT[:, c, W_:W_ + 30]), R=[uTB[c]], W=[uTB[c]])
                    S.op(POOL, lambda e: e.tensor_copy(out=ybf[:, :, :W_], in_=yv[:, :, :W_]), R=yB, W=[ybfB])
                    S.op(ACT, lambda e: e.activation(out=ysq[:, :, :W_], in_=yv[:, :, :W_], func=AF.Square), R=yB, W=[ysqB])
                    pS, pSB = psS
                    for c in range(8):
                        S.op(PE, lambda e, c=c: e.matmul(pS[:, 0, :W_], lhsT=ones_bf[:], rhs=ybf[:, c, :W_], start=(c == 0), stop=(c == 7)),
                             R=[ybfB], W=[pSB], sig=False)
                    for c in range(8):
                        S.op(PE, lambda e, c=c: e.matmul(pS[:, 1, :W_], lhsT=ones_bf[:], rhs=ysq[:, c, :W_], start=(c == 0), stop=(c == 7)),
                             R=[ysqB], W=[pSB], sig=(c == 7))
                    mean_, msq_, var_, lnv_, rstd_ = (stt_[:, i, :W_] for i in range(5))
                    S.op(ACT, lambda e: e.activation(out=mean_, in_=pS[:, 0, :W_], func=AF.Copy, scale=1.0 / D), R=[pSB], W=[stB])
                    S.op(ACT, lambda e: e.activation(out=msq_, in_=mean_, func=AF.Square), R=[stB], W=[stB])
                    S.op(DVE, lambda e: e.scalar_tensor_tensor(out=var_, in0=pS[:, 1, :W_], scalar=1.0 / D, in1=msq_, op0=ALU.mult, op1=ALU.subtract),
                         R=[pSB, stB], W=[stB])
                    S.op(DVE, lambda e: e.tensor_scalar(out=var_, in0=var_, scalar1=0.0, scalar2=None, op0=ALU.max), R=[stB], W=[stB])
                    S.op(ACT, lambda e: e.activation(out=lnv_, in_=var_, func=AF.Ln, bias=EPS), R=[stB], W=[stB])
                    S.op(ACT, lambda e: e.activation(out=rstd_, in_=lnv_, func=AF.Exp, scale=-0.5), R=[stB], W=[stB])
                    S.op(DVE, lambda e: e.tensor_tensor(out=yv[:, :, :W_], in0=yv[:, :, :W_],
                                                        in1=mean_.unsqueeze(1).broadcast_to([128, 8, W_]), op=ALU.subtract), R=yB + [stB], W=yB)
                    S.op(DVE, lambda e: e.tensor_tensor(out=yv[:, :, :W_], in0=yv[:, :, :W_],
                                                        in1=rstd_.unsqueeze(1).broadcast_to([128, 8, W_]), op=ALU.mult), R=yB + [stB], W=yB)
                    for c in range(8):
                        S.op(ACT, lambda e, c=c: e.activation(out=actT[:, c, :W_], in_=yv[:, c, :W_], func=AF.Silu,
                                                              scale=gln[:, c:c + 1], bias=bln[:, c:c + 1]), R=[yB[c], cB], W=[actB[c]])
                    ybo, yboB = ybos.next()
                    for op_ in range(4):
                        pY, pYB = psY.next()
                        for oi in range(2):
                            oc = 2 * op_ + oi
                            for kk in range(8):
                                S.op(PE, lambda e, kk=kk, oc=oc, oi=oi, pY=pY: e.matmul(pY[:, oi, :W_], lhsT=wco[:, kk, oc * 128:(oc + 1) * 128], rhs=actT[:, kk, :W_],
                                                                                        start=(kk == 0), stop=(kk == 7)), R=actB, W=[pYB],
                                     sig=(kk == 7 and oi == 1))
                        if op_ % 2 == 0:
                            S.op(DVE, lambda e, op_=op_, pY=pY, ybo=ybo: e.tensor_copy(out=ybo[:, 2 * op_:2 * op_ + 2, :W_], in_=pY[:, :, :W_]), R=[pYB], W=[yboB])
                        else:
                            S.op(ACT, lambda e, op_=op_, pY=pY, ybo=ybo: e.copy(out=ybo[:, 2 * op_:2 * op_ + 2, :W_], in_=pY[:, :, :W_]), R=[pYB], W=[yboB])
                    for c in range(8):
                        S.dma(POOL, YB[c, :, blk0 * 128:blk0 * 128 + W_], ybo[:, c, :W_], R=[yboB])
                    blk0 += nbk
                S.barrier()

        def phase3b(l):
            Xin = xw if l == 0 else XA
            WT = 256
            with contextlib.ExitStack() as es:
                wg = sbuf(es, [128, 8, 2048], BF16, "wg")
                wap = sbuf(es, [128, 4, D], BF16, "wap")
                wo = sbuf(es, [128, 8, D], BF16, "wo")
                with contextlib.ExitStack() as es2:
                    rot = Ring([DVE, POOL, ACT])
                    load_weight(es2, wg, w_in[l][:, 3 * AW + 2048:INW], 8, 2048, rot)
                    load_weight(es2, wap, w_attn_proj[l], 4, D, rot)
                    load_weight(es2, wo, w_o[l], 8, D, rot)
                    S.barrier()
                cB = Buf()
                Gm = sbuf(es, [128, D], F32, "Gm")
                SHb = sbuf(es, [128, D], F32, "SHb")
                gtb = sbuf(es, [128, D], F32, "gtb")
                g1b = sbuf(es, [128, D], F32, "g1b")
                load_bcast(es, g1b, g_norm1[l], cB)
                S.dma(SP, Gm[:], MOD[l][:, D:2 * D], W=[cB])
                S.dma(SP, SHb[:], MOD[l][:, 0:D], W=[cB])
                S.dma(SP, gtb[:], MOD[l][:, 2 * D:3 * D], W=[cB])
                S.op(DVE, lambda e: e.scalar_tensor_tensor(out=Gm[:], in0=Gm[:], scalar=1.0, in1=g1b[:], op0=ALU.add, op1=ALU.mult),
                     R=[cB], W=[cB])
                norm = NormCtx(es, Gm, SHb, cB)
                xts = Ring([(sbuf(es, [128, D], F32, "xt"), Buf()) for _ in range(4)])
                xms = Ring([(sbuf(es, [128, D], F32, "xm"), Buf()) for _ in range(2)])
                hTs = Ring([(sbuf(es, [128, 8, WT], BF16, "hT"), [Buf(), Buf()]) for _ in range(2)])
                Ars = Ring([[(sbuf(es, [128, 516], F32, "A"), Buf()) for _ in range(3)] for _ in range(2)])
                abfs = Ring([(sbuf(es, [128, 512], BF16, "abf"), Buf()) for _ in range(2)])
                asts = Ring([(sbuf(es, [128, 8], F32, "ast"), Buf()) for _ in range(2)])
                attnTs = Ring([(sbuf(es, [128, 4, WT], BF16, "attnT"), [Buf(), Buf()]) for _ in range(2)])
                ybTs = Ring([(sbuf(es, [128, 8, WT], BF16, "ybT"), Buf()) for _ in range(2)])
                tpa = Ring([(psum(es, [128, 4, 128], BF16, "tpa"), Buf()) for _ in range(1)])
                sg2 = Ring([(sbuf(es, [128, 2, WT], F32, "sg2"), Buf()) for _ in range(2)])
                m1 = Ring([(sbuf(es, [128, 2, WT], F32, "m1"), Buf()) for _ in range(2)])
                mT = sbuf(es, [128, 8, WT], BF16, "mT"); mTB = [Buf() for _ in range(8)]
                psA = Ring([(psum(es, [128, 2, WT], F32, "psA"), Buf()) for _ in range(2)])
                psB_ = Ring([(psum(es, [128, WT], F32, "psB"), Buf()) for _ in range(2)])
                pso = Ring([(psum(es, [128, 512], F32, "pso"), Buf()) for _ in range(2)])
                ybv = YB.rearrange("c p t -> p c t")

                blk0 = Mb[l]
                while blk0 < NB:
                    nbk = min(2, NB - blk0)
                    W_ = 128 * nbk
                    hT, hTBs = hTs.next()
                    attnT, attnTBs = attnTs.next()
                    ybT, ybTB = ybTs.next()
                    for c in range(8):
                        S.dma(SP, ybT[:, c, :W_], YB[c, :, blk0 * 128:blk0 * 128 + W_], W=[ybTB])
                    xl = []
                    for bi in range(nbk):
                        blk = blk0 + bi
                        xt, xtB = xts.next()
                        xl.append((xt, xtB))
                        S.dma(SP, xt[:], Xin[blk * 128:(blk + 1) * 128, :], W=[xtB])
                        norm.run(xt, xtB, blk, hT[:, :, bi * 128:(bi + 1) * 128], hTBs[bi], ACT)
                        As = Ars.next()
                        abf, abfB = abfs.next()
                        ast, astB = asts.next()
                        for g in range(3):
                            S.dma(SP, As[g][0][:], ATT[g][blk * 128:(blk + 1) * 128, :], W=[As[g][1]])
                        A0, A1, A2 = As[0][0], As[1][0], As[2][0]
                        S.op(POOL, lambda e, A0=A0, A1=A1: e.tensor_tensor(out=A0[:], in0=A0[:], in1=A1[:], op=ALU.add), R=[As[1][1]], W=[As[0][1]])
                        S.op(POOL, lambda e, A0=A0, A2=A2: e.tensor_tensor(out=A0[:], in0=A0[:], in1=A2[:], op=ALU.add), R=[As[2][1]], W=[As[0][1]])
                        A3 = A0[:].rearrange("p (h d) -> p h d", h=4)
                        S.op(DVE, lambda e, ast=ast, A3=A3: e.tensor_scalar(out=ast[:, 0:4], in0=A3[:, :, 128], scalar1=1e-30, scalar2=None, op0=ALU.max),
                             R=[As[0][1]], W=[astB])
                        S.op(DVE, lambda e, ast=ast: e.reciprocal(out=ast[:, 4:8], in_=ast[:, 0:4]), R=[astB], W=[astB])
                        for j in range(4):
                            S.op(ACT, lambda e, j=j, abf=abf, A3=A3, ast=ast: e.activation(out=abf[:, j * 128:(j + 1) * 128], in_=A3[:, j, 0:128],
                                                                                         func=AF.Identity, scale=ast[:, 4 + j:5 + j]),
                                 R=[As[0][1], astB], W=[abfB])
                        tp_, tpB_ = tpa.next()
                        for j in range(4):
                            S.op(PE, lambda e, j=j, abf=abf, tp_=tp_: e.transpose(out=tp_[:, j, :], in_=abf[:, j * 128:(j + 1) * 128], identity=ident[:]),
                                 R=[abfB], W=[tpB_], sig=(j == 3))
                        S.op(DVE, lambda e, bi=bi, tp_=tp_: e.tensor_copy(out=attnT[:, :, bi * 128:(bi + 1) * 128], in_=tp_[:, :, :]),
                             R=[tpB_], W=[attnTBs[bi]])
                    hR = hTBs[:nbk]
                    aR = attnTBs[:nbk]
                    for oc in range(8):
                        pA, pAB = psA.next()
                        pB, pBB = psB_.next()
                        s2, s2B = sg2.next()
                        mm1, m1B = m1.next()
                        for gi in range(2):
                            col0 = gi * D + oc * 128
                            for kk in range(8):
                                S.op(PE, lambda e, kk=kk, gi=gi, col0=col0, pA=pA: e.matmul(pA[:, gi, :W_], lhsT=wg[:, kk, col0:col0 + 128], rhs=hT[:, kk, :W_],
                                                                                            start=(kk == 0), stop=(kk == 7)), R=hR, W=[pAB],
                                     sig=(kk == 7 and gi == 1))
                        for j in range(4):
                            S.op(PE, lambda e, j=j, pB=pB, oc=oc: e.matmul(pB[:, :W_], lhsT=wap[:, j, oc * 128:(oc + 1) * 128], rhs=attnT[:, j, :W_],
                                                                           start=(j == 0), stop=(j == 3)), R=aR, W=[pBB], sig=(j == 3))
                        S.op(ACT, lambda e, s2=s2, pA=pA: e.activation(out=s2[:, :, :W_], in_=pA[:, :, :W_], func=AF.Sigmoid), R=[pAB], W=[s2B])
                        S.op(DVE, lambda e, s2=s2, mm1=mm1, pB=pB: e.tensor_tensor(out=mm1[:, 0, :W_], in0=pB[:, :W_], in1=s2[:, 0, :W_], op=ALU.mult),
                             R=[pBB, s2B], W=[m1B])
                        S.op(POOL, lambda e, s2=s2, mm1=mm1, oc=oc: e.tensor_tensor(out=mm1[:, 1, :W_], in0=s2[:, 1, :W_], in1=ybT[:, oc, :W_], op=ALU.mult),
                             R=[s2B, ybTB], W=[m1B])
                        S.op(POOL, lambda e, mm1=mm1, oc=oc: e.tensor_tensor(out=mT[:, oc, :W_], in0=mm1[:, 0, :W_], in1=mm1[:, 1, :W_], op=ALU.add),
                             R=[m1B], W=[mTB[oc]])
                    for bi in range(nbk):
                        blk = blk0 + bi
                        xt, xtB = xl[bi]
                        xm, xmB = xms.next()
                        for hf in range(2):
                            po, poB = pso.next()
                            for kk in range(8):
                                S.op(PE, lambda e, kk=kk, po=po, bi=bi, hf=hf: e.matmul(po[:], lhsT=mT[:, kk, bi * 128:(bi + 1) * 128],
                                                                                        rhs=wo[:, kk, hf * 512:(hf + 1) * 512],
                                                                                        start=(kk == 0), stop=(kk == 7)), R=mTB, W=[poB], sig=(kk == 7))
                            S.op(DVE, lambda e, po=po, hf=hf, xm=xm: e.tensor_tensor(out=xm[:, hf * 512:(hf + 1) * 512], in0=po[:],
                                                                                     in1=gtb[:, hf * 512:(hf + 1) * 512], op=ALU.mult), R=[poB, cB], W=[xmB])
                        S.op(POOL, lambda e, xt=xt, xm=xm: e.tensor_tensor(out=xm[:], in0=xm[:], in1=xt[:], op=ALU.add), R=[xtB, xmB], W=[xmB])
                        S.dma(POOL, XM[blk * 128:(blk + 1) * 128, :], xm[:], R=[xmB])
                    blk0 += nbk
                S.barrier()

        def phase4(l):
            WT = 256
            last = (l == L - 1)
            with contextlib.ExitStack() as es:
                wfi = sbuf(es, [128, 8, 2 * DFF], BF16, "wfi")
                wfd = sbuf(es, [128, 22, D], BF16, "wfd")
                with contextlib.ExitStack() as es2:
                    rot = Ring([DVE, POOL, ACT])
                    load_weight(es2, wfi, w_ffn_in[l], 8, 2 * DFF, rot)
                    load_weight(es2, wfd, w_ffn_down[l], 22, D, rot)
                    S.barrier()
                cB = Buf()
                Gm = sbuf(es, [128, D], F32, "Gm")
                SHb = sbuf(es, [128, D], F32, "SHb")
                gtb = sbuf(es, [128, D], F32, "gtb")
                wf3 = sbuf(es, [128, 22, 3], F32, "wf3")
                bfv = sbuf(es, [128, 22], F32, "bfv")
                xo = sbuf(es, [128, D], F32, "xo")
                load_bcast(es, xo, g_norm2[l], cB)
                S.dma(SP, Gm[:], MOD[l][:, 4 * D:5 * D], W=[cB])
                S.dma(SP, SHb[:], MOD[l][:, 3 * D:4 * D], W=[cB])
                S.dma(SP, gtb[:], MOD[l][:, 5 * D:6 * D], W=[cB])
                S.dma(SP, wf3[:], wf3_t[l], W=[cB])
                S.dma(SP, bfv[:], bf_t[l], W=[cB])
                S.op(DVE, lambda e: e.scalar_tensor_tensor(out=Gm[:], in0=Gm[:], scalar=1.0, in1=xo[:], op0=ALU.add, op1=ALU.mult),
                     R=[cB], W=[cB])
                xoB = Buf()
                xoB.r.append(cB.w)
                norm = NormCtx(es, Gm, SHb, cB)
                xts = Ring([(sbuf(es, [128, D], F32, "xt"), Buf()) for _ in range(2)])
                hTs = Ring([(sbuf(es, [128, 8, WT], BF16, "hT"), [Buf(), Buf()]) for _ in range(2)])
                carry = sbuf(es, [128, 22, 2], F32, "carry"); carB = [Buf() for _ in range(22)]
                S.op(POOL, lambda e: e.memset(carry[:], 0.0), W=carB)
                gbs = Ring([(sbuf(es, [128, 2 + WT], F32, "gb"), Buf()) for _ in range(4)])
                accs = Ring([(sbuf(es, [128, WT], F32, "acc"), Buf()) for _ in range(4)])
                sils = Ring([(sbuf(es, [128, WT], F32, "sil"), Buf()) for _ in range(2)])
                actT = sbuf(es, [128, 22, WT], BF16, "actT"); actB = [Buf() for _ in range(22)]
                psg = Ring([(psum(es, [128, 2, WT], F32, "psg"), Buf()) for _ in range(4)])
                pso = Ring([(psum(es, [128, 512], F32, "pso"), Buf()) for _ in range(2)])
                blk0 = Mb[l]
                while blk0 < NB:
                    nbk = min(2, NB - blk0)
                    W_ = 128 * nbk
                    hT, hTBs = hTs.next()
                    xl = []
                    for bi in range(nbk):
                        blk = blk0 + bi
                        xt, xtB = xts.next()
                        xl.append((xt, xtB))
                        S.dma(SP, xt[:], XM[blk * 128:(blk + 1) * 128, :], W=[xtB])
                        norm.run(xt, xtB, blk, hT[:, :, bi * 128:(bi + 1) * 128], hTBs[bi], ACT)
                    hR = hTBs[:nbk]
                    for fp in range(11):
                        items = []
                        for fi in range(2):
                            f = 2 * fp + fi
                            pg, pgB = psg.next()
                            gb, gbB = gbs.next()
                            acc, accB = accs.next()
                            for half in range(2):
                                col0 = half * DFF + f * 128
                                for kk in range(8):
                                    S.op(PE, lambda e, kk=kk, pg=pg, half=half, col0=col0: e.matmul(
                                        pg[:, half, :W_], lhsT=wfi[:, kk, col0:col0 + 128], rhs=hT[:, kk, :W_],
                                        start=(kk == 0), stop=(kk == 7)), R=hR, W=[pgB], sig=(kk == 7 and half == 1))
                            S.op(POOL, lambda e, gb=gb, f=f: e.tensor_copy(out=gb[:, 0:2], in_=carry[:, f, :]), R=[carB[f]], W=[gbB])
                            S.op(ACT, lambda e, gb=gb, pg=pg: e.copy(out=gb[:, 2:2 + W_], in_=pg[:, 0, :W_]), R=[pgB], W=[gbB])
                            S.op(POOL, lambda e, gb=gb, f=f: e.tensor_copy(out=carry[:, f, :], in_=gb[:, W_:W_ + 2]), R=[gbB], W=[carB[f]])
                            items.append((f, pg, pgB, gb, gbB, acc, accB))
                        for (f, pg, pgB, gb, gbB, acc, accB) in items:
                            S.op(DVE, lambda e, f=f, gb=gb, acc=acc: e.tensor_scalar(out=acc[:, :W_], in0=gb[:, 0:W_], scalar1=wf3[:, f, 0:1],
                                                                                     scalar2=bfv[:, f:f + 1], op0=ALU.mult, op1=ALU.add),
                                 R=[gbB, cB], W=[accB])
                        for tap in (1, 2):
                            for (f, pg, pgB, gb, gbB, acc, accB) in items:
                                S.op(DVE, lambda e, f=f, gb=gb, acc=acc, tap=tap: e.scalar_tensor_tensor(
                                    out=acc[:, :W_], in0=gb[:, tap:tap + W_], scalar=wf3[:, f, tap:tap + 1], in1=acc[:, :W_],
                                    op0=ALU.mult, op1=ALU.add), R=[gbB, accB], W=[accB])
                        for (f, pg, pgB, gb, gbB, acc, accB) in items:
                            sl, slB = sils.next()
                            S.op(ACT, lambda e, sl=sl, acc=acc: e.activation(out=sl[:, :W_], in_=acc[:, :W_], func=AF.Silu), R=[accB], W=[slB])
                            S.op(DVE, lambda e, sl=sl, pg=pg, f=f: e.tensor_tensor(out=actT[:, f, :W_], in0=pg[:, 1, :W_], in1=sl[:, :W_], op=ALU.mult),
                                 R=[pgB, slB], W=[actB[f]])
                    for bi in range(nbk):
                        blk = blk0 + bi
                        xt, xtB = xl[bi]
                        for hf in range(2):
                            po, poB = pso.next()
                            for f in range(22):
                                S.op(PE, lambda e, f=f, po=po, bi=bi, hf=hf: e.matmul(po[:], lhsT=actT[:, f, bi * 128:(bi + 1) * 128],
                                                                                      rhs=wfd[:, f, hf * 512:(hf + 1) * 512],
                                                                                      start=(f == 0), stop=(f == 21)), R=actB, W=[poB], sig=(f == 21))
                            S.op(DVE, lambda e, po=po, hf=hf: e.tensor_tensor(out=xo[:, hf * 512:(hf + 1) * 512], in0=po[:],
                                                                              in1=gtb[:, hf * 512:(hf + 1) * 512], op=ALU.mult), R=[poB, cB], W=[xoB])
                        S.op(POOL, lambda e, xt=xt: e.tensor_tensor(out=xo[:], in0=xo[:], in1=xt[:], op=ALU.add), R=[xtB, xoB], W=[xoB])
                        if last:
                            if blk >= OWN0:
                                S.dma(POOL, y_out[(blk - OWN0) * 128:(blk - OWN0 + 1) * 128, :], xo[:], R=[xoB])
                        else:
                            S.dma(POOL, XA[blk * 128:(blk + 1) * 128, :], xo[:], R=[xoB])
                    blk0 += nbk
                S.barrier()

        S.barrier()
        prologue()
        for l in range(L):
            for pi, ph in enumerate((phase1, phase2, phase3a, phase3b, phase4)):
                if pi < stop_after:
                    ph(l)
        S.barrier()
    return nc, S.n_ins


def make_in_maps(inputs, L=4, cores=range(8)):
    Kb, Mb, OWN0, NB = geometry(L)
    NTOK = NB * 128
    kcols = keyset_columns(L)
    x = np.asarray(inputs["x"], dtype=np.float32)
    c = np.asarray(inputs["c"], dtype=np.float32)
    positions = np.asarray(inputs["positions"], dtype=np.int32)
    f32 = lambda k: np.ascontiguousarray(np.asarray(inputs[k], dtype=np.float32)[:L])
    shared = {k: f32(k) for k in ("w_ada", "b_ada", "g_norm1", "w_in", "g_q", "g_k", "w_attn_proj", "w_conv_out",
                                  "w_o", "g_norm2", "w_ffn_in", "w_ffn_down")}
    wdw = f32("w_conv_dw")
    shared["wdw_t"] = np.ascontiguousarray(wdw.reshape(L, CONVK, 8, 128).transpose(0, 3, 2, 1))
    per_ch = lambda a, n: np.ascontiguousarray(a.reshape(L, n, 128).transpose(0, 2, 1))
    shared["bdw_t"] = per_ch(f32("b_conv_dw"), 8)
    shared["gln_t"] = per_ch(f32("g_conv_ln"), 8)
    shared["bln_t"] = per_ch(f32("b_conv_ln"), 8)
    wf = f32("w_ffn_dw")
    shared["wf3_t"] = np.ascontiguousarray(wf.reshape(L, 3, 22, 128).transpose(0, 3, 2, 1))
    shared["bf_t"] = per_ch(f32("b_ffn_dw"), 22)
    inv = (np.float32(500000.0) ** (-np.arange(0, 32, 2, dtype=np.float32) / np.float32(32))).astype(np.float32)
    shared["invf"] = np.ascontiguousarray(np.broadcast_to(inv[None, :], (128, 16))).astype(np.float32)
    shared["ident"] = np.eye(128, dtype=np.float32)
    kk, qq = np.meshgrid(np.arange(128), np.arange(128), indexing="ij")
    shared["maskp"] = np.where(kk >= qq, 0.0, NEG).astype(np.float32)
    shared["maskc"] = np.where(kk <= qq, 0.0, NEG).astype(np.float32)
    maps = []
    for core in cores:
        b, j = core // 4, core % 4
        off = 4096 * j - OWN0 * 128
        gpos = off + np.arange(NTOK)
        valid = gpos >= 0
        gsafe = np.clip(gpos, 0, SEQ - 1)
        xw = np.where(valid[:, None], x[b, gsafe, :], np.float32(0.0)).astype(np.float32)
        pw = np.where(valid, positions[b, gsafe], 0).astype(np.int32)
        vb = valid.reshape(NB, 128)[:, 0].astype(np.float32)
        kbt = np.zeros((128, len(kcols)), dtype=np.float32)
        for (d, start), col in kcols.items():
            toks = np.clip(start + d * np.arange(128), 0, NTOK - 1)
            kbt[:, col] = np.where(valid[toks], 0.0, NEG)
        m = dict(shared)
        m["xw"] = np.ascontiguousarray(xw)
        m["pos_t"] = np.ascontiguousarray(pw.reshape(NB, 128).T)
        m["vcol"] = np.ascontiguousarray(np.broadcast_to(vb[None, :], (128, NB))).astype(np.float32)
        m["kbias"] = kbt
        m["c_t"] = np.ascontiguousarray(c[b].reshape(8, 128).T)
        maps.append(m)
    return maps


_CACHE = {}


def kernel(**inputs):
    L = 4
    if L not in _CACHE:
        _CACHE[L] = build_program(L)[0]
    nc = _CACHE[L]
    maps = make_in_maps(inputs, L)
    res = run_bass_kernel_spmd(nc, maps, core_ids=list(range(8)))
    out = np.empty((2, SEQ, D), dtype=np.float32)
    for core in range(8):
        b, j = core // 4, core % 4
        out[b, 4096 * j:4096 * (j + 1), :] = res.results[core]["y"]
    return out
```

```python
import contextlib
import os
import numpy as np
import ml_dtypes

import concourse.bass as bass
import concourse.mybir as mybir
from concourse.bass_utils import run_bass_kernel_spmd

F32 = mybir.dt.float32
BF16 = mybir.dt.bfloat16
I32 = mybir.dt.int32
AF = mybir.ActivationFunctionType
ALU = mybir.AluOpType
AX = mybir.AxisListType

D = 1024
NHEAD = 12
DH = 128
AW = 1536
DFF = 2816
INW = 8704
CONVK = 31
EPS = 1e-6
NEG = -30000.0
SCALE = DH ** -0.5
OWN_BLK = 32
SEQ = 16384
DILS = (1, 4, 16)
TWO_PI = 6.283185307179586
C1 = 6.28125
C2 = TWO_PI - C1


def geometry(L):
    Kb = [17 * i for i in range(L)]
    Mb = [k + 16 for k in Kb]
    own0 = 17 * L
    nb = own0 + OWN_BLK
    return Kb, Mb, own0, nb


def attn_units(L):
    Kb, Mb, own0, nb = geometry(L)
    out = []
    for l in range(L):
        q0, q1 = Mb[l] * 128, nb * 128
        lst = []
        for g, d in enumerate(DILS):
            span = 128 * d
            c0 = q0
            while c0 < q1:
                n = min(span, q1 - c0)
                for r in range(d):
                    lst.append((g, d, c0 + r, n // d))
                c0 += span
        out.append(lst)
    return out


def keyset_columns(L):
    cols = {}
    for lst in attn_units(L):
        for (g, d, base, nq) in lst:
            for ks in ((d, base - 128 * d), (d, base)):
                if ks not in cols:
                    cols[ks] = len(cols)
    return cols


class Tok:
    __slots__ = ("eng", "sem", "val")

    def __init__(self, eng, sem, val):
        self.eng, self.sem, self.val = eng, sem, val


class Buf:
    __slots__ = ("name", "w", "r")

    def __init__(self, name=""):
        self.name, self.w, self.r = name, None, []


class Eng:
    def __init__(self, name, e, sem, self_sync):
        self.name, self.e, self.sem, self.self_sync = name, e, sem, self_sync
        self.count = 0
        self.waited = {}
        self.pending = []
        self.slots = []
        self.slot_i = 0


class Sched:
    def __init__(self, nc, es, ndma=14):
        self.nc = nc

        def sem(n):
            return es.enter_context(nc.semaphore(n))

        self.PE = Eng("pe", nc.tensor, sem("s_pe"), False)
        self.ACT = Eng("act", nc.scalar, sem("s_act"), True)
        self.DVE = Eng("dve", nc.vector, sem("s_dve"), True)
        self.POOL = Eng("pool", nc.gpsimd, sem("s_pool"), True)
        self.SP = Eng("sp", nc.sync, sem("s_sp"), True)
        self.engs = [self.PE, self.ACT, self.DVE, self.POOL, self.SP]
        for q in (self.SP, self.POOL, self.ACT):
            nq_ = int(os.environ.get("POOLDMA", "14")) if q is self.POOL else ndma
            q.slots = [[sem("d_%s%d" % (q.name, i)), 0] for i in range(nq_)]
        self.n_ins = 0

    def _wait(self, E, sem, val):
        key = id(sem)
        if E.waited.get(key, 0) < val:
            E.e.wait_ge(sem, val)
            E.waited[key] = val
            self.n_ins += 1

    def _deps(self, E, R, W):
        toks = []
        for b in R:
            if b.w is not None:
                toks.append(b.w)
        for b in W:
            if b.w is not None:
                toks.append(b.w)
            toks.extend(b.r)
        for t in toks:
            if t.eng is E and not E.self_sync:
                continue
            if t.val is None:
                if t.eng is E:
                    continue
                raise RuntimeError("dependency on unsignalled instruction (%s)" % t.eng.name)
            self._wait(E, t.sem, t.val)

    def op(self, E, fn, R=(), W=(), sig=True):
        self._deps(E, R, W)
        ins = fn(E.e)
        self.n_ins += 1
        tok = Tok(E, E.sem, None)
        E.pending.append(tok)
        if sig:
            E.count += 1
            ins.then_inc(E.sem, 1)
            for t in E.pending:
                t.val = E.count
            E.pending = []
        for b in W:
            b.w = tok
            b.r = []
        for b in R:
            b.r = [t for t in b.r if t.eng is not E or t.eng is None]
            b.r.append(tok)
        return ins

    def dma(self, Q, out, in_, R=(), W=()):
        self._deps(Q, R, W)
        slot = Q.slots[Q.slot_i % len(Q.slots)]
        Q.slot_i += 1
        if slot[1] > 0:
            self._wait(Q, slot[0], slot[1])
        slot[1] += 16
        Q.e.dma_start(out=out, in_=in_).then_inc(slot[0], 16)
        self.n_ins += 1
        tok = Tok(None, slot[0], slot[1])
        for b in W:
            b.w = tok
            b.r = []
        for b in R:
            b.r.append(tok)

    def barrier(self):
        for E in self.engs:
            if E.pending:
                raise RuntimeError("pending unsignalled instructions at barrier on " + E.name)
        for E in self.engs:
            for F in self.engs:
                if F.count > 0 and (F is not E or E.self_sync):
                    self._wait(E, F.sem, F.count)
            for Q in (self.SP, self.POOL, self.ACT):
                for s in Q.slots:
                    if s[1] > 0:
                        self._wait(E, s[0], s[1])


class Ring:
    def __init__(self, items):
        self.items = items
        self.i = 0

    def next(self):
        it = self.items[self.i % len(self.items)]
        self.i += 1
        return it


def build_program(L=4, debug=False, stop_after=99, maxblk=None):
    Kb, Mb, OWN0, NB = geometry(L)
    NTOK = NB * 128
    units = attn_units(L)
    kcols = keyset_columns(L)
    NKS = len(kcols)

    nc = bass.Bass("TRN2", target_bir_lowering=False)

    def din(name, shape, dt=F32):
        return nc.dram_tensor(name, list(shape), dt, kind="ExternalInput").ap()

    dbg_kind = "ExternalOutput" if debug else "Internal"

    def dscr(name, shape, dt=F32):
        return nc.dram_tensor(name, list(shape), dt, kind=dbg_kind).ap()

    xw = din("xw", [NTOK, D])
    pos_t = din("pos_t", [128, NB], I32)
    vcol_d = din("vcol", [128, NB])
    kb_d = din("kbias", [128, NKS])
    c_t = din("c_t", [128, 8])
    invf_d = din("invf", [128, 16])
    ident_d = din("ident", [128, 128])
    maskp_d = din("maskp", [128, 128])
    maskc_d = din("maskc", [128, 128])
    w_ada = din("w_ada", [L, D, 6 * D])
    b_ada = din("b_ada", [L, 6 * D])
    g_norm1 = din("g_norm1", [L, D])
    w_in = din("w_in", [L, D, INW])
    g_q = din("g_q", [L, DH])
    g_k = din("g_k", [L, DH])
    w_attn_proj = din("w_attn_proj", [L, 512, D])
    wdw_t = din("wdw_t", [L, 128, 8, CONVK])
    bdw_t = din("bdw_t", [L, 128, 8])
    gln_t = din("gln_t", [L, 128, 8])
    bln_t = din("bln_t", [L, 128, 8])
    w_conv_out = din("w_conv_out", [L, D, D])
    w_o = din("w_o", [L, D, D])
    g_norm2 = din("g_norm2", [L, D])
    w_ffn_in = din("w_ffn_in", [L, D, 2 * DFF])
    wf3_t = din("wf3_t", [L, 128, 22, 3])
    bf_t = din("bf_t", [L, 128, 22])
    w_ffn_down = din("w_ffn_down", [L, DFF, D])
    y_out = nc.dram_tensor("y", [OWN_BLK * 128, D], F32, kind="ExternalOutput").ap()

    QS = dscr("QS", [NTOK, AW], BF16)
    KS = dscr("KS", [NTOK, AW], BF16)
    XA = dscr("XA", [NTOK, D])
    XM = dscr("XM", [NTOK, D])
    VS = dscr("VS", [NTOK, NHEAD * 129], BF16)
    ATT = [dscr("ATT%d" % g, [NTOK, 516]) for g in range(3)]
    MOD = dscr("MOD", [L, 128, 6 * D])
    ROPE = dscr("ROPE", [2, 128, NB * 16])
    YB = dscr("YB", [8, 128, NTOK], BF16)

    uid = [0]

    def nm(p):
        uid[0] += 1
        return "%s_%d" % (p, uid[0])

    with contextlib.ExitStack() as top:
        S = Sched(nc, top)
        PE, ACT, DVE, POOL, SP = S.PE, S.ACT, S.DVE, S.POOL, S.SP

        def sbuf(es, shape, dt, p="t"):
            return es.enter_context(nc.sbuf_tensor(nm(p), list(shape), dt))

        def psum(es, shape, dt, p="ps"):
            return es.enter_context(nc.psum_tensor(nm(p), list(shape), dt))

        ident = sbuf(top, [128, 128], BF16, "ident")
        maskP = sbuf(top, [128, 128], BF16, "maskp")
        maskC = sbuf(top, [128, 128], BF16, "maskc")
        ones_bf = sbuf(top, [128, 128], BF16, "ones")
        vcol = sbuf(top, [128, NB], F32, "vcol")
        kb = sbuf(top, [128, NKS], F32, "kb")
        gB = Buf("glob")
        S.dma(POOL, ident[:], ident_d, W=[gB])
        S.dma(POOL, maskP[:], maskp_d, W=[gB])
        S.dma(POOL, maskC[:], maskc_d, W=[gB])
        S.dma(SP, vcol[:], vcol_d, W=[gB])
        S.dma(SP, kb[:], kb_d, W=[gB])
        S.op(DVE, lambda e: e.memset(ones_bf[:], 1.0), W=[Buf()])

        def load_weight(es_stage, dst, src, kc, ncol, rot):
            PIECE = 2048
            stg = [(sbuf(es_stage, [128, PIECE], F32, "stg"), Buf()) for _ in range(3)]
            ring = Ring(stg)
            for kk in range(kc):
                for c0 in range(0, ncol, PIECE):
                    cw = min(PIECE, ncol - c0)
                    st, sb_ = ring.next()
                    S.dma(SP, st[:, :cw], src[kk * 128:(kk + 1) * 128, c0:c0 + cw], W=[sb_])
                    E = rot.next()
                    if E is ACT:
                        S.op(E, lambda e, st=st, kk=kk, c0=c0, cw=cw: e.copy(out=dst[:, kk, c0:c0 + cw], in_=st[:, :cw]),
                             R=[sb_], W=[Buf()])
                    else:
                        S.op(E, lambda e, st=st, kk=kk, c0=c0, cw=cw: e.tensor_copy(out=dst[:, kk, c0:c0 + cw], in_=st[:, :cw]),
                             R=[sb_], W=[Buf()])

        def rsqrt_small(dst, src, scale, srcB, dstB, tmp, tmpB):
            S.op(ACT, lambda e: e.activation(out=tmp, in_=src, func=AF.Ln, scale=scale, bias=EPS), R=[srcB], W=[tmpB])
            S.op(ACT, lambda e: e.activation(out=dst, in_=tmp, func=AF.Exp, scale=-0.5), R=[tmpB], W=[dstB])

        class NormCtx:
            def __init__(self, es, Gm, SHb, constB):
                self.Gm, self.SHb, self.constB = Gm, SHb, constB
                self.junk = sbuf(es, [128, D], BF16, "junk")
                self.junkB = Buf()
                self.tmp = Ring([(sbuf(es, [128, D], F32, "ntmp"), Buf()) for _ in range(1)])
                self.h = Ring([(sbuf(es, [128, D], BF16, "h"), Buf()) for _ in range(2)])
                self.st = Ring([(sbuf(es, [128, 4], F32, "nst"), Buf()) for _ in range(2)])
                self.tp = Ring([(psum(es, [128, 8, 128], BF16, "tp"), Buf()) for _ in range(1)])

            def run(self, xt, xtB, blk, hT_dst, hTB, evac_eng):
                junk, junkB = self.junk, self.junkB
                tmp, tmpB = self.tmp.next()
                h, hB = self.h.next()
                st, stB = self.st.next()
                tp, tpB = self.tp.next()
                S.op(ACT, lambda e: e.activation(out=junk[:], in_=xt[:], func=AF.Square, accum_out=st[:, 0:1]),
                     R=[xtB], W=[junkB, stB])
                S.op(ACT, lambda e: e.activation(out=st[:, 1:2], in_=st[:, 0:1], func=AF.Ln, scale=1.0 / D, bias=EPS),
                     R=[stB], W=[stB])
                S.op(ACT, lambda e: e.activation(out=st[:, 2:3], in_=st[:, 1:2], func=AF.Exp, scale=-0.5),
                     R=[stB], W=[stB])
                S.op(DVE, lambda e: e.tensor_tensor(out=st[:, 3:4], in0=st[:, 2:3], in1=vcol[:, blk:blk + 1], op=ALU.mult),
                     R=[stB], W=[stB])
                S.op(DVE, lambda e: e.scalar_tensor_tensor(out=tmp[:], in0=xt[:], scalar=st[:, 3:4], in1=self.Gm[:],
                                                           op0=ALU.mult, op1=ALU.mult),
                     R=[xtB, stB, self.constB], W=[tmpB])
                S.op(DVE, lambda e: e.scalar_tensor_tensor(out=h[:], in0=self.SHb[:], scalar=vcol[:, blk:blk + 1], in1=tmp[:],
                                                           op0=ALU.mult, op1=ALU.add),
                     R=[tmpB, self.constB], W=[hB])
                for kk in range(8):
                    S.op(PE, lambda e, kk=kk: e.transpose(out=tp[:, kk, :], in_=h[:, kk * 128:(kk + 1) * 128], identity=ident[:]),
                         R=[hB], W=[tpB], sig=(kk == 7))
                if evac_eng is ACT:
                    S.op(ACT, lambda e: e.copy(out=hT_dst, in_=tp[:]), R=[tpB], W=[hTB])
                else:
                    S.op(evac_eng, lambda e: e.tensor_copy(out=hT_dst, in_=tp[:]), R=[tpB], W=[hTB])

        def load_bcast(es, dst, src_row, B):
            S.dma(SP, dst[:], src_row.partition_broadcast(128), W=[B])

        def prologue():
            with contextlib.ExitStack() as es:
                NE = NB * 16
                pi = sbuf(es, [128, NB], I32)
                pf = sbuf(es, [128, NB], F32)
                invf = sbuf(es, [128, 16], F32)
                ang = sbuf(es, [128, NB, 16], F32)
                kf = sbuf(es, [128, NB, 16], F32)
                ki = sbuf(es, [128, NB, 16], I32)
                r = sbuf(es, [128, NB, 16], F32)
                m = sbuf(es, [128, NB, 16], F32)
                r2 = sbuf(es, [128, NB, 16], F32)
                sn = sbuf(es, [128, NB, 16], F32)
                cs = sbuf(es, [128, NB, 16], F32)
                B = Buf()
                S.dma(SP, pi[:], pos_t, W=[B])
                S.dma(SP, invf[:], invf_d, W=[B])
                V = lambda fn, R=(B,), W=(B,): S.op(DVE, fn, R=list(R), W=list(W))
                V(lambda e: e.tensor_copy(out=pf[:], in_=pi[:]))
                V(lambda e: e.tensor_tensor(out=ang[:], in0=pf[:].unsqueeze(2).broadcast_to([128, NB, 16]),
                                            in1=invf[:].unsqueeze(1).broadcast_to([128, NB, 16]), op=ALU.mult))
                V(lambda e: e.tensor_scalar(out=kf[:], in0=ang[:], scalar1=1.0 / TWO_PI, scalar2=None, op0=ALU.mult))
                V(lambda e: e.tensor_copy(out=ki[:], in_=kf[:]))
                V(lambda e: e.tensor_copy(out=kf[:], in_=ki[:]))
                V(lambda e: e.scalar_tensor_tensor(out=r[:], in0=kf[:], scalar=-C1, in1=ang[:], op0=ALU.mult, op1=ALU.add))
                V(lambda e: e.scalar_tensor_tensor(out=r[:], in0=kf[:], scalar=-C2, in1=r[:], op0=ALU.mult, op1=ALU.add))

                def wrap(t):
                    V(lambda e: e.tensor_scalar(out=m[:], in0=t[:], scalar1=np.pi, scalar2=-TWO_PI, op0=ALU.is_gt, op1=ALU.mult))
                    V(lambda e: e.tensor_tensor(out=t[:], in0=t[:], in1=m[:], op=ALU.add))
                    V(lambda e: e.tensor_scalar(out=m[:], in0=t[:], scalar1=-np.pi, scalar2=TWO_PI, op0=ALU.is_lt, op1=ALU.mult))
                    V(lambda e: e.tensor_tensor(out=t[:], in0=t[:], in1=m[:], op=ALU.add))
                    V(lambda e: e.tensor_scalar(out=t[:], in0=t[:], scalar1=3.1415925, scalar2=-3.1415925, op0=ALU.min, op1=ALU.max))

                wrap(r)
                V(lambda e: e.tensor_scalar(out=r2[:], in0=r[:], scalar1=np.pi / 2, scalar2=None, op0=ALU.add))
                wrap(r2)
                S.op(ACT, lambda e: e.activation(out=sn[:], in_=r[:], func=AF.Sin), R=[B], W=[B])
                S.op(ACT, lambda e: e.activation(out=cs[:], in_=r2[:], func=AF.Sin), R=[B], W=[B])
                S.dma(SP, ROPE[0], cs[:].rearrange("p b i -> p (b i)"), R=[B])
                S.dma(SP, ROPE[1], sn[:].rearrange("p b i -> p (b i)"), R=[B])
                S.barrier()
            with contextlib.ExitStack() as es:
                ct = sbuf(es, [128, 8], F32)
                ca = sbuf(es, [128, 8], F32)
                crep = sbuf(es, [128, 8, 128], F32)
                B = Buf()
                S.dma(SP, ct[:], c_t, W=[B])
                S.op(ACT, lambda e: e.activation(out=ca[:], in_=ct[:], func=AF.Silu), R=[B], W=[B])
                S.op(DVE, lambda e: e.tensor_copy(out=crep[:], in_=ca[:].unsqueeze(2).broadcast_to([128, 8, 128])), R=[B], W=[B])
                stg = Ring([(sbuf(es, [128, 8, 512], F32, "astg"), Buf()) for _ in range(2)])
                pss = Ring([(psum(es, [128, 512], F32, "aps"), Buf()) for _ in range(2)])
                bada = sbuf(es, [128, 6 * D], F32)
                modt = sbuf(es, [128, 6 * D], F32)
                badaB, modB = Buf(), Buf()
                for l in range(L):
                    load_bcast(es, bada, b_ada[l], badaB)
                    wv = w_ada[l].rearrange("(k p) c -> p k c", p=128)
                    for ctile in range(12):
                        st, stB = stg.next()
                        ps, psB = pss.next()
                        S.dma(SP, st[:], wv[:, :, ctile * 512:(ctile + 1) * 512], W=[stB])
                        for kk in range(8):
                            S.op(PE, lambda e, kk=kk, st=st, ps=ps: e.matmul(ps[:], lhsT=crep[:, kk, :], rhs=st[:, kk, :],
                                                                             start=(kk == 0), stop=(kk == 7)),
                                 R=[stB, B], W=[psB], sig=(kk == 7))
                        S.op(DVE, lambda e, ps=ps, ctile=ctile: e.tensor_tensor(out=modt[:, ctile * 512:(ctile + 1) * 512], in0=ps[:],
                                                                                in1=bada[:, ctile * 512:(ctile + 1) * 512], op=ALU.add),
                             R=[psB, badaB], W=[modB])
                    S.dma(SP, MOD[l], modt[:], R=[modB])
                S.barrier()

        def phase1(l):
            Xin = xw if l == 0 else XA
            with contextlib.ExitStack() as es:
                wq = sbuf(es, [128, 8, 3 * AW], BF16, "wq")
                with contextlib.ExitStack() as es2:
                    load_weight(es2, wq, w_in[l][:, 0:3 * AW], 8, 3 * AW, Ring([DVE, POOL, ACT]))
                    S.barrier()
                cB = Buf()
                Gm = sbuf(es, [128, D], F32, "Gm")
                SHb = sbuf(es, [128, D], F32, "SHb")
                g1b = sbuf(es, [128, D], F32, "g1b")
                gq = sbuf(es, [128, DH], F32, "gq")
                gk = sbuf(es, [128, DH], F32, "gk")
                cosT = sbuf(es, [128, NB, 16], F32, "cosT")
                sinT = sbuf(es, [128, NB, 16], F32, "sinT")
                load_bcast(es, g1b, g_norm1[l], cB)
                load_bcast(es, gq, g_q[l], cB)
                load_bcast(es, gk, g_k[l], cB)
                S.dma(SP, Gm[:], MOD[l][:, D:2 * D], W=[cB])
                S.dma(SP, SHb[:], MOD[l][:, 0:D], W=[cB])
                S.dma(SP, cosT[:].rearrange("p b i -> p (b i)"), ROPE[0], W=[cB])
                S.dma(SP, sinT[:].rearrange("p b i -> p (b i)"), ROPE[1], W=[cB])
                S.op(DVE, lambda e: e.scalar_tensor_tensor(out=Gm[:], in0=Gm[:], scalar=1.0, in1=g1b[:], op0=ALU.add, op1=ALU.mult),
                     R=[cB], W=[cB])
                norm = NormCtx(es, Gm, SHb, cB)
                xts = Ring([(sbuf(es, [128, D], F32, "xt"), Buf()) for _ in range(3)])
                hTs = Ring([(sbuf(es, [128, 8, 128], BF16, "hT"), Buf()) for _ in range(2)])
                pss = Ring([(psum(es, [128, 512], F32, "p1ps"), Buf()) for _ in range(4)])
                sqs = Ring([(sbuf(es, [128, 512], F32, "sq"), Buf()) for _ in range(2)])
                s4s = Ring([(sbuf(es, [128, 12], F32, "s4"), Buf()) for _ in range(2)])
                qns = Ring([(sbuf(es, [128, 512], F32, "qn"), Buf()) for _ in range(2)])
                qos = Ring([(sbuf(es, [128, 512], BF16, "qo"), Buf()) for _ in range(3)])
                rts = Ring([(sbuf(es, [128, 4, 4, 16], F32, "rt"), Buf()) for _ in range(2)])
                vos = []
                for _ in range(2):
                    vo = sbuf(es, [128, 4, 129], BF16, "vo")
                    vB = Buf()
                    S.op(POOL, lambda e, vo=vo: e.memset(vo[:], 1.0), W=[vB])
                    vos.append((vo, vB))
                vos = Ring(vos)
                for blk in range(Kb[l], NB):
                    xt, xtB = xts.next()
                    hT, hTB = hTs.next()
                    S.dma(SP, xt[:], Xin[blk * 128:(blk + 1) * 128, :], W=[xtB])
                    norm.run(xt, xtB, blk, hT[:], hTB, DVE)
                    for t in range(9):
                        ps, psB = pss.next()
                        for kk in range(8):
                            S.op(PE, lambda e, kk=kk, ps=ps, t=t: e.matmul(ps[:], lhsT=hT[:, kk, :], rhs=wq[:, kk, t * 512:(t + 1) * 512],
                                                                           start=(kk == 0), stop=(kk == 7)),
                                 R=[hTB], W=[psB], sig=(kk == 7))
                        if t < 6:
                            gvec = gq if t < 3 else gk
                            dstD = QS if t < 3 else KS
                            tt = t % 3
                            sq, sqB = sqs.next()
                            s4, s4B = s4s.next()
                            qn, qnB = qns.next()
                            qo, qoB = qos.next()
                            rt, rtB = rts.next()
                            S.op(ACT, lambda e, sq=sq, ps=ps: e.activation(out=sq[:], in_=ps[:], func=AF.Square), R=[psB], W=[sqB])
                            S.op(DVE, lambda e, s4=s4, sq=sq: e.tensor_reduce(out=s4[:, 0:4], in_=sq[:].rearrange("p (h d) -> p h d", h=4),
                                                                              axis=AX.X, op=ALU.add), R=[sqB], W=[s4B])
                            rsqrt_small(s4[:, 8:12], s4[:, 0:4], 1.0 / DH, s4B, s4B, s4[:, 4:8], s4B)
                            for j in range(4):
                                S.op(DVE, lambda e, j=j, qn=qn, ps=ps, s4=s4, gvec=gvec: e.scalar_tensor_tensor(
                                    out=qn[:, j * 128:(j + 1) * 128], in0=ps[:, j * 128:(j + 1) * 128], scalar=s4[:, 8 + j:9 + j],
                                    in1=gvec[:], op0=ALU.mult, op1=ALU.mult), R=[psB, s4B, cB], W=[qnB])
                            S.op(POOL, lambda e, qo=qo, qn=qn: e.tensor_copy(out=qo[:], in_=qn[:]), R=[qnB], W=[qoB])
                            qn3 = qn[:].rearrange("p (h d) -> p h d", h=4)
                            qo3 = qo[:].rearrange("p (h d) -> p h d", h=4)
                            cb = cosT[:, blk, :].unsqueeze(1).broadcast_to([128, 4, 16])
                            sb_ = sinT[:, blk, :].unsqueeze(1).broadcast_to([128, 4, 16])
                            t1, t2 = qn3[:, :, 0:16], qn3[:, :, 16:32]
                            S.op(DVE, lambda e, rt=rt, t1=t1, cb=cb: e.tensor_tensor(out=rt[:, 0], in0=t1, in1=cb, op=ALU.mult), R=[qnB, cB], W=[rtB])
                            S.op(DVE, lambda e, rt=rt, t2=t2, sb_=sb_: e.tensor_tensor(out=rt[:, 1], in0=t2, in1=sb_, op=ALU.mult), R=[qnB, cB], W=[rtB])
                            S.op(DVE, lambda e, rt=rt, t2=t2, cb=cb: e.tensor_tensor(out=rt[:, 2], in0=t2, in1=cb, op=ALU.mult), R=[qnB, cB], W=[rtB])
                            S.op(DVE, lambda e, rt=rt, t1=t1, sb_=sb_: e.tensor_tensor(out=rt[:, 3], in0=t1, in1=sb_, op=ALU.mult), R=[qnB, cB], W=[rtB])
                            S.op(DVE, lambda e, rt=rt, qo3=qo3: e.tensor_tensor(out=qo3[:, :, 0:16], in0=rt[:, 0], in1=rt[:, 1], op=ALU.subtract),
                                 R=[rtB], W=[qoB])
                            S.op(DVE, lambda e, rt=rt, qo3=qo3: e.tensor_tensor(out=qo3[:, :, 16:32], in0=rt[:, 2], in1=rt[:, 3], op=ALU.add),
                                 R=[rtB], W=[qoB])
                            S.dma(POOL, dstD[blk * 128:(blk + 1) * 128, tt * 512:(tt + 1) * 512], qo[:], R=[qoB])
                        else:
                            tt = t - 6
                            vo, vB = vos.next()
                            S.op(ACT, lambda e, vo=vo, ps=ps: e.copy(out=vo[:, :, 0:128], in_=ps[:].rearrange("p (h d) -> p h d", h=4)),
                                 R=[psB], W=[vB])
                            S.dma(POOL, VS[blk * 128:(blk + 1) * 128, tt * 516:(tt + 1) * 516], vo[:].rearrange("p h d -> p (h d)"), R=[vB])
                S.barrier()

        def phase2(l):
            with contextlib.ExitStack() as es:
                def ring(n, shape, dt, p, ps=False):
                    return Ring([((psum if ps else sbuf)(es, shape, dt, p), Buf()) for _ in range(n)])
                cB = Buf()
                gqp = sbuf(es, [128, 1], F32, "gqp")
                gkp = sbuf(es, [128, 1], F32, "gkp")
                S.dma(SP, gqp[:], g_q[l].rearrange("(p o) -> p o", o=1), W=[cB])
                S.dma(SP, gkp[:], g_k[l].rearrange("(p o) -> p o", o=1), W=[cB])
                S.op(DVE, lambda e: e.memset(gqp[0:32, :], 1.0), R=[cB], W=[cB])
                S.op(DVE, lambda e: e.memset(gkp[0:32, :], 1.0), R=[cB], W=[cB])
                Qts = ring(3, [128, 512], BF16, "Qt")
                Kcs = ring(3, [128, 512], BF16, "Kc")
                Kps = ring(3, [128, 512], BF16, "Kp")
                Vcs = ring(3, [128, 4, 129], BF16, "Vc")
                Vps = ring(3, [128, 4, 129], BF16, "Vp")
                Tps = ring(2, [128, 6, 128], BF16, "Tps", ps=True)
                qkTs = ring(3, [128, 6, 128], BF16, "qkT")
                Sps = ring(3, [128, 2, 2, 128], F32, "Sps", ps=True)
                PTs = ring(3, [128, 2, 2, 128], BF16, "PT")
                Ops = ring(2, [128, 2, 129], F32, "Ops", ps=True)
                Ots = ring(3, [128, 4, 129], F32, "Ot")
                for rg in (Qts, Kcs, Vcs, qkTs, PTs):
                    for (t_, b_) in rg.items:
                        S.op(POOL, lambda e, t_=t_: e.memset(t_[:], 0.0), W=[b_])
                ul = units[l]
                loaded = {}

                def loads(ui):
                    (g, d, base, nq) = ul[ui]
                    m0, r = divmod(base, d)
                    qv = QS.rearrange("(m d) c -> m d c", d=d)
                    kv = KS.rearrange("(m d) c -> m d c", d=d)
                    vv = VS.rearrange("(m d) c -> m d c", d=d)
                    Qt, QtB = Qts.next()
                    Kc, KcB = Kcs.next()
                    Kp, KpB = Kps.next()
                    Vc, VcB = Vcs.next()
                    Vp, VpB = Vps.next()
                    cs_ = slice(512 * g, 512 * g + 512)
                    vs_ = slice(516 * g, 516 * g + 516)
                    S.dma(SP, Qt[:nq, :], qv[m0:m0 + nq, r, cs_], W=[QtB])
                    S.dma(SP, Kp[:, :], kv[m0 - 128:m0, r, cs_], W=[KpB])
                    S.dma(SP, Kc[:nq, :], kv[m0:m0 + nq, r, cs_], W=[KcB])
                    S.dma(SP, Vp[:].rearrange("p h d -> p (h d)"), vv[m0 - 128:m0, r, vs_], W=[VpB])
                    S.dma(SP, Vc[:nq].rearrange("p h d -> p (h d)"), vv[m0:m0 + nq, r, vs_], W=[VcB])
                    loaded[ui] = (Qt, QtB, Kc, KcB, Kp, KpB, Vc, VcB, Vp, VpB, Ots.next())

                def stageA(ui, hp):
                    (g, d, base, nq) = ul[ui]
                    (Qt, QtB, Kc, KcB, Kp, KpB, Vc, VcB, Vp, VpB, (Ot, OtB)) = loaded[ui]
                    colp = kcols[(d, base - 128 * d)]
                    colc = kcols[(d, base)]
                    T, TB = Tps.next()
                    qkT, qkTB = qkTs.next()
                    Sp, SpB = Sps.next()
                    PT, PTB = PTs.next()
                    for jj in range(2):
                        j = 2 * hp + jj
                        S.op(PE, lambda e, jj=jj, j=j: e.transpose(out=T[:, jj, :nq], in_=Qt[:nq, j * 128:(j + 1) * 128], identity=ident[:nq, :nq]),
                             R=[QtB], W=[TB], sig=False)
                        S.op(PE, lambda e, jj=jj, j=j: e.transpose(out=T[:, 2 + jj, :], in_=Kp[:, j * 128:(j + 1) * 128], identity=ident[:]),
                             R=[KpB], W=[TB], sig=False)
                        S.op(PE, lambda e, jj=jj, j=j: e.transpose(out=T[:, 4 + jj, :nq], in_=Kc[:nq, j * 128:(j + 1) * 128], identity=ident[:nq, :nq]),
                             R=[KcB], W=[TB], sig=(jj == 1))
                    S.op(ACT, lambda e: e.copy(out=qkT[:, 0:2, :nq], in_=T[:, 0:2, :nq]), R=[TB], W=[qkTB])
                    if nq == 128:
                        S.op(DVE, lambda e: e.tensor_copy(out=qkT[:, 2:6, :], in_=T[:, 2:6, :]), R=[TB], W=[qkTB])
                    else:
                        S.op(DVE, lambda e: e.tensor_copy(out=qkT[:, 2:4, :], in_=T[:, 2:4, :]), R=[TB], W=[qkTB])
                        S.op(DVE, lambda e: e.tensor_copy(out=qkT[:, 4:6, :nq], in_=T[:, 4:6, :nq]), R=[TB], W=[qkTB])
                    for jj in range(2):
                        S.op(PE, lambda e, jj=jj: e.matmul(Sp[:, 0, jj, :nq], lhsT=ident[:], rhs=maskP[:, :nq], start=True, stop=False),
                             R=[], W=[SpB], sig=False)
                        S.op(PE, lambda e, jj=jj: e.matmul(Sp[:, 0, jj, :nq], lhsT=qkT[:, 2 + jj, :], rhs=qkT[:, jj, :nq], start=False, stop=True),
                             R=[qkTB], W=[SpB], sig=False)
                        S.op(PE, lambda e, jj=jj: e.matmul(Sp[:nq, 1, jj, :nq], lhsT=ident[:nq, :nq], rhs=maskC[:nq, :nq], start=True, stop=False),
                             R=[], W=[SpB], sig=False)
                        S.op(PE, lambda e, jj=jj: e.matmul(Sp[:nq, 1, jj, :nq], lhsT=qkT[:, 4 + jj, :nq], rhs=qkT[:, jj, :nq], start=False, stop=True),
                             R=[qkTB], W=[SpB], sig=(jj == 1))
                    S.op(ACT, lambda e: e.activation(out=PT[:, 0, :, :nq], in_=Sp[:, 0, :, :nq], func=AF.Exp, scale=SCALE,
                                                     bias=kb[:, colp:colp + 1]), R=[SpB], W=[PTB])
                    S.op(ACT, lambda e: e.activation(out=PT[:nq, 1, :, :nq], in_=Sp[:nq, 1, :, :nq], func=AF.Exp, scale=SCALE,
                                                     bias=kb[:nq, colc:colc + 1]), R=[SpB], W=[PTB])
                    return (PT, PTB)

                def stageB(ui, hp, PT, PTB):
                    (g, d, base, nq) = ul[ui]
                    (Qt, QtB, Kc, KcB, Kp, KpB, Vc, VcB, Vp, VpB, (Ot, OtB)) = loaded[ui]
                    m0, r = divmod(base, d)
                    Op, OpB = Ops.next()
                    for jj in range(2):
                        j = 2 * hp + jj
                        S.op(PE, lambda e, jj=jj, j=j: e.matmul(Op[:nq, jj, :], lhsT=PT[:, 0, jj, :nq], rhs=Vp[:, j, :], start=True, stop=False),
                             R=[PTB, VpB], W=[OpB], sig=False)
                        S.op(PE, lambda e, jj=jj, j=j: e.matmul(Op[:nq, jj, :], lhsT=PT[:nq, 1, jj, :nq], rhs=Vc[:nq, j, :], start=False, stop=True),
                             R=[PTB, VcB], W=[OpB], sig=(jj == 1))
                    S.op(DVE, lambda e: e.tensor_copy(out=Ot[:nq, 2 * hp:2 * hp + 2, :], in_=Op[:nq, :, :]), R=[OpB], W=[OtB])
                    if hp == 1:
                        av = ATT[g].rearrange("(m d) c -> m d c", d=d)
                        S.dma(POOL, av[m0:m0 + nq, r, :], Ot[:nq].rearrange("p h d -> p (h d)"), R=[OtB])
                        del loaded[ui]

                nu = len(ul)
                loads(0)
                if nu > 1:
                    loads(1)
                prev = None
                for ui in range(nu):
                    for hp in range(2):
                        cur = (ui, hp) + stageA(ui, hp)
                        if prev is not None:
                            stageB(*prev)
                        if hp == 0 and ui + 2 < nu:
                            loads(ui + 2)
                        prev = cur
                stageB(*prev)
                S.barrier()

        def phase3a(l):
            Xin = xw if l == 0 else XA
            WT = 256
            with contextlib.ExitStack() as es:
                wcv = sbuf(es, [128, 8, 2048], BF16, "wcv")
                wco = sbuf(es, [128, 8, D], BF16, "wco")
                dg = sbuf(es, [128, 8, CONVK, 128], BF16, "dg")
                bdw = sbuf(es, [128, 8], F32, "bdw")
                gln = sbuf(es, [128, 8], F32, "gln")
                bln = sbuf(es, [128, 8], F32, "bln")
                cB = Buf()
                with contextlib.ExitStack() as es2:
                    rot = Ring([DVE, POOL, ACT])
                    load_weight(es2, wcv, w_in[l][:, 3 * AW:3 * AW + 2048], 8, 2048, rot)
                    load_weight(es2, wco, w_conv_out[l], 8, D, rot)
                    wdw = sbuf(es2, [128, 8, CONVK], F32, "wdw")
                    idf = sbuf(es2, [128, 128], F32, "idf")
                    dB = Buf()
                    S.dma(SP, wdw[:], wdw_t[l], W=[dB])
                    S.dma(SP, idf[:], ident_d, W=[dB])
                    for c in range(8):
                        for k in range(CONVK):
                            S.op(ACT, lambda e, c=c, k=k: e.activation(out=dg[:, c, k, :], in_=idf[:], func=AF.Identity, scale=wdw[:, c, k:k + 1]),
                                 R=[dB], W=[Buf()])
                    S.barrier()
                Gm = sbuf(es, [128, D], F32, "Gm")
                SHb = sbuf(es, [128, D], F32, "SHb")
                g1b = sbuf(es, [128, D], F32, "g1b")
                load_bcast(es, g1b, g_norm1[l], cB)
                S.dma(SP, Gm[:], MOD[l][:, D:2 * D], W=[cB])
                S.dma(SP, SHb[:], MOD[l][:, 0:D], W=[cB])
                S.dma(SP, bdw[:], bdw_t[l], W=[cB])
                S.dma(SP, gln[:], gln_t[l], W=[cB])
                S.dma(SP, bln[:], bln_t[l], W=[cB])
                S.op(DVE, lambda e: e.scalar_tensor_tensor(out=Gm[:], in0=Gm[:], scalar=1.0, in1=g1b[:], op0=ALU.add, op1=ALU.mult),
                     R=[cB], W=[cB])
                norm = NormCtx(es, Gm, SHb, cB)
                xts = Ring([(sbuf(es, [128, D], F32, "xt"), Buf()) for _ in range(2)])
                hTs = Ring([(sbuf(es, [128, 8, WT], BF16, "hT"), [Buf(), Buf()]) for _ in range(2)])
                uT = sbuf(es, [128, 8, 30 + WT], BF16, "uT")
                uTB = [Buf() for _ in range(8)]
                S.op(POOL, lambda e: e.memset(uT[:], 0.0), W=uTB)
                sgt = Ring([(sbuf(es, [128, WT], F32, "sgt"), Buf()) for _ in range(2)])
                yvs = Ring([(sbuf(es, [128, 8, WT], F32, "yv"), [Buf() for _ in range(8)]) for _ in range(2)])
                ybf = sbuf(es, [128, 8, WT], BF16, "ybf"); ybfB = Buf()
                ysq = sbuf(es, [128, 8, WT], BF16, "ysq"); ysqB = Buf()
                stt_ = sbuf(es, [128, 5, WT], F32, "lnst"); stB = Buf()
                actTs = Ring([(sbuf(es, [128, 8, WT], BF16, "actT"), [Buf() for _ in range(8)]) for _ in range(2)])
                ybos = Ring([(sbuf(es, [128, 8, WT], BF16, "ybo"), Buf()) for _ in range(2)])
                psv = Ring([(psum(es, [128, 2, WT], F32, "psv"), Buf()) for _ in range(2)])
                psc = Ring([(psum(es, [128, 2, WT], F32, "psc"), [Buf(), Buf()]) for _ in range(2)])
                psS = (psum(es, [128, 2, WT], F32, "psS"), Buf())
                psY = Ring([(psum(es, [128, 2, WT], F32, "psY"), Buf()) for _ in range(2)])
                ybv = YB.rearrange("c p t -> p c t")

                blk0 = Mb[l]
                while blk0 < NB:
                    nbk = min(2, NB - blk0)
                    W_ = 128 * nbk
                    hT, hTBs = hTs.next()
                    yv, yB = yvs.next()
                    actT, actB = actTs.next()
                    for bi in range(nbk):
                        blk = blk0 + bi
                        xt, xtB = xts.next()
                        S.dma(SP, xt[:], Xin[blk * 128:(blk + 1) * 128, :], W=[xtB])
                        norm.run(xt, xtB, blk, hT[:, :, bi * 128:(bi + 1) * 128], hTBs[bi], ACT)
                    hR = hTBs[:nbk]
                    for cp in range(4):
                        pc, pcBs = psc.next()
                        for ci in range(2):
                            c = 2 * cp + ci
                            pv, pvB = psv.next()
                            sg, sgB = sgt.next()
                            for half in range(2):
                                col0 = half * D + c * 128
                                for kk in range(8):
                                    S.op(PE, lambda e, kk=kk, pv=pv, half=half, col0=col0: e.matmul(
                                        pv[:, half, :W_], lhsT=wcv[:, kk, col0:col0 + 128], rhs=hT[:, kk, :W_],
                                        start=(kk == 0), stop=(kk == 7)), R=hR, W=[pvB], sig=(kk == 7 and half == 1))
                            S.op(ACT, lambda e, pv=pv, sg=sg: e.activation(out=sg[:, :W_], in_=pv[:, 1, :W_], func=AF.Sigmoid), R=[pvB], W=[sgB])
                            S.op(DVE, lambda e, pv=pv, sg=sg, c=c: e.tensor_tensor(out=uT[:, c, 30:30 + W_], in0=pv[:, 0, :W_], in1=sg[:, :W_], op=ALU.mult),
                                 R=[pvB, sgB], W=[uTB[c]])
                        for ci in range(2):
                            c = 2 * cp + ci
                            for tap in range(CONVK):
                                S.op(PE, lambda e, c=c, ci=ci, tap=tap, pc=pc: e.matmul(pc[:, ci, :W_], lhsT=dg[:, c, tap, :], rhs=uT[:, c, tap:tap + W_],
                                                                                        start=(tap == 0), stop=(tap == CONVK - 1)),
                                     R=[uTB[c]], W=[pcBs[0]], sig=(tap == CONVK - 1 and ci == 1))
                        for ci in range(2):
                            c = 2 * cp + ci
                            S.op(ACT, lambda e, c=c, ci=ci, pc=pc: e.activation(out=yv[:, c, :W_], in_=pc[:, ci, :W_], func=AF.Identity, bias=bdw[:, c:c + 1]),
                                 R=[pcBs[0], cB], W=[yB[c]])
                            S.op(POOL, lambda e, c=c: e.tensor_copy(out=uT[:, c, 0:30], in_=uT[:, c, W_:W_ + 30]), R=[uTB[c]], W=[uTB[c]])
                    S.op(POOL, lambda e: e.tensor_copy(out=ybf[:, :, :W_], in_=yv[:, :, :W_]), R=yB, W=[ybfB])
                    S.op(ACT, lambda e: e.activation(out=ysq[:, :, :W_], in_=yv[:, :, :W_], func=AF.Square), R=yB, W=[ysqB])
                    pS, pSB = psS
                    for c in range(8):
                        S.op(PE, lambda e, c=c: e.matmul(pS[:, 0, :W_], lhsT=ones_bf[:], rhs=ybf[:, c, :W_], start=(c == 0), stop=(c == 7)),
                             R=[ybfB], W=[pSB], sig=False)
                    for c in range(8):
                        S.op(PE, lambda e, c=c: e.matmul(pS[:, 1, :W_], lhsT=ones_bf[:], rhs=ysq[:, c, :W_], start=(c == 0), stop=(c == 7)),
                             R=[ysqB], W=[pSB], sig=(c == 7))
                    mean_, msq_, var_, lnv_, rstd_ = (stt_[:, i, :W_] for i in range(5))
                    S.op(ACT, lambda e: e.activation(out=mean_, in_=pS[:, 0, :W_], func=AF.Copy, scale=1.0 / D), R=[pSB], W=[stB])
                    S.op(ACT, lambda e: e.activation(out=msq_, in_=mean_, func=AF.Square), R=[stB], W=[stB])
                    S.op(DVE, lambda e: e.scalar_tensor_tensor(out=var_, in0=pS[:, 1, :W_], scalar=1.0 / D, in1=msq_, op0=ALU.mult, op1=ALU.subtract),
                         R=[pSB, stB], W=[stB])
                    S.op(DVE, lambda e: e.tensor_scalar(out=var_, in0=var_, scalar1=0.0, scalar2=None, op0=ALU.max), R=[stB], W=[stB])
                    S.op(ACT, lambda e: e.activation(out=lnv_, in_=var_, func=AF.Ln, bias=EPS), R=[stB], W=[stB])
                    S.op(ACT, lambda e: e.activation(out=rstd_, in_=lnv_, func=AF.Exp, scale=-0.5), R=[stB], W=[stB])
                    S.op(DVE, lambda e: e.tensor_tensor(out=yv[:, :, :W_], in0=yv[:, :, :W_],
                                                        in1=mean_.unsqueeze(1).broadcast_to([128, 8, W_]), op=ALU.subtract), R=yB + [stB], W=yB)
                    S.op(DVE, lambda e: e.tensor_tensor(out=yv[:, :, :W_], in0=yv[:, :, :W_],
                                                        in1=rstd_.unsqueeze(1).broadcast_to([128, 8, W_]), op=ALU.mult), R=yB + [stB], W=yB)
                    for c in range(8):
                        S.op(ACT, lambda e, c=c: e.activation(out=actT[:, c, :W_], in_=yv[:, c, :W_], func=AF.Silu,
                                                              scale=gln[:, c:c + 1], bias=bln[:, c:c + 1]), R=[yB[c], cB], W=[actB[c]])
                    ybo, yboB = ybos.next()
                    for op_ in range(4):
                        pY, pYB = psY.next()
                        for oi in range(2):
                            oc = 2 * op_ + oi
                            for kk in range(8):
                                S.op(PE, lambda e, kk=kk, oc=oc, oi=oi, pY=pY: e.matmul(pY[:, oi, :W_], lhsT=wco[:, kk, oc * 128:(oc + 1) * 128], rhs=actT[:, kk, :W_],
                                                                                        start=(kk == 0), stop=(kk == 7)), R=actB, W=[pYB],
                                     sig=(kk == 7 and oi == 1))
                        if op_ % 2 == 0:
                            S.op(DVE, lambda e, op_=op_, pY=pY, ybo=ybo: e.tensor_copy(out=ybo[:, 2 * op_:2 * op_ + 2, :W_], in_=pY[:, :, :W_]), R=[pYB], W=[yboB])
                        else:
                            S.op(ACT, lambda e, op_=op_, pY=pY, ybo=ybo: e.copy(out=ybo[:, 2 * op_:2 * op_ + 2, :W_], in_=pY[:, :, :W_]), R=[pYB], W=[yboB])
                    for c in range(8):
                        S.dma(POOL, YB[c, :, blk0 * 128:blk0 * 128 + W_], ybo[:, c, :W_], R=[yboB])
                    blk0 += nbk
                S.barrier()

        def phase3b(l):
            Xin = xw if l == 0 else XA
            WT = 256
            with contextlib.ExitStack() as es:
                wg = sbuf(es, [128, 8, 2048], BF16, "wg")
                wap = sbuf(es, [128, 4, D], BF16, "wap")
                wo = sbuf(es, [128, 8, D], BF16, "wo")
                with contextlib.ExitStack() as es2:
                    rot = Ring([DVE, POOL, ACT])
                    load_weight(es2, wg, w_in[l][:, 3 * AW + 2048:INW], 8, 2048, rot)
                    load_weight(es2, wap, w_attn_proj[l], 4, D, rot)
                    load_weight(es2, wo, w_o[l], 8, D, rot)
                    S.barrier()
                cB = Buf()
                Gm = sbuf(es, [128, D], F32, "Gm")
                SHb = sbuf(es, [128, D], F32, "SHb")
                gtb = sbuf(es, [128, D], F32, "gtb")
                g1b = sbuf(es, [128, D], F32, "g1b")
                load_bcast(es, g1b, g_norm1[l], cB)
                S.dma(SP, Gm[:], MOD[l][:, D:2 * D], W=[cB])
                S.dma(SP, SHb[:], MOD[l][:, 0:D], W=[cB])
                S.dma(SP, gtb[:], MOD[l][:, 2 * D:3 * D], W=[cB])
                S.op(DVE, lambda e: e.scalar_tensor_tensor(out=Gm[:], in0=Gm[:], scalar=1.0, in1=g1b[:], op0=ALU.add, op1=ALU.mult),
                     R=[cB], W=[cB])
                norm = NormCtx(es, Gm, SHb, cB)
                xts = Ring([(sbuf(es, [128, D], F32, "xt"), Buf()) for _ in range(4)])
                xms = Ring([(sbuf(es, [128, D], F32, "xm"), Buf()) for _ in range(2)])
                hTs = Ring([(sbuf(es, [128, 8, WT], BF16, "hT"), [Buf(), Buf()]) for _ in range(2)])
                Ars = Ring([[(sbuf(es, [128, 516], F32, "A"), Buf()) for _ in range(3)] for _ in range(2)])
                abfs = Ring([(sbuf(es, [128, 512], BF16, "abf"), Buf()) for _ in range(2)])
                asts = Ring([(sbuf(es, [128, 8], F32, "ast"), Buf()) for _ in range(2)])
                attnTs = Ring([(sbuf(es, [128, 4, WT], BF16, "attnT"), [Buf(), Buf()]) for _ in range(2)])
                ybTs = Ring([(sbuf(es, [128, 8, WT], BF16, "ybT"), Buf()) for _ in range(2)])
                tpa = Ring([(psum(es, [128, 4, 128], BF16, "tpa"), Buf()) for _ in range(1)])
                sg2 = Ring([(sbuf(es, [128, 2, WT], F32, "sg2"), Buf()) for _ in range(2)])
                m1 = Ring([(sbuf(es, [128, 2, WT], F32, "m1"), Buf()) for _ in range(2)])
                mT = sbuf(es, [128, 8, WT], BF16, "mT"); mTB = [Buf() for _ in range(8)]
                psA = Ring([(psum(es, [128, 2, WT], F32, "psA"), Buf()) for _ in range(2)])
                psB_ = Ring([(psum(es, [128, WT], F32, "psB"), Buf()) for _ in range(2)])
                pso = Ring([(psum(es, [128, 512], F32, "pso"), Buf()) for _ in range(2)])
                ybv = YB.rearrange("c p t -> p c t")

                blk0 = Mb[l]
                while blk0 < NB:
                    nbk = min(2, NB - blk0)
                    W_ = 128 * nbk
                    hT, hTBs = hTs.next()
                    attnT, attnTBs = attnTs.next()
                    ybT, ybTB = ybTs.next()
                    for c in range(8):
                        S.dma(SP, ybT[:, c, :W_], YB[c, :, blk0 * 128:blk0 * 128 + W_], W=[ybTB])
                    xl = []
                    for bi in range(nbk):
                        blk = blk0 + bi
                        xt, xtB = xts.next()
                        xl.append((xt, xtB))
                        S.dma(SP, xt[:], Xin[blk * 128:(blk + 1) * 128, :], W=[xtB])
                        norm.run(xt, xtB, blk, hT[:, :, bi * 128:(bi + 1) * 128], hTBs[bi], ACT)
                        As = Ars.next()
                        abf, abfB = abfs.next()
                        ast, astB = asts.next()
                        for g in range(3):
                            S.dma(SP, As[g][0][:], ATT[g][blk * 128:(blk + 1) * 128, :], W=[As[g][1]])
                        A0, A1, A2 = As[0][0], As[1][0], As[2][0]
                        S.op(POOL, lambda e, A0=A0, A1=A1: e.tensor_tensor(out=A0[:], in0=A0[:], in1=A1[:], op=ALU.add), R=[As[1][1]], W=[As[0][1]])
                        S.op(POOL, lambda e, A0=A0, A2=A2: e.tensor_tensor(out=A0[:], in0=A0[:], in1=A2[:], op=ALU.add), R=[As[2][1]], W=[As[0][1]])
                        A3 = A0[:].rearrange("p (h d) -> p h d", h=4)
                        S.op(DVE, lambda e, ast=ast, A3=A3: e.tensor_scalar(out=ast[:, 0:4], in0=A3[:, :, 128], scalar1=1e-30, scalar2=None, op0=ALU.max),
                             R=[As[0][1]], W=[astB])
                        S.op(DVE, lambda e, ast=ast: e.reciprocal(out=ast[:, 4:8], in_=ast[:, 0:4]), R=[astB], W=[astB])
                        for j in range(4):
                            S.op(ACT, lambda e, j=j, abf=abf, A3=A3, ast=ast: e.activation(out=abf[:, j * 128:(j + 1) * 128], in_=A3[:, j, 0:128],
                                                                                         func=AF.Identity, scale=ast[:, 4 + j:5 + j]),
                                 R=[As[0][1], astB], W=[abfB])
                        tp_, tpB_ = tpa.next()
                        for j in range(4):
                            S.op(PE, lambda e, j=j, abf=abf, tp_=tp_: e.transpose(out=tp_[:, j, :], in_=abf[:, j * 128:(j + 1) * 128], identity=ident[:]),
                                 R=[abfB], W=[tpB_], sig=(j == 3))
                        S.op(DVE, lambda e, bi=bi, tp_=tp_: e.tensor_copy(out=attnT[:, :, bi * 128:(bi + 1) * 128], in_=tp_[:, :, :]),
                             R=[tpB_], W=[attnTBs[bi]])
                    hR = hTBs[:nbk]
                    aR = attnTBs[:nbk]
                    for oc in range(8):
                        pA, pAB = psA.next()
                        pB, pBB = psB_.next()
                        s2, s2B = sg2.next()
                        mm1, m1B = m1.next()
                        for gi in range(2):
                            col0 = gi * D + oc * 128
                            for kk in range(8):
                                S.op(PE, lambda e, kk=kk, gi=gi, col0=col0, pA=pA: e.matmul(pA[:, gi, :W_], lhsT=wg[:, kk, col0:col0 + 128], rhs=hT[:, kk, :W_],
                                                                                            start=(kk == 0), stop=(kk == 7)), R=hR, W=[pAB],
                                     sig=(kk == 7 and gi == 1))
                        for j in range(4):
                            S.op(PE, lambda e, j=j, pB=pB, oc=oc: e.matmul(pB[:, :W_], lhsT=wap[:, j, oc * 128:(oc + 1) * 128], rhs=attnT[:, j, :W_],
                                                                           start=(j == 0), stop=(j == 3)), R=aR, W=[pBB], sig=(j == 3))
                        S.op(ACT, lambda e, s2=s2, pA=pA: e.activation(out=s2[:, :, :W_], in_=pA[:, :, :W_], func=AF.Sigmoid), R=[pAB], W=[s2B])
                        S.op(DVE, lambda e, s2=s2, mm1=mm1, pB=pB: e.tensor_tensor(out=mm1[:, 0, :W_], in0=pB[:, :W_], in1=s2[:, 0, :W_], op=ALU.mult),
                             R=[pBB, s2B], W=[m1B])
                        S.op(DVE, lambda e, s2=s2, mm1=mm1, oc=oc: e.tensor_tensor(out=mm1[:, 1, :W_], in0=s2[:, 1, :W_], in1=ybT[:, oc, :W_], op=ALU.mult),
                             R=[s2B, ybTB], W=[m1B])
                        S.op(DVE, lambda e, mm1=mm1, oc=oc: e.tensor_tensor(out=mT[:, oc, :W_], in0=mm1[:, 0, :W_], in1=mm1[:, 1, :W_], op=ALU.add),
                             R=[m1B], W=[mTB[oc]])
                    for bi in range(nbk):
                        blk = blk0 + bi
                        xt, xtB = xl[bi]
                        xm, xmB = xms.next()
                        for hf in range(2):
                            po, poB = pso.next()
                            for kk in range(8):
                                S.op(PE, lambda e, kk=kk, po=po, bi=bi, hf=hf: e.matmul(po[:], lhsT=mT[:, kk, bi * 128:(bi + 1) * 128],
                                                                                        rhs=wo[:, kk, hf * 512:(hf + 1) * 512],
                                                                                        start=(kk == 0), stop=(kk == 7)), R=mTB, W=[poB], sig=(kk == 7))
                            S.op(DVE, lambda e, po=po, hf=hf, xm=xm: e.tensor_tensor(out=xm[:, hf * 512:(hf + 1) * 512], in0=po[:],
                                                                                     in1=gtb[:, hf * 512:(hf + 1) * 512], op=ALU.mult), R=[poB, cB], W=[xmB])
                        S.op(DVE, lambda e, xt=xt, xm=xm: e.tensor_tensor(out=xm[:], in0=xm[:], in1=xt[:], op=ALU.add), R=[xtB, xmB], W=[xmB])
                        S.dma(POOL, XM[blk * 128:(blk + 1) * 128, :], xm[:], R=[xmB])
                    blk0 += nbk
                S.barrier()

        def phase4(l):
            WT = 256
            last = (l == L - 1)
            with contextlib.ExitStack() as es:
                wfi = sbuf(es, [128, 8, 2 * DFF], BF16, "wfi")
                wfd = sbuf(es, [128, 22, D], BF16, "wfd")
                with contextlib.ExitStack() as es2:
                    rot = Ring([DVE, POOL, ACT])
                    load_weight(es2, wfi, w_ffn_in[l], 8, 2 * DFF, rot)
                    load_weight(es2, wfd, w_ffn_down[l], 22, D, rot)
                    S.barrier()
                cB = Buf()
                Gm = sbuf(es, [128, D], F32, "Gm")
                SHb = sbuf(es, [128, D], F32, "SHb")
                gtb = sbuf(es, [128, D], F32, "gtb")
                wf3 = sbuf(es, [128, 22, 3], F32, "wf3")
                bfv = sbuf(es, [128, 22], F32, "bfv")
                xo = sbuf(es, [128, D], F32, "xo")
                load_bcast(es, xo, g_norm2[l], cB)
                S.dma(SP, Gm[:], MOD[l][:, 4 * D:5 * D], W=[cB])
                S.dma(SP, SHb[:], MOD[l][:, 3 * D:4 * D], W=[cB])
                S.dma(SP, gtb[:], MOD[l][:, 5 * D:6 * D], W=[cB])
                S.dma(SP, wf3[:], wf3_t[l], W=[cB])
                S.dma(SP, bfv[:], bf_t[l], W=[cB])
                S.op(DVE, lambda e: e.scalar_tensor_tensor(out=Gm[:], in0=Gm[:], scalar=1.0, in1=xo[:], op0=ALU.add, op1=ALU.mult),
                     R=[cB], W=[cB])
                xoB = Buf()
                xoB.r.append(cB.w)
                norm = NormCtx(es, Gm, SHb, cB)
                xts = Ring([(sbuf(es, [128, D], F32, "xt"), Buf()) for _ in range(2)])
                hTs = Ring([(sbuf(es, [128, 8, WT], BF16, "hT"), [Buf(), Buf()]) for _ in range(2)])
                carry = sbuf(es, [128, 22, 2], F32, "carry"); carB = [Buf() for _ in range(22)]
                S.op(POOL, lambda e: e.memset(carry[:], 0.0), W=carB)
                gbs = Ring([(sbuf(es, [128, 2 + WT], F32, "gb"), Buf()) for _ in range(4)])
                accs = Ring([(sbuf(es, [128, WT], F32, "acc"), Buf()) for _ in range(4)])
                sils = Ring([(sbuf(es, [128, WT], F32, "sil"), Buf()) for _ in range(2)])
                actT = sbuf(es, [128, 22, WT], BF16, "actT"); actB = [Buf() for _ in range(22)]
                psg = Ring([(psum(es, [128, 2, WT], F32, "psg"), Buf()) for _ in range(4)])
                pso = Ring([(psum(es, [128, 512], F32, "pso"), Buf()) for _ in range(2)])
                blk0 = Mb[l]
                while blk0 < NB:
                    nbk = min(2, NB - blk0)
                    W_ = 128 * nbk
                    hT, hTBs = hTs.next()
                    xl = []
                    for bi in range(nbk):
                        blk = blk0 + bi
                        xt, xtB = xts.next()
                        xl.append((xt, xtB))
                        S.dma(SP, xt[:], XM[blk * 128:(blk + 1) * 128, :], W=[xtB])
                        norm.run(xt, xtB, blk, hT[:, :, bi * 128:(bi + 1) * 128], hTBs[bi], ACT)
                    hR = hTBs[:nbk]
                    for fp in range(11):
                        items = []
                        for fi in range(2):
                            f = 2 * fp + fi
                            pg, pgB = psg.next()
                            gb, gbB = gbs.next()
                            acc, accB = accs.next()
                            for half in range(2):
                                col0 = half * DFF + f * 128
                                for kk in range(8):
                                    S.op(PE, lambda e, kk=kk, pg=pg, half=half, col0=col0: e.matmul(
                                        pg[:, half, :W_], lhsT=wfi[:, kk, col0:col0 + 128], rhs=hT[:, kk, :W_],
                                        start=(kk == 0), stop=(kk == 7)), R=hR, W=[pgB], sig=(kk == 7 and half == 1))
                            S.op(POOL, lambda e, gb=gb, f=f: e.tensor_copy(out=gb[:, 0:2], in_=carry[:, f, :]), R=[carB[f]], W=[gbB])
                            S.op(ACT, lambda e, gb=gb, pg=pg: e.copy(out=gb[:, 2:2 + W_], in_=pg[:, 0, :W_]), R=[pgB], W=[gbB])
                            S.op(POOL, lambda e, gb=gb, f=f: e.tensor_copy(out=carry[:, f, :], in_=gb[:, W_:W_ + 2]), R=[gbB], W=[carB[f]])
                            items.append((f, pg, pgB, gb, gbB, acc, accB))
                        for (f, pg, pgB, gb, gbB, acc, accB) in items:
                            S.op(DVE, lambda e, f=f, gb=gb, acc=acc: e.tensor_scalar(out=acc[:, :W_], in0=gb[:, 0:W_], scalar1=wf3[:, f, 0:1],
                                                                                     scalar2=bfv[:, f:f + 1], op0=ALU.mult, op1=ALU.add),
                                 R=[gbB, cB], W=[accB])
                        for tap in (1, 2):
                            for (f, pg, pgB, gb, gbB, acc, accB) in items:
                                S.op(DVE, lambda e, f=f, gb=gb, acc=acc, tap=tap: e.scalar_tensor_tensor(
                                    out=acc[:, :W_], in0=gb[:, tap:tap + W_], scalar=wf3[:, f, tap:tap + 1], in1=acc[:, :W_],
                                    op0=ALU.mult, op1=ALU.add), R=[gbB, accB], W=[accB])
                        for (f, pg, pgB, gb, gbB, acc, accB) in items:
                            sl, slB = sils.next()
                            S.op(ACT, lambda e, sl=sl, acc=acc: e.activation(out=sl[:, :W_], in_=acc[:, :W_], func=AF.Silu), R=[accB], W=[slB])
                            S.op(DVE, lambda e, sl=sl, pg=pg, f=f: e.tensor_tensor(out=actT[:, f, :W_], in0=pg[:, 1, :W_], in1=sl[:, :W_], op=ALU.mult),
                                 R=[pgB, slB], W=[actB[f]])
                    for bi in range(nbk):
                        blk = blk0 + bi
                        xt, xtB = xl[bi]
                        for hf in range(2):
                            po, poB = pso.next()
                            for f in range(22):
                                S.op(PE, lambda e, f=f, po=po, bi=bi, hf=hf: e.matmul(po[:], lhsT=actT[:, f, bi * 128:(bi + 1) * 128],
                                                                                      rhs=wfd[:, f, hf * 512:(hf + 1) * 512],
                                                                                      start=(f == 0), stop=(f == 21)), R=actB, W=[poB], sig=(f == 21))
                            S.op(DVE, lambda e, po=po, hf=hf: e.tensor_tensor(out=xo[:, hf * 512:(hf + 1) * 512], in0=po[:],
                                                                              in1=gtb[:, hf * 512:(hf + 1) * 512], op=ALU.mult), R=[poB, cB], W=[xoB])
                        S.op(POOL, lambda e, xt=xt: e.tensor_tensor(out=xo[:], in0=xo[:], in1=xt[:], op=ALU.add), R=[xtB, xoB], W=[xoB])
                        if last:
                            if blk >= OWN0:
                                S.dma(POOL, y_out[(blk - OWN0) * 128:(blk - OWN0 + 1) * 128, :], xo[:], R=[xoB])
                        else:
                            S.dma(POOL, XA[blk * 128:(blk + 1) * 128, :], xo[:], R=[xoB])
                    blk0 += nbk
                S.barrier()

        S.barrier()
        prologue()
        for l in range(L):
            for pi, ph in enumerate((phase1, phase2, phase3a, phase3b, phase4)):
                if pi < stop_after:
                    ph(l)
        S.barrier()
    return nc, S.n_ins


def make_in_maps(inputs, L=4, cores=range(8)):
    Kb, Mb, OWN0, NB = geometry(L)
    NTOK = NB * 128
    kcols = keyset_columns(L)
    x = np.asarray(inputs["x"], dtype=np.float32)
    c = np.asarray(inputs["c"], dtype=np.float32)
    positions = np.asarray(inputs["positions"], dtype=np.int32)
    f32 = lambda k: np.ascontiguousarray(np.asarray(inputs[k], dtype=np.float32)[:L])
    shared = {k: f32(k) for k in ("w_ada", "b_ada", "g_norm1", "w_in", "g_q", "g_k", "w_attn_proj", "w_conv_out",
                                  "w_o", "g_norm2", "w_ffn_in", "w_ffn_down")}
    wdw = f32("w_conv_dw")
    shared["wdw_t"] = np.ascontiguousarray(wdw.reshape(L, CONVK, 8, 128).transpose(0, 3, 2, 1))
    per_ch = lambda a, n: np.ascontiguousarray(a.reshape(L, n, 128).transpose(0, 2, 1))
    shared["bdw_t"] = per_ch(f32("b_conv_dw"), 8)
    shared["gln_t"] = per_ch(f32("g_conv_ln"), 8)
    shared["bln_t"] = per_ch(f32("b_conv_ln"), 8)
    wf = f32("w_ffn_dw")
    shared["wf3_t"] = np.ascontiguousarray(wf.reshape(L, 3, 22, 128).transpose(0, 3, 2, 1))
    shared["bf_t"] = per_ch(f32("b_ffn_dw"), 22)
    inv = (np.float32(500000.0) ** (-np.arange(0, 32, 2, dtype=np.float32) / np.float32(32))).astype(np.float32)
    shared["invf"] = np.ascontiguousarray(np.broadcast_to(inv[None, :], (128, 16))).astype(np.float32)
    shared["ident"] = np.eye(128, dtype=np.float32)
    kk, qq = np.meshgrid(np.arange(128), np.arange(128), indexing="ij")
    shared["maskp"] = np.where(kk >= qq, 0.0, NEG).astype(np.float32)
    shared["maskc"] = np.where(kk <= qq, 0.0, NEG).astype(np.float32)
    maps = []
    for core in cores:
        b, j = core // 4, core % 4
        off = 4096 * j - OWN0 * 128
        gpos = off + np.arange(NTOK)
        valid = gpos >= 0
        gsafe = np.clip(gpos, 0, SEQ - 1)
        xw = np.where(valid[:, None], x[b, gsafe, :], np.float32(0.0)).astype(np.float32)
        pw = np.where(valid, positions[b, gsafe], 0).astype(np.int32)
        vb = valid.reshape(NB, 128)[:, 0].astype(np.float32)
        kbt = np.zeros((128, len(kcols)), dtype=np.float32)
        for (d, start), col in kcols.items():
            toks = np.clip(start + d * np.arange(128), 0, NTOK - 1)
            kbt[:, col] = np.where(valid[toks], 0.0, NEG)
        m = dict(shared)
        m["xw"] = np.ascontiguousarray(xw)
        m["pos_t"] = np.ascontiguousarray(pw.reshape(NB, 128).T)
        m["vcol"] = np.ascontiguousarray(np.broadcast_to(vb[None, :], (128, NB))).astype(np.float32)
        m["kbias"] = kbt
        m["c_t"] = np.ascontiguousarray(c[b].reshape(8, 128).T)
        maps.append(m)
    return maps


_CACHE = {}


def kernel(**inputs):
    L = 4
    if L not in _CACHE:
        _CACHE[L] = build_program(L)[0]
    nc = _CACHE[L]
    maps = make_in_maps(inputs, L)
    res = run_bass_kernel_spmd(nc, maps, core_ids=list(range(8)))
    out = np.empty((2, SEQ, D), dtype=np.float32)
    for core in range(8):
        b, j = core // 4, core % 4
        out[b, 4096 * j:4096 * (j + 1), :] = res.results[core]["y"]
    return out
```

```python
import contextlib
import os
import numpy as np
import ml_dtypes

import concourse.bass as bass
import concourse.mybir as mybir
from concourse.bass_utils import run_bass_kernel_spmd

F32 = mybir.dt.float32
BF16 = mybir.dt.bfloat16
I32 = mybir.dt.int32
AF = mybir.ActivationFunctionType
ALU = mybir.AluOpType
AX = mybir.AxisListType

D = 1024
NHEAD = 12
DH = 128
AW = 1536
DFF = 2816
INW = 8704
CONVK = 31
EPS = 1e-6
NEG = -30000.0
SCALE = DH ** -0.5
OWN_BLK = 32
SEQ = 16384
DILS = (1, 4, 16)
TWO_PI = 6.283185307179586
C1 = 6.28125
C2 = TWO_PI - C1


def geometry(L):
    Kb = [17 * i for i in range(L)]
    Mb = [k + 16 for k in Kb]
    own0 = 17 * L
    nb = own0 + OWN_BLK
    return Kb, Mb, own0, nb


def attn_units(L):
    Kb, Mb, own0, nb = geometry(L)
    out = []
    for l in range(L):
        q0, q1 = Mb[l] * 128, nb * 128
        lst = []
        for g, d in enumerate(DILS):
            span = 128 * d
            c0 = q0
            while c0 < q1:
                n = min(span, q1 - c0)
                for r in range(d):
                    lst.append((g, d, c0 + r, n // d))
                c0 += span
        out.append(lst)
    return out


def keyset_columns(L):
    cols = {}
    for lst in attn_units(L):
        for (g, d, base, nq) in lst:
            for ks in ((d, base - 128 * d), (d, base)):
                if ks not in cols:
                    cols[ks] = len(cols)
    return cols


class Tok:
    __slots__ = ("eng", "sem", "val")

    def __init__(self, eng, sem, val):
        self.eng, self.sem, self.val = eng, sem, val


class Buf:
    __slots__ = ("name", "w", "r")

    def __init__(self, name=""):
        self.name, self.w, self.r = name, None, []


class Eng:
    def __init__(self, name, e, sem, self_sync):
        self.name, self.e, self.sem, self.self_sync = name, e, sem, self_sync
        self.count = 0
        self.waited = {}
        self.pending = []
        self.slots = []
        self.slot_i = 0


class Sched:
    def __init__(self, nc, es, ndma=14):
        self.nc = nc

        def sem(n):
            return es.enter_context(nc.semaphore(n))

        self.PE = Eng("pe", nc.tensor, sem("s_pe"), False)
        self.ACT = Eng("act", nc.scalar, sem("s_act"), True)
        self.DVE = Eng("dve", nc.vector, sem("s_dve"), True)
        self.POOL = Eng("pool", nc.gpsimd, sem("s_pool"), True)
        self.SP = Eng("sp", nc.sync, sem("s_sp"), True)
        self.engs = [self.PE, self.ACT, self.DVE, self.POOL, self.SP]
        for q in (self.SP, self.POOL, self.ACT):
            nq_ = int(os.environ.get("POOLDMA", "14")) if q is self.POOL else ndma
            q.slots = [[sem("d_%s%d" % (q.name, i)), 0] for i in range(nq_)]
        self.n_ins = 0

    def _wait(self, E, sem, val):
        key = id(sem)
        if E.waited.get(key, 0) < val:
            E.e.wait_ge(sem, val)
            E.waited[key] = val
            self.n_ins += 1

    def _deps(self, E, R, W):
        toks = []
        for b in R:
            if b.w is not None:
                toks.append(b.w)
        for b in W:
            if b.w is not None:
                toks.append(b.w)
            toks.extend(b.r)
        for t in toks:
            if t.eng is E and not E.self_sync:
                continue
            if t.val is None:
                if t.eng is E:
                    continue
                raise RuntimeError("dependency on unsignalled instruction (%s)" % t.eng.name)
            self._wait(E, t.sem, t.val)

    def op(self, E, fn, R=(), W=(), sig=True):
        self._deps(E, R, W)
        ins = fn(E.e)
        self.n_ins += 1
        tok = Tok(E, E.sem, None)
        E.pending.append(tok)
        if sig:
            E.count += 1
            ins.then_inc(E.sem, 1)
            for t in E.pending:
                t.val = E.count
            E.pending = []
        for b in W:
            b.w = tok
            b.r = []
        for b in R:
            b.r = [t for t in b.r if t.eng is not E or t.eng is None]
            b.r.append(tok)
        return ins

    def dma(self, Q, out, in_, R=(), W=()):
        self._deps(Q, R, W)
        slot = Q.slots[Q.slot_i % len(Q.slots)]
        Q.slot_i += 1
        if slot[1] > 0:
            self._wait(Q, slot[0], slot[1])
        slot[1] += 16
        Q.e.dma_start(out=out, in_=in_).then_inc(slot[0], 16)
        self.n_ins += 1
        tok = Tok(None, slot[0], slot[1])
        for b in W:
            b.w = tok
            b.r = []
        for b in R:
            b.r.append(tok)

    def barrier(self):
        for E in self.engs:
            if E.pending:
                raise RuntimeError("pending unsignalled instructions at barrier on " + E.name)
        for E in self.engs:
            for F in self.engs:
                if F.count > 0 and (F is not E or E.self_sync):
                    self._wait(E, F.sem, F.count)
            for Q in (self.SP, self.POOL, self.ACT):
                for s in Q.slots:
                    if s[1] > 0:
                        self._wait(E, s[0], s[1])


class Ring:
    def __init__(self, items):
        self.items = items
        self.i = 0

    def next(self):
        it = self.items[self.i % len(self.items)]
        self.i += 1
        return it


def build_program(L=4, debug=False, stop_after=99, maxblk=None):
    Kb, Mb, OWN0, NB = geometry(L)
    NTOK = NB * 128
    units = attn_units(L)
    kcols = keyset_columns(L)
    NKS = len(kcols)

    nc = bass.Bass("TRN2", target_bir_lowering=False)

    def din(name, shape, dt=F32):
        return nc.dram_tensor(name, list(shape), dt, kind="ExternalInput").ap()

    dbg_kind = "ExternalOutput" if debug else "Internal"

    def dscr(name, shape, dt=F32):
        return nc.dram_tensor(name, list(shape), dt, kind=dbg_kind).ap()

    xw = din("xw", [NTOK, D])
    pos_t = din("pos_t", [128, NB], I32)
    vcol_d = din("vcol", [128, NB])
    kb_d = din("kbias", [128, NKS])
    c_t = din("c_t", [128, 8])
    invf_d = din("invf", [128, 16])
    ident_d = din("ident", [128, 128])
    maskp_d = din("maskp", [128, 128])
    maskc_d = din("maskc", [128, 128])
    w_ada = din("w_ada", [L, D, 6 * D])
    b_ada = din("b_ada", [L, 6 * D])
    g_norm1 = din("g_norm1", [L, D])
    w_in = din("w_in", [L, D, INW])
    g_q = din("g_q", [L, DH])
    g_k = din("g_k", [L, DH])
    w_attn_proj = din("w_attn_proj", [L, 512, D])
    wdw_t = din("wdw_t", [L, 128, 8, CONVK])
    bdw_t = din("bdw_t", [L, 128, 8])
    gln_t = din("gln_t", [L, 128, 8])
    bln_t = din("bln_t", [L, 128, 8])
    w_conv_out = din("w_conv_out", [L, D, D])
    w_o = din("w_o", [L, D, D])
    g_norm2 = din("g_norm2", [L, D])
    w_ffn_in = din("w_ffn_in", [L, D, 2 * DFF])
    wf3_t = din("wf3_t", [L, 128, 22, 3])
    bf_t = din("bf_t", [L, 128, 22])
    w_ffn_down = din("w_ffn_down", [L, DFF, D])
    y_out = nc.dram_tensor("y", [OWN_BLK * 128, D], F32, kind="ExternalOutput").ap()

    QS = dscr("QS", [NTOK, AW], BF16)
    KS = dscr("KS", [NTOK, AW], BF16)
    XA = dscr("XA", [NTOK, D])
    XM = dscr("XM", [NTOK, D])
    VS = dscr("VS", [NTOK, NHEAD * 129], BF16)
    ATT = [dscr("ATT%d" % g, [NTOK, 516]) for g in range(3)]
    MOD = dscr("MOD", [L, 128, 6 * D])
    ROPE = dscr("ROPE", [2, 128, NB * 16])
    YB = dscr("YB", [8, 128, NTOK], BF16)

    uid = [0]

    def nm(p):
        uid[0] += 1
        return "%s_%d" % (p, uid[0])

    with contextlib.ExitStack() as top:
        S = Sched(nc, top)
        PE, ACT, DVE, POOL, SP = S.PE, S.ACT, S.DVE, S.POOL, S.SP

        def sbuf(es, shape, dt, p="t"):
            return es.enter_context(nc.sbuf_tensor(nm(p), list(shape), dt))

        def psum(es, shape, dt, p="ps"):
            return es.enter_context(nc.psum_tensor(nm(p), list(shape), dt))

        ident = sbuf(top, [128, 128], BF16, "ident")
        maskP = sbuf(top, [128, 128], BF16, "maskp")
        maskC = sbuf(top, [128, 128], BF16, "maskc")
        ones_bf = sbuf(top, [128, 128], BF16, "ones")
        vcol = sbuf(top, [128, NB], F32, "vcol")
        kb = sbuf(top, [128, NKS], F32, "kb")
        gB = Buf("glob")
        S.dma(POOL, ident[:], ident_d, W=[gB])
        S.dma(POOL, maskP[:], maskp_d, W=[gB])
        S.dma(POOL, maskC[:], maskc_d, W=[gB])
        S.dma(SP, vcol[:], vcol_d, W=[gB])
        S.dma(SP, kb[:], kb_d, W=[gB])
        S.op(DVE, lambda e: e.memset(ones_bf[:], 1.0), W=[Buf()])

        def load_weight(es_stage, dst, src, kc, ncol, rot):
            PIECE = 2048
            stg = [(sbuf(es_stage, [128, PIECE], F32, "stg"), Buf()) for _ in range(3)]
            ring = Ring(stg)
            for kk in range(kc):
                for c0 in range(0, ncol, PIECE):
                    cw = min(PIECE, ncol - c0)
                    st, sb_ = ring.next()
                    S.dma(SP, st[:, :cw], src[kk * 128:(kk + 1) * 128, c0:c0 + cw], W=[sb_])
                    E = rot.next()
                    if E is ACT:
                        S.op(E, lambda e, st=st, kk=kk, c0=c0, cw=cw: e.copy(out=dst[:, kk, c0:c0 + cw], in_=st[:, :cw]),
                             R=[sb_], W=[Buf()])
                    else:
                        S.op(E, lambda e, st=st, kk=kk, c0=c0, cw=cw: e.tensor_copy(out=dst[:, kk, c0:c0 + cw], in_=st[:, :cw]),
                             R=[sb_], W=[Buf()])

        def rsqrt_small(dst, src, scale, srcB, dstB, tmp, tmpB):
            S.op(ACT, lambda e: e.activation(out=tmp, in_=src, func=AF.Ln, scale=scale, bias=EPS), R=[srcB], W=[tmpB])
            S.op(ACT, lambda e: e.activation(out=dst, in_=tmp, func=AF.Exp, scale=-0.5), R=[tmpB], W=[dstB])

        class NormCtx:
            def __init__(self, es, Gm, SHb, constB):
                self.Gm, self.SHb, self.constB = Gm, SHb, constB
                self.junk = sbuf(es, [128, D], BF16, "junk")
                self.junkB = Buf()
                self.tmp = Ring([(sbuf(es, [128, D], F32, "ntmp"), Buf()) for _ in range(1)])
                self.h = Ring([(sbuf(es, [128, D], BF16, "h"), Buf()) for _ in range(2)])
                self.st = Ring([(sbuf(es, [128, 4], F32, "nst"), Buf()) for _ in range(2)])
                self.tp = Ring([(psum(es, [128, 8, 128], BF16, "tp"), Buf()) for _ in range(1)])

            def run(self, xt, xtB, blk, hT_dst, hTB, evac_eng):
                junk, junkB = self.junk, self.junkB
                tmp, tmpB = self.tmp.next()
                h, hB = self.h.next()
                st, stB = self.st.next()
                tp, tpB = self.tp.next()
                S.op(ACT, lambda e: e.activation(out=junk[:], in_=xt[:], func=AF.Square, accum_out=st[:, 0:1]),
                     R=[xtB], W=[junkB, stB])
                S.op(ACT, lambda e: e.activation(out=st[:, 1:2], in_=st[:, 0:1], func=AF.Ln, scale=1.0 / D, bias=EPS),
                     R=[stB], W=[stB])
                S.op(ACT, lambda e: e.activation(out=st[:, 2:3], in_=st[:, 1:2], func=AF.Exp, scale=-0.5),
                     R=[stB], W=[stB])
                S.op(DVE, lambda e: e.tensor_tensor(out=st[:, 3:4], in0=st[:, 2:3], in1=vcol[:, blk:blk + 1], op=ALU.mult),
                     R=[stB], W=[stB])
                S.op(DVE, lambda e: e.scalar_tensor_tensor(out=tmp[:], in0=xt[:], scalar=st[:, 3:4], in1=self.Gm[:],
                                                           op0=ALU.mult, op1=ALU.mult),
                     R=[xtB, stB, self.constB], W=[tmpB])
                S.op(DVE, lambda e: e.scalar_tensor_tensor(out=h[:], in0=self.SHb[:], scalar=vcol[:, blk:blk + 1], in1=tmp[:],
                                                           op0=ALU.mult, op1=ALU.add),
                     R=[tmpB, self.constB], W=[hB])
                for kk in range(8):
                    S.op(PE, lambda e, kk=kk: e.transpose(out=tp[:, kk, :], in_=h[:, kk * 128:(kk + 1) * 128], identity=ident[:]),
                         R=[hB], W=[tpB], sig=(kk == 7))
                if evac_eng is ACT:
                    S.op(ACT, lambda e: e.copy(out=hT_dst, in_=tp[:]), R=[tpB], W=[hTB])
                else:
                    S.op(evac_eng, lambda e: e.tensor_copy(out=hT_dst, in_=tp[:]), R=[tpB], W=[hTB])

        def load_bcast(es, dst, src_row, B):
            S.dma(SP, dst[:], src_row.partition_broadcast(128), W=[B])

        def prologue():
            with contextlib.ExitStack() as es:
                NE = NB * 16
                pi = sbuf(es, [128, NB], I32)
                pf = sbuf(es, [128, NB], F32)
                invf = sbuf(es, [128, 16], F32)
                ang = sbuf(es, [128, NB, 16], F32)
                kf = sbuf(es, [128, NB, 16], F32)
                ki = sbuf(es, [128, NB, 16], I32)
                r = sbuf(es, [128, NB, 16], F32)
                m = sbuf(es, [128, NB, 16], F32)
                r2 = sbuf(es, [128, NB, 16], F32)
                sn = sbuf(es, [128, NB, 16], F32)
                cs = sbuf(es, [128, NB, 16], F32)
                B = Buf()
                S.dma(SP, pi[:], pos_t, W=[B])
                S.dma(SP, invf[:], invf_d, W=[B])
                V = lambda fn, R=(B,), W=(B,): S.op(DVE, fn, R=list(R), W=list(W))
                V(lambda e: e.tensor_copy(out=pf[:], in_=pi[:]))
                V(lambda e: e.tensor_tensor(out=ang[:], in0=pf[:].unsqueeze(2).broadcast_to([128, NB, 16]),
                                            in1=invf[:].unsqueeze(1).broadcast_to([128, NB, 16]), op=ALU.mult))
                V(lambda e: e.tensor_scalar(out=kf[:], in0=ang[:], scalar1=1.0 / TWO_PI, scalar2=None, op0=ALU.mult))
                V(lambda e: e.tensor_copy(out=ki[:], in_=kf[:]))
                V(lambda e: e.tensor_copy(out=kf[:], in_=ki[:]))
                V(lambda e: e.scalar_tensor_tensor(out=r[:], in0=kf[:], scalar=-C1, in1=ang[:], op0=ALU.mult, op1=ALU.add))
                V(lambda e: e.scalar_tensor_tensor(out=r[:], in0=kf[:], scalar=-C2, in1=r[:], op0=ALU.mult, op1=ALU.add))

                def wrap(t):
                    V(lambda e: e.tensor_scalar(out=m[:], in0=t[:], scalar1=np.pi, scalar2=-TWO_PI, op0=ALU.is_gt, op1=ALU.mult))
                    V(lambda e: e.tensor_tensor(out=t[:], in0=t[:], in1=m[:], op=ALU.add))
                    V(lambda e: e.tensor_scalar(out=m[:], in0=t[:], scalar1=-np.pi, scalar2=TWO_PI, op0=ALU.is_lt, op1=ALU.mult))
                    V(lambda e: e.tensor_tensor(out=t[:], in0=t[:], in1=m[:], op=ALU.add))
                    V(lambda e: e.tensor_scalar(out=t[:], in0=t[:], scalar1=3.1415925, scalar2=-3.1415925, op0=ALU.min, op1=ALU.max))

                wrap(r)
                V(lambda e: e.tensor_scalar(out=r2[:], in0=r[:], scalar1=np.pi / 2, scalar2=None, op0=ALU.add))
                wrap(r2)
                S.op(ACT, lambda e: e.activation(out=sn[:], in_=r[:], func=AF.Sin), R=[B], W=[B])
                S.op(ACT, lambda e: e.activation(out=cs[:], in_=r2[:], func=AF.Sin), R=[B], W=[B])
                S.dma(SP, ROPE[0], cs[:].rearrange("p b i -> p (b i)"), R=[B])
                S.dma(SP, ROPE[1], sn[:].rearrange("p b i -> p (b i)"), R=[B])
                S.barrier()
            with contextlib.ExitStack() as es:
                ct = sbuf(es, [128, 8], F32)
                ca = sbuf(es, [128, 8], F32)
                crep = sbuf(es, [128, 8, 128], F32)
                B = Buf()
                S.dma(SP, ct[:], c_t, W=[B])
                S.op(ACT, lambda e: e.activation(out=ca[:], in_=ct[:], func=AF.Silu), R=[B], W=[B])
                S.op(DVE, lambda e: e.tensor_copy(out=crep[:], in_=ca[:].unsqueeze(2).broadcast_to([128, 8, 128])), R=[B], W=[B])
                stg = Ring([(sbuf(es, [128, 8, 512], F32, "astg"), Buf()) for _ in range(2)])
                pss = Ring([(psum(es, [128, 512], F32, "aps"), Buf()) for _ in range(2)])
                bada = sbuf(es, [128, 6 * D], F32)
                modt = sbuf(es, [128, 6 * D], F32)
                badaB, modB = Buf(), Buf()
                for l in range(L):
                    load_bcast(es, bada, b_ada[l], badaB)
                    wv = w_ada[l].rearrange("(k p) c -> p k c", p=128)
                    for ctile in range(12):
                        st, stB = stg.next()
                        ps, psB = pss.next()
                        S.dma(SP, st[:], wv[:, :, ctile * 512:(ctile + 1) * 512], W=[stB])
                        for kk in range(8):
                            S.op(PE, lambda e, kk=kk, st=st, ps=ps: e.matmul(ps[:], lhsT=crep[:, kk, :], rhs=st[:, kk, :],
                                                                             start=(kk == 0), stop=(kk == 7)),
                                 R=[stB, B], W=[psB], sig=(kk == 7))
                        S.op(DVE, lambda e, ps=ps, ctile=ctile: e.tensor_tensor(out=modt[:, ctile * 512:(ctile + 1) * 512], in0=ps[:],
                                                                                in1=bada[:, ctile * 512:(ctile + 1) * 512], op=ALU.add),
                             R=[psB, badaB], W=[modB])
                    S.dma(SP, MOD[l], modt[:], R=[modB])
                S.barrier()

        def phase1(l):
            Xin = xw if l == 0 else XA
            with contextlib.ExitStack() as es:
                wq = sbuf(es, [128, 8, 3 * AW], BF16, "wq")
                with contextlib.ExitStack() as es2:
                    load_weight(es2, wq, w_in[l][:, 0:3 * AW], 8, 3 * AW, Ring([DVE, POOL, ACT]))
                    S.barrier()
                cB = Buf()
                Gm = sbuf(es, [128, D], F32, "Gm")
                SHb = sbuf(es, [128, D], F32, "SHb")
                g1b = sbuf(es, [128, D], F32, "g1b")
                gq = sbuf(es, [128, DH], F32, "gq")
                gk = sbuf(es, [128, DH], F32, "gk")
                cosT = sbuf(es, [128, NB, 16], F32, "cosT")
                sinT = sbuf(es, [128, NB, 16], F32, "sinT")
                load_bcast(es, g1b, g_norm1[l], cB)
                load_bcast(es, gq, g_q[l], cB)
                load_bcast(es, gk, g_k[l], cB)
                S.dma(SP, Gm[:], MOD[l][:, D:2 * D], W=[cB])
                S.dma(SP, SHb[:], MOD[l][:, 0:D], W=[cB])
                S.dma(SP, cosT[:].rearrange("p b i -> p (b i)"), ROPE[0], W=[cB])
                S.dma(SP, sinT[:].rearrange("p b i -> p (b i)"), ROPE[1], W=[cB])
                S.op(DVE, lambda e: e.scalar_tensor_tensor(out=Gm[:], in0=Gm[:], scalar=1.0, in1=g1b[:], op0=ALU.add, op1=ALU.mult),
                     R=[cB], W=[cB])
                norm = NormCtx(es, Gm, SHb, cB)
                xts = Ring([(sbuf(es, [128, D], F32, "xt"), Buf()) for _ in range(3)])
                hTs = Ring([(sbuf(es, [128, 8, 128], BF16, "hT"), Buf()) for _ in range(2)])
                pss = Ring([(psum(es, [128, 512], F32, "p1ps"), Buf()) for _ in range(4)])
                sqs = Ring([(sbuf(es, [128, 512], F32, "sq"), Buf()) for _ in range(2)])
                s4s = Ring([(sbuf(es, [128, 12], F32, "s4"), Buf()) for _ in range(2)])
                qns = Ring([(sbuf(es, [128, 512], F32, "qn"), Buf()) for _ in range(2)])
                qos = Ring([(sbuf(es, [128, 512], BF16, "qo"), Buf()) for _ in range(3)])
                rts = Ring([(sbuf(es, [128, 4, 4, 16], F32, "rt"), Buf()) for _ in range(2)])
                vos = []
                for _ in range(2):
                    vo = sbuf(es, [128, 4, 129], BF16, "vo")
                    vB = Buf()
                    S.op(POOL, lambda e, vo=vo: e.memset(vo[:], 1.0), W=[vB])
                    vos.append((vo, vB))
                vos = Ring(vos)
                for blk in range(Kb[l], NB):
                    xt, xtB = xts.next()
                    hT, hTB = hTs.next()
                    S.dma(SP, xt[:], Xin[blk * 128:(blk + 1) * 128, :], W=[xtB])
                    norm.run(xt, xtB, blk, hT[:], hTB, DVE)
                    for t in range(9):
                        ps, psB = pss.next()
                        for kk in range(8):
                            S.op(PE, lambda e, kk=kk, ps=ps, t=t: e.matmul(ps[:], lhsT=hT[:, kk, :], rhs=wq[:, kk, t * 512:(t + 1) * 512],
                                                                           start=(kk == 0), stop=(kk == 7)),
                                 R=[hTB], W=[psB], sig=(kk == 7))
                        if t < 6:
                            gvec = gq if t < 3 else gk
                            dstD = QS if t < 3 else KS
                            tt = t % 3
                            sq, sqB = sqs.next()
                            s4, s4B = s4s.next()
                            qn, qnB = qns.next()
                            qo, qoB = qos.next()
                            rt, rtB = rts.next()
                            S.op(ACT, lambda e, sq=sq, ps=ps: e.activation(out=sq[:], in_=ps[:], func=AF.Square), R=[psB], W=[sqB])
                            S.op(DVE, lambda e, s4=s4, sq=sq: e.tensor_reduce(out=s4[:, 0:4], in_=sq[:].rearrange("p (h d) -> p h d", h=4),
                                                                              axis=AX.X, op=ALU.add), R=[sqB], W=[s4B])
                            rsqrt_small(s4[:, 8:12], s4[:, 0:4], 1.0 / DH, s4B, s4B, s4[:, 4:8], s4B)
                            for j in range(4):
                                S.op(DVE, lambda e, j=j, qn=qn, ps=ps, s4=s4, gvec=gvec: e.scalar_tensor_tensor(
                                    out=qn[:, j * 128:(j + 1) * 128], in0=ps[:, j * 128:(j + 1) * 128], scalar=s4[:, 8 + j:9 + j],
                                    in1=gvec[:], op0=ALU.mult, op1=ALU.mult), R=[psB, s4B, cB], W=[qnB])
                            S.op(POOL, lambda e, qo=qo, qn=qn: e.tensor_copy(out=qo[:], in_=qn[:]), R=[qnB], W=[qoB])
                            qn3 = qn[:].rearrange("p (h d) -> p h d", h=4)
                            qo3 = qo[:].rearrange("p (h d) -> p h d", h=4)
                            cb = cosT[:, blk, :].unsqueeze(1).broadcast_to([128, 4, 16])
                            sb_ = sinT[:, blk, :].unsqueeze(1).broadcast_to([128, 4, 16])
                            t1, t2 = qn3[:, :, 0:16], qn3[:, :, 16:32]
                            S.op(POOL, lambda e, rt=rt, t1=t1, cb=cb: e.tensor_tensor(out=rt[:, 0], in0=t1, in1=cb, op=ALU.mult), R=[qnB, cB], W=[rtB])
                            S.op(POOL, lambda e, rt=rt, t2=t2, sb_=sb_: e.tensor_tensor(out=rt[:, 1], in0=t2, in1=sb_, op=ALU.mult), R=[qnB, cB], W=[rtB])
                            S.op(POOL, lambda e, rt=rt, t2=t2, cb=cb: e.tensor_tensor(out=rt[:, 2], in0=t2, in1=cb, op=ALU.mult), R=[qnB, cB], W=[rtB])
                            S.op(POOL, lambda e, rt=rt, t1=t1, sb_=sb_: e.tensor_tensor(out=rt[:, 3], in0=t1, in1=sb_, op=ALU.mult), R=[qnB, cB], W=[rtB])
                            S.op(POOL, lambda e, rt=rt, qo3=qo3: e.tensor_tensor(out=qo3[:, :, 0:16], in0=rt[:, 0], in1=rt[:, 1], op=ALU.subtract),
                                 R=[rtB], W=[qoB])
                            S.op(POOL, lambda e, rt=rt, qo3=qo3: e.tensor_tensor(out=qo3[:, :, 16:32], in0=rt[:, 2], in1=rt[:, 3], op=ALU.add),
                                 R=[rtB], W=[qoB])
                            S.dma(POOL, dstD[blk * 128:(blk + 1) * 128, tt * 512:(tt + 1) * 512], qo[:], R=[qoB])
                        else:
                            tt = t - 6
                            vo, vB = vos.next()
                            S.op(ACT, lambda e, vo=vo, ps=ps: e.copy(out=vo[:, :, 0:128], in_=ps[:].rearrange("p (h d) -> p h d", h=4)),
                                 R=[psB], W=[vB])
                            S.dma(POOL, VS[blk * 128:(blk + 1) * 128, tt * 516:(tt + 1) * 516], vo[:].rearrange("p h d -> p (h d)"), R=[vB])
                S.barrier()

        def phase2(l):
            with contextlib.ExitStack() as es:
                def ring(n, shape, dt, p, ps=False):
                    return Ring([((psum if ps else sbuf)(es, shape, dt, p), Buf()) for _ in range(n)])
                cB = Buf()
                gqp = sbuf(es, [128, 1], F32, "gqp")
                gkp = sbuf(es, [128, 1], F32, "gkp")
                S.dma(SP, gqp[:], g_q[l].rearrange("(p o) -> p o", o=1), W=[cB])
                S.dma(SP, gkp[:], g_k[l].rearrange("(p o) -> p o", o=1), W=[cB])
                S.op(DVE, lambda e: e.memset(gqp[0:32, :], 1.0), R=[cB], W=[cB])
                S.op(DVE, lambda e: e.memset(gkp[0:32, :], 1.0), R=[cB], W=[cB])
                Qts = ring(3, [128, 512], BF16, "Qt")
                Kcs = ring(3, [128, 512], BF16, "Kc")
                Kps = ring(3, [128, 512], BF16, "Kp")
                Vcs = ring(3, [128, 4, 129], BF16, "Vc")
                Vps = ring(3, [128, 4, 129], BF16, "Vp")
                Tps = ring(2, [128, 6, 128], BF16, "Tps", ps=True)
                qkTs = ring(3, [128, 6, 128], BF16, "qkT")
                Sps = ring(3, [128, 2, 2, 128], F32, "Sps", ps=True)
                PTs = ring(3, [128, 2, 2, 128], BF16, "PT")
                Ops = ring(2, [128, 2, 129], F32, "Ops", ps=True)
                Ots = ring(3, [128, 4, 129], F32, "Ot")
                for rg in (Qts, Kcs, Vcs, qkTs, PTs):
                    for (t_, b_) in rg.items:
                        S.op(POOL, lambda e, t_=t_: e.memset(t_[:], 0.0), W=[b_])
                ul = units[l]
                loaded = {}

                def loads(ui):
                    (g, d, base, nq) = ul[ui]
                    m0, r = divmod(base, d)
                    qv = QS.rearrange("(m d) c -> m d c", d=d)
                    kv = KS.rearrange("(m d) c -> m d c", d=d)
                    vv = VS.rearrange("(m d) c -> m d c", d=d)
                    Qt, QtB = Qts.next()
                    Kc, KcB = Kcs.next()
                    Kp, KpB = Kps.next()
                    Vc, VcB = Vcs.next()
                    Vp, VpB = Vps.next()
                    cs_ = slice(512 * g, 512 * g + 512)
                    vs_ = slice(516 * g, 516 * g + 516)
                    S.dma(SP, Qt[:nq, :], qv[m0:m0 + nq, r, cs_], W=[QtB])
                    S.dma(SP, Kp[:, :], kv[m0 - 128:m0, r, cs_], W=[KpB])
                    S.dma(SP, Kc[:nq, :], kv[m0:m0 + nq, r, cs_], W=[KcB])
                    S.dma(SP, Vp[:].rearrange("p h d -> p (h d)"), vv[m0 - 128:m0, r, vs_], W=[VpB])
                    S.dma(SP, Vc[:nq].rearrange("p h d -> p (h d)"), vv[m0:m0 + nq, r, vs_], W=[VcB])
                    loaded[ui] = (Qt, QtB, Kc, KcB, Kp, KpB, Vc, VcB, Vp, VpB, Ots.next())

                def stageA(ui, hp):
                    (g, d, base, nq) = ul[ui]
                    (Qt, QtB, Kc, KcB, Kp, KpB, Vc, VcB, Vp, VpB, (Ot, OtB)) = loaded[ui]
                    colp = kcols[(d, base - 128 * d)]
                    colc = kcols[(d, base)]
                    T, TB = Tps.next()
                    qkT, qkTB = qkTs.next()
                    Sp, SpB = Sps.next()
                    PT, PTB = PTs.next()
                    for jj in range(2):
                        j = 2 * hp + jj
                        S.op(PE, lambda e, jj=jj, j=j: e.transpose(out=T[:, jj, :nq], in_=Qt[:nq, j * 128:(j + 1) * 128], identity=ident[:nq, :nq]),
                             R=[QtB], W=[TB], sig=False)
                        S.op(PE, lambda e, jj=jj, j=j: e.transpose(out=T[:, 2 + jj, :], in_=Kp[:, j * 128:(j + 1) * 128], identity=ident[:]),
                             R=[KpB], W=[TB], sig=False)
                        S.op(PE, lambda e, jj=jj, j=j: e.transpose(out=T[:, 4 + jj, :nq], in_=Kc[:nq, j * 128:(j + 1) * 128], identity=ident[:nq, :nq]),
                             R=[KcB], W=[TB], sig=(jj == 1))
                    S.op(ACT, lambda e: e.copy(out=qkT[:, 0:2, :nq], in_=T[:, 0:2, :nq]), R=[TB], W=[qkTB])
                    if nq == 128:
                        S.op(DVE, lambda e: e.tensor_copy(out=qkT[:, 2:6, :], in_=T[:, 2:6, :]), R=[TB], W=[qkTB])
                    else:
                        S.op(DVE, lambda e: e.tensor_copy(out=qkT[:, 2:4, :], in_=T[:, 2:4, :]), R=[TB], W=[qkTB])
                        S.op(DVE, lambda e: e.tensor_copy(out=qkT[:, 4:6, :nq], in_=T[:, 4:6, :nq]), R=[TB], W=[qkTB])
                    for jj in range(2):
                        S.op(PE, lambda e, jj=jj: e.matmul(Sp[:, 0, jj, :nq], lhsT=ident[:], rhs=maskP[:, :nq], start=True, stop=False),
                             R=[], W=[SpB], sig=False)
                        S.op(PE, lambda e, jj=jj: e.matmul(Sp[:, 0, jj, :nq], lhsT=qkT[:, 2 + jj, :], rhs=qkT[:, jj, :nq], start=False, stop=True),
                             R=[qkTB], W=[SpB], sig=False)
                        S.op(PE, lambda e, jj=jj: e.matmul(Sp[:nq, 1, jj, :nq], lhsT=ident[:nq, :nq], rhs=maskC[:nq, :nq], start=True, stop=False),
                             R=[], W=[SpB], sig=False)
                        S.op(PE, lambda e, jj=jj: e.matmul(Sp[:nq, 1, jj, :nq], lhsT=qkT[:, 4 + jj, :nq], rhs=qkT[:, jj, :nq], start=False, stop=True),
                             R=[qkTB], W=[SpB], sig=(jj == 1))
                    S.op(ACT, lambda e: e.activation(out=PT[:, 0, :, :nq], in_=Sp[:, 0, :, :nq], func=AF.Exp, scale=SCALE,
                                                     bias=kb[:, colp:colp + 1]), R=[SpB], W=[PTB])
                    S.op(ACT, lambda e: e.activation(out=PT[:nq, 1, :, :nq], in_=Sp[:nq, 1, :, :nq], func=AF.Exp, scale=SCALE,
                                                     bias=kb[:nq, colc:colc + 1]), R=[SpB], W=[PTB])
                    return (PT, PTB)

                def stageB(ui, hp, PT, PTB):
                    (g, d, base, nq) = ul[ui]
                    (Qt, QtB, Kc, KcB, Kp, KpB, Vc, VcB, Vp, VpB, (Ot, OtB)) = loaded[ui]
                    m0, r = divmod(base, d)
                    Op, OpB = Ops.next()
                    for jj in range(2):
                        j = 2 * hp + jj
                        S.op(PE, lambda e, jj=jj, j=j: e.matmul(Op[:nq, jj, :], lhsT=PT[:, 0, jj, :nq], rhs=Vp[:, j, :], start=True, stop=False),
                             R=[PTB, VpB], W=[OpB], sig=False)
                        S.op(PE, lambda e, jj=jj, j=j: e.matmul(Op[:nq, jj, :], lhsT=PT[:nq, 1, jj, :nq], rhs=Vc[:nq, j, :], start=False, stop=True),
                             R=[PTB, VcB], W=[OpB], sig=(jj == 1))
                    S.op(DVE, lambda e: e.tensor_copy(out=Ot[:nq, 2 * hp:2 * hp + 2, :], in_=Op[:nq, :, :]), R=[OpB], W=[OtB])
                    if hp == 1:
                        av = ATT[g].rearrange("(m d) c -> m d c", d=d)
                        S.dma(POOL, av[m0:m0 + nq, r, :], Ot[:nq].rearrange("p h d -> p (h d)"), R=[OtB])
                        del loaded[ui]

                nu = len(ul)
                loads(0)
                if nu > 1:
                    loads(1)
                prev = None
                for ui in range(nu):
                    for hp in range(2):
                        cur = (ui, hp) + stageA(ui, hp)
                        if prev is not None:
                            stageB(*prev)
                        if hp == 0 and ui + 2 < nu:
                            loads(ui + 2)
                        prev = cur
                stageB(*prev)
                S.barrier()

        def phase3a(l):
            Xin = xw if l == 0 else XA
            WT = 256
            with contextlib.ExitStack() as es:
                wcv = sbuf(es, [128, 8, 2048], BF16, "wcv")
                wco = sbuf(es, [128, 8, D], BF16, "wco")
                dg = sbuf(es, [128, 8, CONVK, 128], BF16, "dg")
                bdw = sbuf(es, [128, 8], F32, "bdw")
                gln = sbuf(es, [128, 8], F32, "gln")
                bln = sbuf(es, [128, 8], F32, "bln")
                cB = Buf()
                with contextlib.ExitStack() as es2:
                    rot = Ring([DVE, POOL, ACT])
                    load_weight(es2, wcv, w_in[l][:, 3 * AW:3 * AW + 2048], 8, 2048, rot)
                    load_weight(es2, wco, w_conv_out[l], 8, D, rot)
                    wdw = sbuf(es2, [128, 8, CONVK], F32, "wdw")
                    idf = sbuf(es2, [128, 128], F32, "idf")
                    dB = Buf()
                    S.dma(SP, wdw[:], wdw_t[l], W=[dB])
                    S.dma(SP, idf[:], ident_d, W=[dB])
                    for c in range(8):
                        for k in range(CONVK):
                            S.op(ACT, lambda e, c=c, k=k: e.activation(out=dg[:, c, k, :], in_=idf[:], func=AF.Identity, scale=wdw[:, c, k:k + 1]),
                                 R=[dB], W=[Buf()])
                    S.barrier()
                Gm = sbuf(es, [128, D], F32, "Gm")
                SHb = sbuf(es, [128, D], F32, "SHb")
                g1b = sbuf(es, [128, D], F32, "g1b")
                load_bcast(es, g1b, g_norm1[l], cB)
                S.dma(SP, Gm[:], MOD[l][:, D:2 * D], W=[cB])
                S.dma(SP, SHb[:], MOD[l][:, 0:D], W=[cB])
                S.dma(SP, bdw[:], bdw_t[l], W=[cB])
                S.dma(SP, gln[:], gln_t[l], W=[cB])
                S.dma(SP, bln[:], bln_t[l], W=[cB])
                S.op(DVE, lambda e: e.scalar_tensor_tensor(out=Gm[:], in0=Gm[:], scalar=1.0, in1=g1b[:], op0=ALU.add, op1=ALU.mult),
                     R=[cB], W=[cB])
                norm = NormCtx(es, Gm, SHb, cB)
                xts = Ring([(sbuf(es, [128, D], F32, "xt"), Buf()) for _ in range(2)])
                hTs = Ring([(sbuf(es, [128, 8, WT], BF16, "hT"), [Buf(), Buf()]) for _ in range(2)])
                uT = sbuf(es, [128, 8, 30 + WT], BF16, "uT")
                uTB = [Buf() for _ in range(8)]
                S.op(POOL, lambda e: e.memset(uT[:], 0.0), W=uTB)
                sgt = Ring([(sbuf(es, [128, WT], F32, "sgt"), Buf()) for _ in range(2)])
                yvs = Ring([(sbuf(es, [128, 8, WT], F32, "yv"), [Buf() for _ in range(8)]) for _ in range(2)])
                ybf = sbuf(es, [128, 8, WT], BF16, "ybf"); ybfB = Buf()
                ysq = sbuf(es, [128, 8, WT], BF16, "ysq"); ysqB = Buf()
                stt_ = sbuf(es, [128, 5, WT], F32, "lnst"); stB = Buf()
                actTs = Ring([(sbuf(es, [128, 8, WT], BF16, "actT"), [Buf() for _ in range(8)]) for _ in range(2)])
                ybos = Ring([(sbuf(es, [128, 8, WT], BF16, "ybo"), Buf()) for _ in range(2)])
                psv = Ring([(psum(es, [128, 2, WT], F32, "psv"), Buf()) for _ in range(2)])
                psc = Ring([(psum(es, [128, 2, WT], F32, "psc"), [Buf(), Buf()]) for _ in range(2)])
                psS = (psum(es, [128, 2, WT], F32, "psS"), Buf())
                psY = Ring([(psum(es, [128, 2, WT], F32, "psY"), Buf()) for _ in range(2)])
                ybv = YB.rearrange("c p t -> p c t")

                blk0 = Mb[l]
                while blk0 < NB:
                    nbk = min(2, NB - blk0)
                    W_ = 128 * nbk
                    hT, hTBs = hTs.next()
                    yv, yB = yvs.next()
                    actT, actB = actTs.next()
                    for bi in range(nbk):
                        blk = blk0 + bi
                        xt, xtB = xts.next()
                        S.dma(SP, xt[:], Xin[blk * 128:(blk + 1) * 128, :], W=[xtB])
                        norm.run(xt, xtB, blk, hT[:, :, bi * 128:(bi + 1) * 128], hTBs[bi], ACT)
                    hR = hTBs[:nbk]
                    for cp in range(4):
                        pc, pcBs = psc.next()
                        for ci in range(2):
                            c = 2 * cp + ci
                            pv, pvB = psv.next()
                            sg, sgB = sgt.next()
                            for half in range(2):
                                col0 = half * D + c * 128
                                for kk in range(8):
                                    S.op(PE, lambda e, kk=kk, pv=pv, half=half, col0=col0: e.matmul(
                                        pv[:, half, :W_], lhsT=wcv[:, kk, col0:col0 + 128], rhs=hT[:, kk, :W_],
                                        start=(kk == 0), stop=(kk == 7)), R=hR, W=[pvB], sig=(kk == 7 and half == 1))
                            S.op(ACT, lambda e, pv=pv, sg=sg: e.activation(out=sg[:, :W_], in_=pv[:, 1, :W_], func=AF.Sigmoid), R=[pvB], W=[sgB])
                            S.op(DVE, lambda e, pv=pv, sg=sg, c=c: e.tensor_tensor(out=uT[:, c, 30:30 + W_], in0=pv[:, 0, :W_], in1=sg[:, :W_], op=ALU.mult),
                                 R=[pvB, sgB], W=[uTB[c]])
                        for ci in range(2):
                            c = 2 * cp + ci
                            for tap in range(CONVK):
                                S.op(PE, lambda e, c=c, ci=ci, tap=tap, pc=pc: e.matmul(pc[:, ci, :W_], lhsT=dg[:, c, tap, :], rhs=uT[:, c, tap:tap + W_],
                                                                                        start=(tap == 0), stop=(tap == CONVK - 1)),
                                     R=[uTB[c]], W=[pcBs[0]], sig=(tap == CONVK - 1 and ci == 1))
                        for ci in range(2):
                            c = 2 * cp + ci
                            S.op(ACT, lambda e, c=c, ci=ci, pc=pc: e.activation(out=yv[:, c, :W_], in_=pc[:, ci, :W_], func=AF.Identity, bias=bdw[:, c:c + 1]),
                                 R=[pcBs[0], cB], W=[yB[c]])
                            S.op(POOL, lambda e, c=c: e.tensor_copy(out=uT[:, c, 0:30], in_=uT[:, c, W_:W_ + 30]), R=[uTB[c]], W=[uTB[c]])
                    S.op(POOL, lambda e: e.tensor_copy(out=ybf[:, :, :W_], in_=yv[:, :, :W_]), R=yB, W=[ybfB])
                    S.op(ACT, lambda e: e.activation(out=ysq[:, :, :W_], in_=yv[:, :, :W_], func=AF.Square), R=yB, W=[ysqB])
                    pS, pSB = psS
                    for c in range(8):
                        S.op(PE, lambda e, c=c: e.matmul(pS[:, 0, :W_], lhsT=ones_bf[:], rhs=ybf[:, c, :W_], start=(c == 0), stop=(c == 7)),
                             R=[ybfB], W=[pSB], sig=False)
                    for c in range(8):
                        S.op(PE, lambda e, c=c: e.matmul(pS[:, 1, :W_], lhsT=ones_bf[:], rhs=ysq[:, c, :W_], start=(c == 0), stop=(c == 7)),
                             R=[ysqB], W=[pSB], sig=(c == 7))
                    mean_, msq_, var_, lnv_, rstd_ = (stt_[:, i, :W_] for i in range(5))
                    S.op(ACT, lambda e: e.activation(out=mean_, in_=pS[:, 0, :W_], func=AF.Copy, scale=1.0 / D), R=[pSB], W=[stB])
                    S.op(ACT, lambda e: e.activation(out=msq_, in_=mean_, func=AF.Square), R=[stB], W=[stB])
                    S.op(DVE, lambda e: e.scalar_tensor_tensor(out=var_, in0=pS[:, 1, :W_], scalar=1.0 / D, in1=msq_, op0=ALU.mult, op1=ALU.subtract),
                         R=[pSB, stB], W=[stB])
                    S.op(DVE, lambda e: e.tensor_scalar(out=var_, in0=var_, scalar1=0.0, scalar2=None, op0=ALU.max), R=[stB], W=[stB])
                    S.op(ACT, lambda e: e.activation(out=lnv_, in_=var_, func=AF.Ln, bias=EPS), R=[stB], W=[stB])
                    S.op(ACT, lambda e: e.activation(out=rstd_, in_=lnv_, func=AF.Exp, scale=-0.5), R=[stB], W=[stB])
                    S.op(DVE, lambda e: e.tensor_tensor(out=yv[:, :, :W_], in0=yv[:, :, :W_],
                                                        in1=mean_.unsqueeze(1).broadcast_to([128, 8, W_]), op=ALU.subtract), R=yB + [stB], W=yB)
                    S.op(DVE, lambda e: e.tensor_tensor(out=yv[:, :, :W_], in0=yv[:, :, :W_],
                                                        in1=rstd_.unsqueeze(1).broadcast_to([128, 8, W_]), op=ALU.mult), R=yB + [stB], W=yB)
                    for c in range(8):
                        S.op(ACT, lambda e, c=c: e.activation(out=actT[:, c, :W_], in_=yv[:, c, :W_], func=AF.Silu,
                                                              scale=gln[:, c:c + 1], bias=bln[:, c:c + 1]), R=[yB[c], cB], W=[actB[c]])
                    ybo, yboB = ybos.next()
                    for op_ in range(4):
                        pY, pYB = psY.next()
                        for oi in range(2):
                            oc = 2 * op_ + oi
                            for kk in range(8):
                                S.op(PE, lambda e, kk=kk, oc=oc, oi=oi, pY=pY: e.matmul(pY[:, oi, :W_], lhsT=wco[:, kk, oc * 128:(oc + 1) * 128], rhs=actT[:, kk, :W_],
                                                                                        start=(kk == 0), stop=(kk == 7)), R=actB, W=[pYB],
                                     sig=(kk == 7 and oi == 1))
                        if op_ % 2 == 0:
                            S.op(DVE, lambda e, op_=op_, pY=pY, ybo=ybo: e.tensor_copy(out=ybo[:, 2 * op_:2 * op_ + 2, :W_], in_=pY[:, :, :W_]), R=[pYB], W=[yboB])
                        else:
                            S.op(ACT, lambda e, op_=op_, pY=pY, ybo=ybo: e.copy(out=ybo[:, 2 * op_:2 * op_ + 2, :W_], in_=pY[:, :, :W_]), R=[pYB], W=[yboB])
                    for c in range(8):
                        S.dma(POOL, YB[c, :, blk0 * 128:blk0 * 128 + W_], ybo[:, c, :W_], R=[yboB])
                    blk0 += nbk
                S.barrier()

        def phase3b(l):
            Xin = xw if l == 0 else XA
            WT = 256
            with contextlib.ExitStack() as es:
                wg = sbuf(es, [128, 8, 2048], BF16, "wg")
                wap = sbuf(es, [128, 4, D], BF16, "wap")
                wo = sbuf(es, [128, 8, D], BF16, "wo")
                with contextlib.ExitStack() as es2:
                    rot = Ring([DVE, POOL, ACT])
                    load_weight(es2, wg, w_in[l][:, 3 * AW + 2048:INW], 8, 2048, rot)
                    load_weight(es2, wap, w_attn_proj[l], 4, D, rot)
                    load_weight(es2, wo, w_o[l], 8, D, rot)
                    S.barrier()
                cB = Buf()
                Gm = sbuf(es, [128, D], F32, "Gm")
                SHb = sbuf(es, [128, D], F32, "SHb")
                gtb = sbuf(es, [128, D], F32, "gtb")
                g1b = sbuf(es, [128, D], F32, "g1b")
                load_bcast(es, g1b, g_norm1[l], cB)
                S.dma(SP, Gm[:], MOD[l][:, D:2 * D], W=[cB])
                S.dma(SP, SHb[:], MOD[l][:, 0:D], W=[cB])
                S.dma(SP, gtb[:], MOD[l][:, 2 * D:3 * D], W=[cB])
                S.op(DVE, lambda e: e.scalar_tensor_tensor(out=Gm[:], in0=Gm[:], scalar=1.0, in1=g1b[:], op0=ALU.add, op1=ALU.mult),
                     R=[cB], W=[cB])
                norm = NormCtx(es, Gm, SHb, cB)
                xts = Ring([(sbuf(es, [128, D], F32, "xt"), Buf()) for _ in range(4)])
                xms = Ring([(sbuf(es, [128, D], F32, "xm"), Buf()) for _ in range(2)])
                hTs = Ring([(sbuf(es, [128, 8, WT], BF16, "hT"), [Buf(), Buf()]) for _ in range(2)])
                Ars = Ring([[(sbuf(es, [128, 516], F32, "A"), Buf()) for _ in range(3)] for _ in range(2)])
                abfs = Ring([(sbuf(es, [128, 512], BF16, "abf"), Buf()) for _ in range(2)])
                asts = Ring([(sbuf(es, [128, 8], F32, "ast"), Buf()) for _ in range(2)])
                attnTs = Ring([(sbuf(es, [128, 4, WT], BF16, "attnT"), [Buf(), Buf()]) for _ in range(2)])
                ybTs = Ring([(sbuf(es, [128, 8, WT], BF16, "ybT"), Buf()) for _ in range(2)])
                tpa = Ring([(psum(es, [128, 4, 128], BF16, "tpa"), Buf()) for _ in range(1)])
                sg2 = Ring([(sbuf(es, [128, 2, WT], F32, "sg2"), Buf()) for _ in range(2)])
                m1 = Ring([(sbuf(es, [128, 2, WT], F32, "m1"), Buf()) for _ in range(2)])
                mT = sbuf(es, [128, 8, WT], BF16, "mT"); mTB = [Buf() for _ in range(8)]
                psA = Ring([(psum(es, [128, 2, WT], F32, "psA"), Buf()) for _ in range(2)])
                psB_ = Ring([(psum(es, [128, WT], F32, "psB"), Buf()) for _ in range(2)])
                pso = Ring([(psum(es, [128, 512], F32, "pso"), Buf()) for _ in range(2)])
                ybv = YB.rearrange("c p t -> p c t")

                blk0 = Mb[l]
                while blk0 < NB:
                    nbk = min(2, NB - blk0)
                    W_ = 128 * nbk
                    hT, hTBs = hTs.next()
                    attnT, attnTBs = attnTs.next()
                    ybT, ybTB = ybTs.next()
                    for c in range(8):
                        S.dma(SP, ybT[:, c, :W_], YB[c, :, blk0 * 128:blk0 * 128 + W_], W=[ybTB])
                    xl = []
                    for bi in range(nbk):
                        blk = blk0 + bi
                        xt, xtB = xts.next()
                        xl.append((xt, xtB))
                        S.dma(SP, xt[:], Xin[blk * 128:(blk + 1) * 128, :], W=[xtB])
                        norm.run(xt, xtB, blk, hT[:, :, bi * 128:(bi + 1) * 128], hTBs[bi], ACT)
                        As = Ars.next()
                        abf, abfB = abfs.next()
                        ast, astB = asts.next()
                        for g in range(3):
                            S.dma(SP, As[g][0][:], ATT[g][blk * 128:(blk + 1) * 128, :], W=[As[g][1]])
                        A0, A1, A2 = As[0][0], As[1][0], As[2][0]
                        S.op(POOL, lambda e, A0=A0, A1=A1: e.tensor_tensor(out=A0[:], in0=A0[:], in1=A1[:], op=ALU.add), R=[As[1][1]], W=[As[0][1]])
                        S.op(POOL, lambda e, A0=A0, A2=A2: e.tensor_tensor(out=A0[:], in0=A0[:], in1=A2[:], op=ALU.add), R=[As[2][1]], W=[As[0][1]])
                        A3 = A0[:].rearrange("p (h d) -> p h d", h=4)
                        S.op(DVE, lambda e, ast=ast, A3=A3: e.tensor_scalar(out=ast[:, 0:4], in0=A3[:, :, 128], scalar1=1e-30, scalar2=None, op0=ALU.max),
                             R=[As[0][1]], W=[astB])
                        S.op(DVE, lambda e, ast=ast: e.reciprocal(out=ast[:, 4:8], in_=ast[:, 0:4]), R=[astB], W=[astB])
                        for j in range(4):
                            S.op(ACT, lambda e, j=j, abf=abf, A3=A3, ast=ast: e.activation(out=abf[:, j * 128:(j + 1) * 128], in_=A3[:, j, 0:128],
                                                                                         func=AF.Identity, scale=ast[:, 4 + j:5 + j]),
                                 R=[As[0][1], astB], W=[abfB])
                        tp_, tpB_ = tpa.next()
                        for j in range(4):
                            S.op(PE, lambda e, j=j, abf=abf, tp_=tp_: e.transpose(out=tp_[:, j, :], in_=abf[:, j * 128:(j + 1) * 128], identity=ident[:]),
                                 R=[abfB], W=[tpB_], sig=(j == 3))
                        S.op(DVE, lambda e, bi=bi, tp_=tp_: e.tensor_copy(out=attnT[:, :, bi * 128:(bi + 1) * 128], in_=tp_[:, :, :]),
                             R=[tpB_], W=[attnTBs[bi]])
                    hR = hTBs[:nbk]
                    aR = attnTBs[:nbk]
                    for oc in range(8):
                        pA, pAB = psA.next()
                        pB, pBB = psB_.next()
                        s2, s2B = sg2.next()
                        mm1, m1B = m1.next()
                        for gi in range(2):
                            col0 = gi * D + oc * 128
                            for kk in range(8):
                                S.op(PE, lambda e, kk=kk, gi=gi, col0=col0, pA=pA: e.matmul(pA[:, gi, :W_], lhsT=wg[:, kk, col0:col0 + 128], rhs=hT[:, kk, :W_],
                                                                                            start=(kk == 0), stop=(kk == 7)), R=hR, W=[pAB],
                                     sig=(kk == 7 and gi == 1))
                        for j in range(4):
                            S.op(PE, lambda e, j=j, pB=pB, oc=oc: e.matmul(pB[:, :W_], lhsT=wap[:, j, oc * 128:(oc + 1) * 128], rhs=attnT[:, j, :W_],
                                                                           start=(j == 0), stop=(j == 3)), R=aR, W=[pBB], sig=(j == 3))
                        S.op(ACT, lambda e, s2=s2, pA=pA: e.activation(out=s2[:, :, :W_], in_=pA[:, :, :W_], func=AF.Sigmoid), R=[pAB], W=[s2B])
                        S.op(DVE, lambda e, s2=s2, mm1=mm1, pB=pB: e.tensor_tensor(out=mm1[:, 0, :W_], in0=pB[:, :W_], in1=s2[:, 0, :W_], op=ALU.mult),
                             R=[pBB, s2B], W=[m1B])
                        S.op(DVE, lambda e, s2=s2, mm1=mm1, oc=oc: e.tensor_tensor(out=mm1[:, 1, :W_], in0=s2[:, 1, :W_], in1=ybT[:, oc, :W_], op=ALU.mult),
                             R=[s2B, ybTB], W=[m1B])
                        S.op(DVE, lambda e, mm1=mm1, oc=oc: e.tensor_tensor(out=mT[:, oc, :W_], in0=mm1[:, 0, :W_], in1=mm1[:, 1, :W_], op=ALU.add),
                             R=[m1B], W=[mTB[oc]])
                    for bi in range(nbk):
                        blk = blk0 + bi
                        xt, xtB = xl[bi]
                        xm, xmB = xms.next()
                        for hf in range(2):
                            po, poB = pso.next()
                            for kk in range(8):
                                S.op(PE, lambda e, kk=kk, po=po, bi=bi, hf=hf: e.matmul(po[:], lhsT=mT[:, kk, bi * 128:(bi + 1) * 128],
                                                                                        rhs=wo[:, kk, hf * 512:(hf + 1) * 512],
                                                                                        start=(kk == 0), stop=(kk == 7)), R=mTB, W=[poB], sig=(kk == 7))
                            S.op(DVE, lambda e, po=po, hf=hf, xm=xm: e.tensor_tensor(out=xm[:, hf * 512:(hf + 1) * 512], in0=po[:],
                                                                                     in1=gtb[:, hf * 512:(hf + 1) * 512], op=ALU.mult), R=[poB, cB], W=[xmB])
                        S.op(DVE, lambda e, xt=xt, xm=xm: e.tensor_tensor(out=xm[:], in0=xm[:], in1=xt[:], op=ALU.add), R=[xtB, xmB], W=[xmB])
                        S.dma(POOL, XM[blk * 128:(blk + 1) * 128, :], xm[:], R=[xmB])
                    blk0 += nbk
                S.barrier()

        def phase4(l):
            WT = 256
            last = (l == L - 1)
            with contextlib.ExitStack() as es:
                wfi = sbuf(es, [128, 8, 2 * DFF], BF16, "wfi")
                wfd = sbuf(es, [128, 22, D], BF16, "wfd")
                with contextlib.ExitStack() as es2:
                    rot = Ring([DVE, POOL, ACT])
                    load_weight(es2, wfi, w_ffn_in[l], 8, 2 * DFF, rot)
                    load_weight(es2, wfd, w_ffn_down[l], 22, D, rot)
                    S.barrier()
                cB = Buf()
                Gm = sbuf(es, [128, D], F32, "Gm")
                SHb = sbuf(es, [128, D], F32, "SHb")
                gtb = sbuf(es, [128, D], F32, "gtb")
                wf3 = sbuf(es, [128, 22, 3], F32, "wf3")
                bfv = sbuf(es, [128, 22], F32, "bfv")
                xo = sbuf(es, [128, D], F32, "xo")
                load_bcast(es, xo, g_norm2[l], cB)
                S.dma(SP, Gm[:], MOD[l][:, 4 * D:5 * D], W=[cB])
                S.dma(SP, SHb[:], MOD[l][:, 3 * D:4 * D], W=[cB])
                S.dma(SP, gtb[:], MOD[l][:, 5 * D:6 * D], W=[cB])
                S.dma(SP, wf3[:], wf3_t[l], W=[cB])
                S.dma(SP, bfv[:], bf_t[l], W=[cB])
                S.op(DVE, lambda e: e.scalar_tensor_tensor(out=Gm[:], in0=Gm[:], scalar=1.0, in1=xo[:], op0=ALU.add, op1=ALU.mult),
                     R=[cB], W=[cB])
                xoB = Buf()
                xoB.r.append(cB.w)
                norm = NormCtx(es, Gm, SHb, cB)
                xts = Ring([(sbuf(es, [128, D], F32, "xt"), Buf()) for _ in range(2)])
                hTs = Ring([(sbuf(es, [128, 8, WT], BF16, "hT"), [Buf(), Buf()]) for _ in range(2)])
                carry = sbuf(es, [128, 22, 2], F32, "carry"); carB = [Buf() for _ in range(22)]
                S.op(POOL, lambda e: e.memset(carry[:], 0.0), W=carB)
                gbs = Ring([(sbuf(es, [128, 2 + WT], F32, "gb"), Buf()) for _ in range(4)])
                accs = Ring([(sbuf(es, [128, WT], F32, "acc"), Buf()) for _ in range(4)])
                sils = Ring([(sbuf(es, [128, WT], F32, "sil"), Buf()) for _ in range(2)])
                actT = sbuf(es, [128, 22, WT], BF16, "actT"); actB = [Buf() for _ in range(22)]
                psg = Ring([(psum(es, [128, 2, WT], F32, "psg"), Buf()) for _ in range(4)])
                pso = Ring([(psum(es, [128, 512], F32, "pso"), Buf()) for _ in range(2)])
                blk0 = Mb[l]
                while blk0 < NB:
                    nbk = min(2, NB - blk0)
                    W_ = 128 * nbk
                    hT, hTBs = hTs.next()
                    xl = []
                    for bi in range(nbk):
                        blk = blk0 + bi
                        xt, xtB = xts.next()
                        xl.append((xt, xtB))
                        S.dma(SP, xt[:], XM[blk * 128:(blk + 1) * 128, :], W=[xtB])
                        norm.run(xt, xtB, blk, hT[:, :, bi * 128:(bi + 1) * 128], hTBs[bi], ACT)
                    hR = hTBs[:nbk]
                    for fp in range(11):
                        items = []
                        for fi in range(2):
                            f = 2 * fp + fi
                            pg, pgB = psg.next()
                            gb, gbB = gbs.next()
                            acc, accB = accs.next()
                            for half in range(2):
                                col0 = half * DFF + f * 128
                                for kk in range(8):
                                    S.op(PE, lambda e, kk=kk, pg=pg, half=half, col0=col0: e.matmul(
                                        pg[:, half, :W_], lhsT=wfi[:, kk, col0:col0 + 128], rhs=hT[:, kk, :W_],
                                        start=(kk == 0), stop=(kk == 7)), R=hR, W=[pgB], sig=(kk == 7 and half == 1))
                            S.op(POOL, lambda e, gb=gb, f=f: e.tensor_copy(out=gb[:, 0:2], in_=carry[:, f, :]), R=[carB[f]], W=[gbB])
                            S.op(ACT, lambda e, gb=gb, pg=pg: e.copy(out=gb[:, 2:2 + W_], in_=pg[:, 0, :W_]), R=[pgB], W=[gbB])
                            S.op(POOL, lambda e, gb=gb, f=f: e.tensor_copy(out=carry[:, f, :], in_=gb[:, W_:W_ + 2]), R=[gbB], W=[carB[f]])
                            items.append((f, pg, pgB, gb, gbB, acc, accB))
                        for (f, pg, pgB, gb, gbB, acc, accB) in items:
                            S.op(DVE, lambda e, f=f, gb=gb, acc=acc: e.tensor_scalar(out=acc[:, :W_], in0=gb[:, 0:W_], scalar1=wf3[:, f, 0:1],
                                                                                     scalar2=bfv[:, f:f + 1], op0=ALU.mult, op1=ALU.add),
                                 R=[gbB, cB], W=[accB])
                        for tap in (1, 2):
                            for (f, pg, pgB, gb, gbB, acc, accB) in items:
                                S.op(DVE, lambda e, f=f, gb=gb, acc=acc, tap=tap: e.scalar_tensor_tensor(
                                    out=acc[:, :W_], in0=gb[:, tap:tap + W_], scalar=wf3[:, f, tap:tap + 1], in1=acc[:, :W_],
                                    op0=ALU.mult, op1=ALU.add), R=[gbB, accB], W=[accB])
                        for (f, pg, pgB, gb, gbB, acc, accB) in items:
                            sl, slB = sils.next()
                            S.op(ACT, lambda e, sl=sl, acc=acc: e.activation(out=sl[:, :W_], in_=acc[:, :W_], func=AF.Silu), R=[accB], W=[slB])
                            S.op(DVE, lambda e, sl=sl, pg=pg, f=f: e.tensor_tensor(out=actT[:, f, :W_], in0=pg[:, 1, :W_], in1=sl[:, :W_], op=ALU.mult),
                                 R=[pgB, slB], W=[actB[f]])
                    for bi in range(nbk):
                        blk = blk0 + bi
                        xt, xtB = xl[bi]
                        for hf in range(2):
                            po, poB = pso.next()
                            for f in range(22):
                                S.op(PE, lambda e, f=f, po=po, bi=bi, hf=hf: e.matmul(po[:], lhsT=actT[:, f, bi * 128:(bi + 1) * 128],
                                                                                      rhs=wfd[:, f, hf * 512:(hf + 1) * 512],
                                                                                      start=(f == 0), stop=(f == 21)), R=actB, W=[poB], sig=(f == 21))
                            S.op(DVE, lambda e, po=po, hf=hf: e.tensor_tensor(out=xo[:, hf * 512:(hf + 1) * 512], in0=po[:],
                                                                              in1=gtb[:, hf * 512:(hf + 1) * 512], op=ALU.mult), R=[poB, cB], W=[xoB])
                        S.op(POOL, lambda e, xt=xt: e.tensor_tensor(out=xo[:], in0=xo[:], in1=xt[:], op=ALU.add), R=[xtB, xoB], W=[xoB])
                        if last:
                            if blk >= OWN0:
                                S.dma(POOL, y_out[(blk - OWN0) * 128:(blk - OWN0 + 1) * 128, :], xo[:], R=[xoB])
                        else:
                            S.dma(POOL, XA[blk * 128:(blk + 1) * 128, :], xo[:], R=[xoB])
                    blk0 += nbk
                S.barrier()

        S.barrier()
        prologue()
        for l in range(L):
            for pi, ph in enumerate((phase1, phase2, phase3a, phase3b, phase4)):
                if pi < stop_after:
                    ph(l)
        S.barrier()
    return nc, S.n_ins


def make_in_maps(inputs, L=4, cores=range(8)):
    Kb, Mb, OWN0, NB = geometry(L)
    NTOK = NB * 128
    kcols = keyset_columns(L)
    x = np.asarray(inputs["x"], dtype=np.float32)
    c = np.asarray(inputs["c"], dtype=np.float32)
    positions = np.asarray(inputs["positions"], dtype=np.int32)
    f32 = lambda k: np.ascontiguousarray(np.asarray(inputs[k], dtype=np.float32)[:L])
    shared = {k: f32(k) for k in ("w_ada", "b_ada", "g_norm1", "w_in", "g_q", "g_k", "w_attn_proj", "w_conv_out",
                                  "w_o", "g_norm2", "w_ffn_in", "w_ffn_down")}
    wdw = f32("w_conv_dw")
    shared["wdw_t"] = np.ascontiguousarray(wdw.reshape(L, CONVK, 8, 128).transpose(0, 3, 2, 1))
    per_ch = lambda a, n: np.ascontiguousarray(a.reshape(L, n, 128).transpose(0, 2, 1))
    shared["bdw_t"] = per_ch(f32("b_conv_dw"), 8)
    shared["gln_t"] = per_ch(f32("g_conv_ln"), 8)
    shared["bln_t"] = per_ch(f32("b_conv_ln"), 8)
    wf = f32("w_ffn_dw")
    shared["wf3_t"] = np.ascontiguousarray(wf.reshape(L, 3, 22, 128).transpose(0, 3, 2, 1))
    shared["bf_t"] = per_ch(f32("b_ffn_dw"), 22)
    inv = (np.float32(500000.0) ** (-np.arange(0, 32, 2, dtype=np.float32) / np.float32(32))).astype(np.float32)
    shared["invf"] = np.ascontiguousarray(np.broadcast_to(inv[None, :], (128, 16))).astype(np.float32)
    shared["ident"] = np.eye(128, dtype=np.float32)
    kk, qq = np.meshgrid(np.arange(128), np.arange(128), indexing="ij")
    shared["maskp"] = np.where(kk >= qq, 0.0, NEG).astype(np.float32)
    shared["maskc"] = np.where(kk <= qq, 0.0, NEG).astype(np.float32)
    maps = []
    for core in cores:
        b, j = core // 4, core % 4
        off = 4096 * j - OWN0 * 128
        gpos = off + np.arange(NTOK)
        valid = gpos >= 0
        gsafe = np.clip(gpos, 0, SEQ - 1)
        xw = np.where(valid[:, None], x[b, gsafe, :], np.float32(0.0)).astype(np.float32)
        pw = np.where(valid, positions[b, gsafe], 0).astype(np.int32)
        vb = valid.reshape(NB, 128)[:, 0].astype(np.float32)
        kbt = np.zeros((128, len(kcols)), dtype=np.float32)
        for (d, start), col in kcols.items():
            toks = np.clip(start + d * np.arange(128), 0, NTOK - 1)
            kbt[:, col] = np.where(valid[toks], 0.0, NEG)
        m = dict(shared)
        m["xw"] = np.ascontiguousarray(xw)
        m["pos_t"] = np.ascontiguousarray(pw.reshape(NB, 128).T)
        m["vcol"] = np.ascontiguousarray(np.broadcast_to(vb[None, :], (128, NB))).astype(np.float32)
        m["kbias"] = kbt
        m["c_t"] = np.ascontiguousarray(c[b].reshape(8, 128).T)
        maps.append(m)
    return maps


_CACHE = {}


def kernel(**inputs):
    L = 4
    if L not in _CACHE:
        _CACHE[L] = build_program(L)[0]
    nc = _CACHE[L]
    maps = make_in_maps(inputs, L)
    res = run_bass_kernel_spmd(nc, maps, core_ids=list(range(8)))
    out = np.empty((2, SEQ, D), dtype=np.float32)
    for core in range(8):
        b, j = core // 4, core % 4
        out[b, 4096 * j:4096 * (j + 1), :] = res.results[core]["y"]
    return out
```
